# Optimizing a Trainium2 kernel written in Bass

```python
import jax, jax.numpy as jnp
from jax import lax
import numpy as np

D_MODEL = 1024
BATCH = 4
SEQ = 8192
DEPTH = 2

PLE_DIM = 256
MLA_HEADS = 8
MLA_NOPE = 64
MLA_ROPE = 32
MLA_V = 64
MLA_QK = MLA_NOPE + MLA_ROPE
MLA_Q_LORA = 384
MLA_KV_LORA = 128
ROPE_THETA = 10000.0
Q_BLOCK = 128
FNET_GROUPS = 4
GQA_Q_HEADS = 8
GQA_KV_HEADS = 2
GQA_GROUP = GQA_Q_HEADS // GQA_KV_HEADS
GQA_HEAD_DIM = 64
WINDOW = 128
WIN_BLOCK = 128
FFN_DIM = -(-8 * D_MODEL // (3 * 256)) * 256
RMS_EPS = 1e-6

IN_SPLITS = (MLA_Q_LORA, MLA_KV_LORA, MLA_ROPE,
             GQA_Q_HEADS * GQA_HEAD_DIM, GQA_KV_HEADS * GQA_HEAD_DIM, GQA_KV_HEADS * GQA_HEAD_DIM,
             D_MODEL, D_MODEL, D_MODEL)
IN_COLS = sum(IN_SPLITS)

kernel_name = 'hybrid_mla_fnet_swagqa_encoder'


def rms_norm(x, gain):
    xf = x.astype(jnp.float32)
    y = xf * lax.rsqrt(jnp.mean(xf * xf, axis=-1, keepdims=True) + RMS_EPS)
    return (y * gain.astype(jnp.float32)).astype(x.dtype)


def apply_rope(t, positions):
    half = t.shape[-1] // 2
    inv_freq = ROPE_THETA ** (-jnp.arange(half, dtype=jnp.float32) / half)
    ang = positions.astype(jnp.float32)[..., None] * inv_freq
    ang = ang.reshape(ang.shape[:2] + (1,) * (t.ndim - 3) + (half,))
    cos, sin = jnp.cos(ang), jnp.sin(ang)
    tf = t.astype(jnp.float32)
    t1, t2 = tf[..., :half], tf[..., half:]
    return jnp.concatenate([t1 * cos - t2 * sin, t1 * sin + t2 * cos], axis=-1).astype(t.dtype)


def alibi_slopes(n_heads):
    return 2.0 ** (-8.0 * (np.arange(n_heads, dtype=np.float32) + 1.0) / n_heads)


def mla_attention(c_q, c_kv, k_rope, positions, q_norm, w_uq, kv_norm, w_ukv):
    B, S, _ = c_q.shape
    q = (rms_norm(c_q, q_norm) @ w_uq).reshape(B, S, MLA_HEADS, MLA_QK)
    q_nope = q[..., :MLA_NOPE]
    q_rope = apply_rope(q[..., MLA_NOPE:], positions)
    kv = (rms_norm(c_kv, kv_norm) @ w_ukv).reshape(B, S, MLA_HEADS, MLA_NOPE + MLA_V)
    k_nope, v = kv[..., :MLA_NOPE], kv[..., MLA_NOPE:]
    k_r = apply_rope(k_rope, positions)
    scale = MLA_QK ** -0.5
    nb = S // Q_BLOCK
    qn_b = q_nope.reshape(B, nb, Q_BLOCK, MLA_HEADS, MLA_NOPE).transpose(1, 0, 2, 3, 4)
    qr_b = q_rope.reshape(B, nb, Q_BLOCK, MLA_HEADS, MLA_ROPE).transpose(1, 0, 2, 3, 4)

    def attend(blk):
        qn, qr = blk
        s = (jnp.einsum('bqhd,bkhd->bhqk', qn, k_nope)
             + jnp.einsum('bqhr,bkr->bhqk', qr, k_r)).astype(jnp.float32) * scale
        probs = jax.nn.softmax(s, axis=-1).astype(v.dtype)
        return jnp.einsum('bhqk,bkhd->bqhd', probs, v)

    o = lax.map(attend, (qn_b, qr_b))
    return o.transpose(1, 0, 2, 3, 4).reshape(B, S, MLA_HEADS * MLA_V)


def fourier_mix(h):
    B, S, D = h.shape
    hg = h.astype(jnp.float32).reshape(B, S, FNET_GROUPS, D // FNET_GROUPS)
    f = jnp.fft.fftn(hg, axes=(1, 3), norm='ortho').real
    return f.reshape(B, S, D).astype(h.dtype)


def windowed_gqa(q, k, v, positions, sink):
    B, S, _ = q.shape
    nb = S // WIN_BLOCK
    q = q.reshape(B, nb, WIN_BLOCK, GQA_KV_HEADS, GQA_GROUP, GQA_HEAD_DIM)
    k = k.reshape(B, S, GQA_KV_HEADS, GQA_HEAD_DIM)
    v = v.reshape(B, S, GQA_KV_HEADS, GQA_HEAD_DIM)

    def windows(t):
        pad = ((0, 0), (WIN_BLOCK, WIN_BLOCK)) + ((0, 0),) * (t.ndim - 2)
        tb = jnp.pad(t, pad).reshape((B, nb + 2, WIN_BLOCK) + t.shape[2:])
        return jnp.concatenate([tb[:, :-2], tb[:, 1:-1], tb[:, 2:]], axis=2)

    kw, vw, pw = windows(k), windows(v), windows(positions)
    scale = GQA_HEAD_DIM ** -0.5
    s = jnp.einsum('bnqhgd,bnkhd->bnhgqk', q, kw).astype(jnp.float32) * scale
    pq = positions.reshape(B, nb, WIN_BLOCK)
    dist = jnp.abs(pq[..., :, None] - pw[..., None, :]).astype(jnp.float32)
    slopes = jnp.asarray(alibi_slopes(GQA_Q_HEADS)).reshape(GQA_KV_HEADS, GQA_GROUP)
    s = s - slopes[None, None, :, :, None, None] * dist[:, :, None, None]
    qi = jnp.arange(WIN_BLOCK)
    kj = jnp.arange(3 * WIN_BLOCK)
    band = jnp.abs(kj[None, :] - WIN_BLOCK - qi[:, None]) <= WINDOW
    key_idx = jnp.arange(nb)[:, None] * WIN_BLOCK - WIN_BLOCK + kj[None, :]
    valid = (key_idx >= 0) & (key_idx < S)
    mask = band[None] & valid[:, None, :]
    s = jnp.where(mask[None, :, None, None], s, -jnp.inf)
    sink_l = sink.astype(jnp.float32).reshape(GQA_KV_HEADS, GQA_GROUP)[None, None, :, :, None, None]
    m = jnp.maximum(jnp.max(s, axis=-1, keepdims=True), sink_l)
    e = jnp.exp(s - m)
    probs = (e / (jnp.sum(e, axis=-1, keepdims=True) + jnp.exp(sink_l - m))).astype(v.dtype)
    o = jnp.einsum('bnhgqk,bnkhd->bnqhgd', probs, vw)
    return o.reshape(B, S, GQA_Q_HEADS * GQA_HEAD_DIM)


def setup_inputs(seed: int = 0) -> dict:
    key = jax.random.key(seed)
    ks = iter(jax.random.split(key, 32))

    def w(shape, fan_in):
        return jax.random.normal(next(ks), shape, jnp.float32) * (fan_in ** -0.5)

    def gain(n):
        return 1.0 + 0.1 * jax.random.normal(next(ks), (DEPTH, n), jnp.float32)

    x = jax.random.normal(next(ks), (BATCH, SEQ, D_MODEL), jnp.float32)
    p = jax.random.normal(next(ks), (DEPTH, BATCH, SEQ, PLE_DIM), jnp.float32)
    offset = jax.random.randint(next(ks), (BATCH, 1), 0, 4096, jnp.int32)
    positions = (offset + jnp.arange(SEQ, dtype=jnp.int32)[None, :]).astype(jnp.int32)
    return {
        'x': x,
        'p': p,
        'positions': positions,
        'norm_mix_pre': gain(D_MODEL),
        'w_in': w((DEPTH, D_MODEL, IN_COLS), D_MODEL),
        'mla_q_norm': gain(MLA_Q_LORA),
        'w_uq': w((DEPTH, MLA_Q_LORA, MLA_HEADS * MLA_QK), MLA_Q_LORA),
        'mla_kv_norm': gain(MLA_KV_LORA),
        'w_ukv': w((DEPTH, MLA_KV_LORA, MLA_HEADS * (MLA_NOPE + MLA_V)), MLA_KV_LORA),
        'gqa_sink': 0.5 * jax.random.normal(next(ks), (DEPTH, GQA_Q_HEADS), jnp.float32),
        'w_branch_a': w((DEPTH, MLA_HEADS * MLA_V, D_MODEL), MLA_HEADS * MLA_V),
        'w_branch_b': w((DEPTH, D_MODEL, D_MODEL), D_MODEL),
        'w_branch_c': w((DEPTH, GQA_Q_HEADS * GQA_HEAD_DIM, D_MODEL), GQA_Q_HEADS * GQA_HEAD_DIM),
        'w_out': w((DEPTH, D_MODEL, D_MODEL), D_MODEL),
        'norm_mix_post': gain(D_MODEL),
        'norm_ffn_pre': gain(D_MODEL),
        'w_ffn_gate': w((DEPTH, D_MODEL, FFN_DIM), D_MODEL),
        'w_ffn_up': w((DEPTH, D_MODEL, FFN_DIM), D_MODEL),
        'w_ffn_down': w((DEPTH, FFN_DIM, D_MODEL), FFN_DIM),
        'norm_ffn_post': gain(D_MODEL),
        'w_ple_proj': w((DEPTH, PLE_DIM, D_MODEL), PLE_DIM),
        'w_ple_gate': w((DEPTH, D_MODEL, D_MODEL), D_MODEL),
        'norm_ple': gain(D_MODEL),
    }


def reference(x, p, positions, norm_mix_pre, w_in, mla_q_norm, w_uq, mla_kv_norm, w_ukv,
              gqa_sink, w_branch_a, w_branch_b, w_branch_c, w_out, norm_mix_post,
              norm_ffn_pre, w_ffn_gate, w_ffn_up, w_ffn_down, norm_ffn_post,
              w_ple_proj, w_ple_gate, norm_ple):
    split_points = list(np.cumsum(IN_SPLITS)[:-1])
    for i in range(DEPTH):
        h = rms_norm(x, norm_mix_pre[i])
        z = h @ w_in[i]
        c_q, c_kv, k_rope, q_c, k_c, v_c, g_a, g_b, g_c = jnp.split(z, split_points, axis=-1)
        o_a = mla_attention(c_q, c_kv, k_rope, positions, mla_q_norm[i], w_uq[i],
                            mla_kv_norm[i], w_ukv[i])
        o_b = fourier_mix(h)
        o_c = windowed_gqa(q_c, k_c, v_c, positions, gqa_sink[i])
        merged = (jax.nn.sigmoid(g_a) * (o_a @ w_branch_a[i])
                  + jax.nn.sigmoid(g_b) * (o_b @ w_branch_b[i])
                  + jax.nn.sigmoid(g_c) * (o_c @ w_branch_c[i]))
        x = x + rms_norm(merged @ w_out[i], norm_mix_post[i])
        h = rms_norm(x, norm_ffn_pre[i])
        ff = (jax.nn.silu(h @ w_ffn_gate[i]) * (h @ w_ffn_up[i])) @ w_ffn_down[i]
        x = x + rms_norm(ff, norm_ffn_post[i])
        e = (p[i] @ w_ple_proj[i]) * jax.nn.sigmoid(x @ w_ple_gate[i])
        x = x + rms_norm(e, norm_ple[i])
    return x
```

```python
import numpy as np
import ml_dtypes
from contextlib import ExitStack
import concourse.bass as bass
import concourse.mybir as mybir
from concourse.bass_utils import run_bass_kernel_spmd

F32 = mybir.dt.float32
BF16 = mybir.dt.bfloat16
I32 = mybir.dt.int32
ALU = mybir.AluOpType
AF = mybir.ActivationFunctionType

D = 1024
SEQ = 8192
NB = 4
DEPTH = 2
TOK = 4096
NT = TOK // 128
NCH = TOK // 512
FFN = 2816
NF = FFN // 128
EPS = 1e-6
HQ = 8
SLOPES = [2.0 ** (-8.0 * (h + 1.0) / 8.0) for h in range(8)]
MLA_SCALE = 96 ** -0.5
NSMALL = 1472
O_CQ, O_CKV, O_KRA, O_KRB, O_QC, O_KC, O_VC = 0, 384, 512, 608, 704, 1216, 1344
PAIRS = [[0, 1], [2, 3], [4, 5], [6, 7]]
DBG = {}


def dbg(k):
    return k not in DBG.get('skip', ())

XROWS = 176


class Buf:
    __slots__ = ("name", "w", "r", "dsem", "px")

    def __init__(self, name):
        self.name = name
        self.w = None
        self.r = {}
        self.dsem = None
        self.px = False


class DSem:
    __slots__ = ("sem", "count", "key", "lazy")

    def __init__(self, sem, key):
        self.sem = sem
        self.count = 0
        self.key = key
        self.lazy = False


class Tl:
    __slots__ = ("t", "b", "bs")

    def __init__(self, t, name, nsub=0):
        self.t = t
        self.b = Buf(name)
        self.bs = [Buf(f"{name}.{i}") for i in range(nsub)]

    def __getitem__(self, k):
        return self.t[k]


class Ctx:
    def __init__(self, nc, es):
        self.nc = nc
        self.es = es
        self.eng = {"pe": nc.tensor, "act": nc.scalar, "dve": nc.vector, "pool": nc.gpsimd, "sp": nc.sync}
        self.psem = {}
        for k in self.eng:
            self.psem[k] = es.enter_context(nc.semaphore("p_" + k))
        self.cnt = {k: 0 for k in self.eng}
        self.known = {k: {} for k in self.eng}
        self.pending = {k: False for k in self.eng}
        self.dsems = []
        self.free_dsems = []
        self.n_ins = 0

    def new_dsem(self):
        if self.free_dsems:
            return self.free_dsems.pop()
        s = self.es.enter_context(self.nc.semaphore(f"d{len(self.dsems)}"))
        d = DSem(s, f"d{len(self.dsems)}")
        self.dsems.append(d)
        return d

    def release_dsem(self, d):
        self.free_dsems.append(d)

    def _wait(self, e, ev):
        key, sem, val = ev
        if self.known[e].get(key, 0) >= val:
            return
        self.eng[e].wait_ge(sem, val)
        self.known[e][key] = val

    def _deps(self, e, reads, writes, skip_key=None):
        for b in reads:
            if b.w is not None and b.w[0] != skip_key:
                self._wait(e, b.w)
            if b.px:
                for ev in b.r.values():
                    if ev[0] != e:
                        self._wait(e, ev)
        for b in writes:
            if b.w is not None and b.w[0] != skip_key:
                self._wait(e, b.w)
            for ev in b.r.values():
                if ev[0] != skip_key:
                    self._wait(e, ev)

    def op(self, e, fn, reads=(), writes=(), inc=True):
        skip = "pe" if e == "pe" else None
        self._deps(e, reads, writes, skip)
        ins = fn(self.eng[e])
        if inc:
            self.cnt[e] += 1
            ins.then_inc(self.psem[e], 1)
            ev = (e, self.psem[e], self.cnt[e])
            self.pending[e] = False
        else:
            ev = (e, self.psem[e], self.cnt[e] + 1)
            self.pending[e] = True
        for b in reads:
            b.r[e] = ev
        for b in writes:
            b.w = ev
            b.r = {}
        self.n_ins += 1
        return ins

    def dma(self, q, out, in_, reads=(), writes=(), **kw):
        b0 = writes[0]
        if b0.dsem is None:
            b0.dsem = self.new_dsem()
        ds = b0.dsem
        self._deps(q, reads, writes, ds.key)
        ins = self.eng[q].dma_start(out=out, in_=in_, **kw)
        ds.count += 16
        ins.then_inc(ds.sem, 16)
        ev = (ds.key, ds.sem, ds.count)
        for b in reads:
            b.r[ds.key] = ev
        for b in writes:
            b.w = ev
            b.r = {}
        self.n_ins += 1
        return ins

    def cc(self, kind, op, in_ap, out_ap, reads, writes):
        b0 = writes[0]
        if b0.dsem is None:
            b0.dsem = self.new_dsem()
        ds = b0.dsem
        ds.lazy = True
        self._deps("pool", reads, writes, None)
        ins = self.nc.gpsimd.collective_compute(kind, op, replica_groups=PAIRS, ins=[in_ap], outs=[out_ap])
        ds.count += 1
        ins.then_inc(ds.sem, 1)
        ev = (ds.key, ds.sem, ds.count)
        for b in reads:
            b.r[ds.key] = ev
        for b in writes:
            b.w = ev
            b.r = {}
        return ins

    def barrier(self):
        assert not any(self.pending.values()), self.pending
        for e in self.eng:
            for e2 in self.eng:
                if e2 != e and self.cnt[e2] > 0:
                    self._wait(e, (e2, self.psem[e2], self.cnt[e2]))
            for d in self.dsems:
                if d.count > 0 and not d.lazy:
                    self._wait(e, (d.key, d.sem, d.count))


class Phase:
    uid = 0

    def __init__(self, cx):
        self.cx = cx
        self.es = ExitStack()
        self.tiles = []

    def __enter__(self):
        self.es.__enter__()
        return self

    def __exit__(self, *a):
        self.cx.barrier()
        for t in self.tiles:
            for b in [t.b] + t.bs:
                if b.dsem is not None:
                    self.cx.release_dsem(b.dsem)
                    b.dsem = None
        return self.es.__exit__(*a)

    def sb(self, name, shape, dtype, nsub=0):
        Phase.uid += 1
        name = f"{name}_u{Phase.uid}"
        t = self.es.enter_context(self.cx.nc.sbuf_tensor(name, list(shape), dtype))
        tl = Tl(t, name, nsub)
        self.tiles.append(tl)
        return tl

    def ps(self, name, shape, dtype=F32):
        Phase.uid += 1
        name = f"{name}_u{Phase.uid}"
        t = self.es.enter_context(self.cx.nc.psum_tensor(name, list(shape), dtype))
        tl = Tl(t, name)
        tl.b.px = True
        self.tiles.append(tl)
        return tl


def skew(n, stages, lags=None):
    if lags is None:
        lags = list(range(len(stages)))
    for step in range(n + max(lags)):
        for st, lg in zip(stages, lags):
            t = step - lg
            if 0 <= t < n:
                st(t)


class Rot:
    def __init__(self, items):
        self.items = items
        self.i = 0

    def next(self):
        x = self.items[self.i % len(self.items)]
        self.i += 1
        return x


def build_program(dump=(), phases=None, nlayers=DEPTH):
    nc = bass.Bass("TRN2", target_bir_lowering=False)
    es = ExitStack()
    with es:
        cx = Ctx(nc, es)

        def din(name, shape, dt=F32):
            return nc.dram_tensor(name, list(shape), dt, kind="ExternalInput").ap()

        def dscr(name, shape, dt):
            kind = "ExternalOutput" if name in dump else "Internal"
            return Tl(nc.dram_tensor(name, list(shape), dt, kind=kind).ap(), name)

        x_in = din("x", [TOK, D])
        p_in = din("p", [DEPTH, TOK, 256])
        pos_in = din("pos", [1, TOK], I32)
        posx_in = din("posx", [34, 128], I32)
        w_small = din("w_small", [DEPTH, D, NSMALL])
        w_gates = din("w_gates", [DEPTH, D, 3 * D])
        w_uqr = din("w_uqr", [DEPTH, 384, 768])
        w_uqs = din("w_uqs", [DEPTH, 384, 768])
        w_ukv = din("w_ukv", [DEPTH, 128, 1024])
        w_a = din("w_a", [DEPTH, 512, D])
        w_b = din("w_b", [DEPTH, D, D])
        w_c = din("w_c", [DEPTH, 512, D])
        w_out = din("w_out", [DEPTH, D, D])
        w_fg = din("w_fg", [DEPTH, D, FFN])
        w_fu = din("w_fu", [DEPTH, D, FFN])
        w_fd = din("w_fd", [DEPTH, FFN, D])
        w_pp = din("w_pp", [DEPTH, 256, D])
        w_pg = din("w_pg", [DEPTH, D, D])
        g_mix_pre = din("g_mix_pre", [DEPTH, D])
        g_q = din("g_q", [DEPTH, 384])
        g_kv = din("g_kv", [DEPTH, 128])
        g_mix_post = din("g_mix_post", [DEPTH, D])
        g_ffn_pre = din("g_ffn_pre", [DEPTH, D])
        g_ffn_post = din("g_ffn_post", [DEPTH, D])
        g_ple = din("g_ple", [DEPTH, D])
        sink_in = din("sink", [DEPTH, 8])
        ident_in = din("ident", [128, 128], BF16)
        invf_in = din("invf", [32, 1])
        cdft_in = din("cdft", [2, 256, 256], BF16)
        G_in = din("Gmat", [32, 3, 128, 256], BF16)
        E_in = din("Emat", [2, 128, 128], BF16)
        wmask_in = din("wmask", [4, 128, 128])
        y_out = nc.dram_tensor("y", [TOK, D], F32, kind="ExternalOutput").ap()
        ybuf = Buf("y")

        xa = dscr("xa", [TOK, D], F32)
        xb = dscr("xb", [TOK, D], F32)
        xc = dscr("xc", [TOK, D], F32)
        act_d = dscr("act_d", [FFN, TOK], BF16)
        hT_d = dscr("hT_d", [D, TOK], BF16)
        Yr_d = dscr("Yr_d", [TOK, D], BF16)
        Yi_d = dscr("Yi_d", [TOK, D], BF16)
        Yn_d = dscr("Yn_d", [TOK, D], BF16)
        cqT_d = dscr("cqT_d", [384, TOK], BF16)
        xin_d = dscr("xin_d", [XROWS, TOK], BF16)
        xout_d = dscr("xout_d", [2 * XROWS, TOK], BF16)
        qcT_d = dscr("qcT_d", [128, 4, TOK], BF16)
        kcT_d = dscr("kcT_d", [128, TOK + 256], BF16)
        vc_d = dscr("vc_d", [TOK + 256, 128], BF16)
        oaT_d = dscr("oaT_d", [512, TOK], BF16)
        ocT_d = dscr("ocT_d", [512, TOK], BF16)
        Zr_d = dscr("Zr_d", [SEQ, D], BF16)
        Zi_d = dscr("Zi_d", [SEQ, D], BF16)
        Bp_d = dscr("Bp_d", [SEQ, D], F32)
        Brs_d = dscr("Brs_d", [TOK, D], F32)
        rope_d = dscr("rope_d", [2, 32, TOK], F32)
        xin_buf = Buf("x_in")

        def want(name):
            return phases is None or name in phases

        CAST_ENG = ["dve", "act"]
        cast_i = [0]
        def load_cast(ph, dst_ap_fn, src_ap_fn, nparts, ncols, stage, colchunk=2048):
            for c0 in range(0, ncols, colchunk):
                c1 = min(ncols, c0 + colchunk)
                st = stage.next()
                cx.dma("sp", st[0:nparts, 0:c1 - c0], src_ap_fn(c0, c1), reads=(), writes=(st.b,))
                dst, dbuf = dst_ap_fn(c0, c1)
                ce = CAST_ENG[cast_i[0] % len(CAST_ENG)]
                cast_i[0] += 1
                if ce == "act":
                    cx.op("act", lambda e, dst=dst, st=st, n=c1 - c0: e.copy(out=dst, in_=st[0:nparts, 0:n]),
                          reads=(st.b,), writes=(dbuf,))
                else:
                    cx.op(ce, lambda e, dst=dst, st=st, n=c1 - c0: e.tensor_copy(out=dst, in_=st[0:nparts, 0:n]),
                          reads=(st.b,), writes=(dbuf,))

        def load_w_kc(ph, wt, src, nkc, ncols, stage):
            for kc in range(nkc):
                load_cast(ph, lambda c0, c1, kc=kc: (wt[:, kc, c0:c1], wt.b),
                          lambda c0, c1, kc=kc: src[kc * 128:(kc + 1) * 128, c0:c1], 128, ncols, stage)

        def mk_stage(ph, n=3):
            return Rot([ph.sb(f"wstage{i}", [128, 2048], F32) for i in range(n)])

        def gain_bc(ph, name, src_row):
            n = src_row.shape[-1]
            t = ph.sb(name, [128, n], F32)
            cx.dma("sp", t[:], src_row.partition_broadcast(128), reads=(), writes=(t.b,))
            return t

        def rms_rstd_tokmajor(ph, src_ap, src_bufs, junk, ss, n):
            cx.op("act", lambda e: e.activation(out=junk[:, 0:n], in_=src_ap, func=AF.Square, accum_out=ss[:, 0:1]),
                  reads=src_bufs, writes=(junk.b, ss.b))
            cx.op("dve", lambda e: e.tensor_scalar(out=ss[:, 0:1], in0=ss[:, 0:1], scalar1=1.0 / n, scalar2=EPS,
                                                   op0=ALU.mult, op1=ALU.add), reads=(ss.b,), writes=(ss.b,))
            cx.op("act", lambda e: e.activation(out=ss[:, 0:1], in_=ss[:, 0:1], func=AF.Sqrt), reads=(ss.b,), writes=(ss.b,))
            cx.op("dve", lambda e: e.reciprocal(out=ss[:, 0:1], in_=ss[:, 0:1]), reads=(ss.b,), writes=(ss.b,))

        if want("P0"):
            with Phase(cx) as ph:
                posi = ph.sb("posi", [32, TOK], I32)
                posf = ph.sb("posf", [32, TOK], F32)
                ang = ph.sb("ang", [32, TOK], F32)
                nn = ph.sb("nn", [32, TOK], F32)
                tab = ph.sb("tab", [32, TOK], F32)
                invf = ph.sb("invf", [32, 1], F32)
                cx.dma("sp", posi[:], pos_in.partition_broadcast(32), reads=(), writes=(posi.b,))
                cx.dma("sp", invf[:], invf_in, reads=(), writes=(invf.b,))
                cx.op("dve", lambda e: e.tensor_copy(out=posf[:], in_=posi[:]), reads=(posi.b,), writes=(posf.b,))
                TWO_PI = 2.0 * np.pi
                C1 = 6.28125
                C2 = TWO_PI - C1
                MAGIC = 12582912.0
                for which in range(2):
                    shift = (np.pi / 2.0) if which == 0 else 0.0
                    cx.op("dve", lambda e, shift=shift: e.tensor_scalar(out=ang[:], in0=posf[:], scalar1=invf[:, 0:1], scalar2=shift,
                                                                       op0=ALU.mult, op1=ALU.add), reads=(posf.b, invf.b), writes=(ang.b,))
                    cx.op("dve", lambda e: e.tensor_scalar(out=nn[:], in0=ang[:], scalar1=1.0 / TWO_PI, scalar2=MAGIC,
                                                           op0=ALU.mult, op1=ALU.add), reads=(ang.b,), writes=(nn.b,))
                    cx.op("dve", lambda e: e.tensor_scalar(out=nn[:], in0=nn[:], scalar1=MAGIC, scalar2=None,
                                                           op0=ALU.subtract), reads=(nn.b,), writes=(nn.b,))
                    cx.op("dve", lambda e: e.scalar_tensor_tensor(out=ang[:], in0=nn[:], scalar=-C1, in1=ang[:],
                                                                  op0=ALU.mult, op1=ALU.add), reads=(nn.b, ang.b), writes=(ang.b,))
                    cx.op("dve", lambda e: e.scalar_tensor_tensor(out=ang[:], in0=nn[:], scalar=-C2, in1=ang[:],
                                                                  op0=ALU.mult, op1=ALU.add), reads=(nn.b, ang.b), writes=(ang.b,))
                    cx.op("dve", lambda e: e.tensor_scalar(out=ang[:], in0=ang[:], scalar1=3.1415925, scalar2=-3.1415925,
                                                           op0=ALU.min, op1=ALU.max), reads=(ang.b,), writes=(ang.b,))
                    cx.op("act", lambda e: e.activation(out=tab[:], in_=ang[:], func=AF.Sin), reads=(ang.b,), writes=(tab.b,))
                    if which == 1:
                        cx.op("dve", lambda e: e.tensor_scalar(out=tab[0:16, :], in0=tab[0:16, :], scalar1=-1.0, scalar2=None,
                                                               op0=ALU.mult), reads=(tab.b,), writes=(tab.b,))
                    cx.dma("pool", rope_d[which], tab[:], reads=(tab.b,), writes=(rope_d.b,))

        for l in range(nlayers):
            x_src, x_src_b = (x_in, xin_buf) if l == 0 else (xc.t, xc.b)
            if want("P1"):
                with Phase(cx) as ph:
                    stage = mk_stage(ph)
                    Ws = ph.sb("Ws", [128, 8, NSMALL], BF16)
                    load_w_kc(ph, Ws, w_small[l], 8, NSMALL, stage)
                    Wb = ph.sb("Wb", [128, 8, D], BF16)
                    load_w_kc(ph, Wb, w_b[l], 8, D, stage)
                    cd = ph.sb("cd", [128, 2, 2, 256], BF16)
                    cx.dma("sp", cd[:], cdft_in.rearrange("w (kr p) c -> p w kr c", p=128), reads=(), writes=(cd.b,))
                    Mri = ph.sb("Mri", [128, 2, 8, D], BF16)
                    ident = ph.sb("ident", [128, 128], BF16)
                    cx.dma("sp", ident[:], ident_in, reads=(), writes=(ident.b,))
                    ones = ph.sb("ones", [128, 128], BF16)
                    cx.op("dve", lambda e: e.memset(ones[:], 1.0), writes=(ones.b,))
                    gpre = gain_bc(ph, "gpre", g_mix_pre[l:l + 1, :])
                    gq = ph.sb("gq", [128, 3], F32)
                    cx.dma("sp", gq[:], g_q[l].rearrange("(c p) -> p c", p=128), reads=(), writes=(gq.b,), allow_slow_non_contiguous=True)
                    gkv = ph.sb("gkv", [128, 1], F32)
                    cx.dma("sp", gkv[:], g_kv[l].rearrange("(c p) -> p c", p=128), reads=(), writes=(gkv.b,), allow_slow_non_contiguous=True)
                    ropet = ph.sb("ropet", [96, 2, TOK], F32)
                    cx.dma("sp", ropet[64:96], rope_d.t.rearrange("w p t -> p w t"), reads=(rope_d.b,), writes=(ropet.b,))

                    pF = Rot([ph.ps(f"pF{i}", [128, 512]) for i in range(4)])
                    pY = ph.ps("pY", [128, 1024])
                    pT = ph.ps("pT", [128, 8, 128], BF16)
                    pT2 = ph.ps("pT2", [128, 8, 128], BF16)

                    for which in range(2):
                        for m in range(8):
                            gr, mc = m // 2, m % 2
                            for n in range(2):
                                for kr in range(2):
                                    cx.op("pe", lambda e, which=which, kr=kr, mc=mc, gr=gr, n=n: e.matmul(
                                        out=pY[:, n * 512:(n + 1) * 512], lhsT=cd[:, which, kr, mc * 128:(mc + 1) * 128],
                                        rhs=Wb[:, 2 * gr + kr, n * 512:(n + 1) * 512], start=(kr == 0), stop=(kr == 1)),
                                        inc=(kr == 1), reads=(cd.b, Wb.b), writes=(pY.b,))
                            cx.op("act", lambda e, which=which, m=m: e.copy(out=Mri[:, which, m, :], in_=pY[:]),
                                  reads=(pY.b,), writes=(Mri.b,))

                    xts = Rot([ph.sb(f"xt{i}", [128, D], F32) for i in range(3)])
                    junk = ph.sb("junk", [128, D], F32)
                    sss = Rot([ph.sb(f"ss{i}", [128, 1], F32) for i in range(3)])
                    hbs = Rot([ph.sb(f"hb{i}", [128, D], BF16) for i in range(2)])
                    hTs = Rot([ph.sb(f"hT{i}", [128, 8, 512], BF16, nsub=4) for i in range(2)])
                    ysb = Rot([ph.sb(f"ysb{i}", [128, D], BF16) for i in range(3)])
                    vst = Rot([ph.sb(f"vst{i}", [128, 4, 128], BF16) for i in range(2)])
                    sqs = [ph.sb(f"sq{i}", [128, 512], BF16) for i in range(3)]
                    msb = ph.sb("msb", [128, 512], F32)
                    cqst = Rot([ph.sb(f"cqst{i}", [128, 3, 512], BF16) for i in range(2)])
                    fst = Rot([ph.sb(f"fst{i}", [128, 512], BF16) for i in range(3)])
                    r1 = ph.sb("r1", [96, 512], F32)
                    r2 = ph.sb("r2", [96, 512], F32)
                    krst = Rot([ph.sb(f"krst{i}", [96, 512], BF16) for i in range(2)])

                    pTs = [pT, pT2]
                    hT_l = hTs.items
                    hb_l = hbs.items
                    chunk_pV = {}

                    xt_l = xts.items

                    def st_ld(t):
                        xt = xt_l[t % 3]
                        cx.dma("sp", xt[:], x_src[t * 128:(t + 1) * 128, :], reads=(x_src_b,), writes=(xt.b,))

                    def st_a(t):
                        xt = xt_l[t % 3]
                        ss = sss.next()
                        hb = hb_l[t % 2]
                        rms_rstd_tokmajor(ph, xt[:], (xt.b,), junk, ss, D)
                        cx.op("dve", lambda e: e.scalar_tensor_tensor(
                            out=hb[:], in0=xt[:], scalar=ss[:, 0:1], in1=gpre[:], op0=ALU.mult, op1=ALU.mult),
                            reads=(xt.b, ss.b, gpre.b), writes=(hb.b,))

                    def st_b(t):
                        ci, tt = t // 4, t % 4
                        hb, hT, pT_ = hb_l[t % 2], hT_l[ci % 2], pTs[t % 2]
                        for kc in range(8):
                            cx.op("pe", lambda e, kc=kc: e.transpose(out=pT_[:, kc, :], in_=hb[:, kc * 128:(kc + 1) * 128], identity=ident[:]),
                                  inc=(kc == 7), reads=(hb.b, ident.b), writes=(pT_.b,))
                        cx.op("act", lambda e: e.copy(out=hT[:, :, tt * 128:(tt + 1) * 128], in_=pT_[:]),
                              reads=(pT_.b,), writes=(hT.bs[tt],))

                    def st_c(t):
                        ci, tt = t // 4, t % 4
                        hT = hT_l[ci % 2]
                        if tt == 0:
                            chunk_pV[ci] = pF.next()
                        pV = chunk_pV[ci]
                        for kc in range(8):
                            cx.op("pe", lambda e, kc=kc: e.matmul(
                                out=pV[:, tt * 128:(tt + 1) * 128], lhsT=hT[:, kc, tt * 128:(tt + 1) * 128],
                                rhs=Ws[:, kc, O_VC:O_VC + 128], start=(kc == 0), stop=(kc == 7)),
                                inc=(kc == 7), reads=(hT.bs[tt], Ws.b), writes=(pV.b,))
                        for which in range(2):
                            for n in range(2):
                                for kc in range(8):
                                    cx.op("pe", lambda e, kc=kc, n=n, which=which: e.matmul(
                                        out=pY[:, n * 512:(n + 1) * 512], lhsT=hT[:, kc, tt * 128:(tt + 1) * 128],
                                        rhs=Mri[:, which, kc, n * 512:(n + 1) * 512], start=(kc == 0), stop=(kc == 7)),
                                        inc=(kc == 7), reads=(hT.bs[tt], Mri.b), writes=(pY.b,))
                            if which == 0:
                                ys = ysb.next()
                                cx.op("act", lambda e, ys=ys: e.copy(out=ys[:], in_=pY[:]), reads=(pY.b,), writes=(ys.b,))
                                cx.dma("pool", Yr_d[t * 128:(t + 1) * 128, :], ys[:], reads=(ys.b,), writes=(Yr_d.b,))
                            else:
                                ys = ysb.next()
                                cx.op("dve", lambda e, ys=ys: e.tensor_copy(out=ys[:], in_=pY[:]), reads=(pY.b,), writes=(ys.b,))
                                cx.dma("pool", Yi_d[t * 128:(t + 1) * 128, :], ys[:], reads=(ys.b,), writes=(Yi_d.b,))
                        if tt == 3:
                            chunk_level(ci, hT, pV)

                    def chunk_level(ci, hT, pV):
                        tok0 = ci * 512
                        for _once in (0,):
                            vs = vst.next()
                            cx.op("dve", lambda e, vs=vs, pV=pV: e.tensor_copy(out=vs[:], in_=pV[:].rearrange("p (t c) -> p t c", t=4)),
                                  reads=(pV.b,), writes=(vs.b,))
                            cx.dma("pool", vc_d[128 + tok0:128 + tok0 + 512, :].rearrange("(t p) c -> p t c", p=128), vs[:],
                                   reads=(vs.b,), writes=(vc_d.b,))
                            cx.dma("pool", hT_d.t.rearrange("(kc p) t -> p kc t", p=128)[:, :, tok0:tok0 + 512], hT[:],
                                   reads=tuple(hT.bs), writes=(hT_d.b,))

                            def fm_proj(col0, m, pbank, hT=hT):
                                for kc in range(8):
                                    cx.op("pe", lambda e, kc=kc: e.matmul(out=pbank[0:m, :], lhsT=Ws[:, kc, col0:col0 + m], rhs=hT[:, kc, :],
                                                                           start=(kc == 0), stop=(kc == 7)),
                                          inc=(kc == 7), reads=tuple(hT.bs) + (Ws.b,), writes=(pbank.b,))

                            def fm_norm(banks, nfeat, gcol, dst_fn):
                                pS = pF.next()
                                for c, bk in enumerate(banks):
                                    cx.op("act", lambda e, c=c, bk=bk: e.activation(out=sqs[c][:], in_=bk[:], func=AF.Square),
                                          reads=(bk.b,), writes=(sqs[c].b,))
                                for c in range(len(banks)):
                                    cx.op("pe", lambda e, c=c: e.matmul(out=pS[:], lhsT=ones[:], rhs=sqs[c][:], start=(c == 0),
                                                                        stop=(c == len(banks) - 1)), reads=(ones.b, sqs[c].b), writes=(pS.b,))
                                cx.op("dve", lambda e: e.tensor_scalar(out=msb[:], in0=pS[:], scalar1=1.0 / nfeat, scalar2=EPS,
                                                                       op0=ALU.mult, op1=ALU.add), reads=(pS.b,), writes=(msb.b,))
                                cx.op("act", lambda e: e.activation(out=msb[:], in_=msb[:], func=AF.Sqrt), reads=(msb.b,), writes=(msb.b,))
                                cx.op("dve", lambda e: e.reciprocal(out=msb[:], in_=msb[:]), reads=(msb.b,), writes=(msb.b,))
                                for c, bk in enumerate(banks):
                                    dst, dbuf = dst_fn(c)
                                    cx.op("dve", lambda e, c=c, bk=bk, dst=dst: e.scalar_tensor_tensor(
                                        out=dst, in0=bk[:], scalar=gcol[:, c:c + 1], in1=msb[:], op0=ALU.mult, op1=ALU.mult),
                                        reads=(bk.b, gcol.b, msb.b), writes=(dbuf,))

                            banks = [pF.next() for _ in range(3)]
                            for c in range(3):
                                fm_proj(O_CQ + c * 128, 128, banks[c])
                            cq = cqst.next()
                            fm_norm(banks, 384, gq, lambda c: (cq[:, c, :], cq.b))
                            cx.dma("pool", cqT_d.t.rearrange("(c p) t -> p c t", p=128)[:, :, tok0:tok0 + 512], cq[:],
                                   reads=(cq.b,), writes=(cqT_d.b,))
                            bk = pF.next()
                            fm_proj(O_CKV, 128, bk)
                            f1 = fst.next()
                            fm_norm([bk], 128, gkv, lambda c: (f1[:], f1.b))
                            cx.dma("pool", xin_d[0:128, tok0:tok0 + 512], f1[:], reads=(f1.b,), writes=(xin_d.b,))
                            bA = pF.next()
                            bB = pF.next()
                            fm_proj(O_KRA, 96, bA)
                            fm_proj(O_KRB, 96, bB)
                            cx.op("dve", lambda e, bA=bA: e.tensor_tensor(out=r1[64:96, :], in0=bA[64:96, :], in1=ropet[64:96, 0, tok0:tok0 + 512],
                                                                          op=ALU.mult), reads=(bA.b, ropet.b), writes=(r1.b,))
                            cx.op("dve", lambda e, bB=bB: e.tensor_tensor(out=r2[64:96, :], in0=bB[64:96, :], in1=ropet[64:96, 1, tok0:tok0 + 512],
                                                                          op=ALU.mult), reads=(bB.b, ropet.b), writes=(r2.b,))
                            kr = krst.next()
                            cx.op("dve", lambda e, kr=kr: e.tensor_tensor(out=kr[64:96, :], in0=r1[64:96, :], in1=r2[64:96, :], op=ALU.add),
                                  reads=(r1.b, r2.b), writes=(kr.b,))
                            cx.dma("pool", xin_d[128:160, tok0:tok0 + 512], kr[64:96, :], reads=(kr.b,), writes=(xin_d.b,))
                            for g in range(4):
                                bk = pF.next()
                                fm_proj(O_QC + g * 128, 128, bk)
                                f1 = fst.next()
                                cx.op("act", lambda e, f1=f1, bk=bk: e.mul(f1[:], bk[:], 0.125),
                                      reads=(bk.b,), writes=(f1.b,))
                                cx.dma("pool", qcT_d[:, g, tok0:tok0 + 512], f1[:], reads=(f1.b,), writes=(qcT_d.b,))
                            bk = pF.next()
                            fm_proj(O_KC, 128, bk)
                            f1 = fst.next()
                            cx.op("act", lambda e, f1=f1, bk=bk: e.copy(out=f1[:], in_=bk[:]), reads=(bk.b,), writes=(f1.b,))
                            cx.dma("pool", kcT_d[:, 128 + tok0:128 + tok0 + 512], f1[:], reads=(f1.b,), writes=(kcT_d.b,))

                    skew(NT, [st_ld, st_a, st_b, st_c], [0, 2, 3, 4])

                    if dbg('misc'):
                        misc = xin_d.t[160:176, :].rearrange("r (q c) -> (r q) c", c=128)
                        cx.dma("pool", misc[0:128, :], kcT_d[:, 128:256], reads=(kcT_d.b,), writes=(xin_d.b,))
                        cx.dma("pool", misc[128:256, :], kcT_d[:, TOK:TOK + 128], reads=(kcT_d.b,), writes=(xin_d.b,))
                        cx.dma("pool", misc[256:384, :], vc_d[128:256, :], reads=(vc_d.b,), writes=(xin_d.b,))
                        cx.dma("pool", misc[384:512, :], vc_d[TOK:TOK + 128, :], reads=(vc_d.b,), writes=(xin_d.b,))

            if want("P2"):
                cx.cc("AllGather", ALU.bypass, xin_d.t, xout_d.t, reads=(xin_d.b,), writes=(xout_d.b,))

            if want("P5"):
                with Phase(cx) as ph:
                    Gs = Rot([ph.sb(f"G{i}", [128, 3, 256], BF16) for i in range(2)])
                    Yt = Rot([ph.sb(f"Yt{i}", [128, 2, D], BF16) for i in range(2)])
                    zst = Rot([ph.sb(f"zst{i}", [128, D], BF16) for i in range(4)])
                    pZ = Rot([ph.ps(f"pZ{i}", [128, 1024]) for i in range(4)])
                    Yv = [y.t.rearrange("(s1 s2) c -> s2 s1 c", s2=32) for y in (Yr_d, Yi_d)]
                    Zv = [z.t.rearrange("(k1 s2) c -> s2 k1 c", s2=32) for z in (Zr_d, Zi_d)]
                    for s2 in range(32):
                        G = Gs.next()
                        Y = Yt.next()
                        cx.dma("sp", G[:], G_in[s2].rearrange("w p c -> p w c"), reads=(), writes=(G.b,))
                        for w in range(2):
                            cx.dma("sp", Y[:, w, :], Yv[w][s2], reads=(Yr_d.b, Yi_d.b), writes=(Y.b,))
                        for m in range(2):
                            for part in range(2):
                                pz = pZ.next()
                                srcs = ((0, 0), (2, 1)) if part == 0 else ((0, 1), (1, 0))
                                for n in range(2):
                                    for i, (gw, yw) in enumerate(srcs):
                                        cx.op("pe", lambda e, pz=pz, G=G, Y=Y, gw=gw, yw=yw, n=n, i=i, m=m: e.matmul(
                                            out=pz[:, n * 512:(n + 1) * 512], lhsT=G[:, gw, m * 128:(m + 1) * 128],
                                            rhs=Y[:, yw, n * 512:(n + 1) * 512], start=(i == 0), stop=(i == 1)),
                                            inc=(i == 1), reads=(G.b, Y.b), writes=(pz.b,))
                                z = zst.next()
                                eng = "act" if part == 0 else "dve"
                                if eng == "act":
                                    cx.op("act", lambda e, z=z, pz=pz: e.copy(out=z[:], in_=pz[:]), reads=(pz.b,), writes=(z.b,))
                                else:
                                    cx.op("dve", lambda e, z=z, pz=pz: e.tensor_copy(out=z[:], in_=pz[:]), reads=(pz.b,), writes=(z.b,))
                                cx.dma("pool", Zv[part][s2, m * 128:(m + 1) * 128, :], z[:], reads=(z.b,), writes=((Zr_d, Zi_d)[part].b,))
                    cx.barrier()
                    Eb = ph.sb("Eb", [128, 2, 128], BF16)
                    cx.dma("sp", Eb[:], E_in.rearrange("w p c -> p w c"), reads=(), writes=(Eb.b,))
                    Zt = Rot([ph.sb(f"Zt{i}", [128, 2, D], BF16) for i in range(2)])
                    bst = Rot([ph.sb(f"bst{i}", [128, D], F32) for i in range(3)])
                    Bv = Bp_d.t.rearrange("(k2 r) c -> r k2 c", r=256)
                    for jj in range(64):
                        Z = Zt.next()
                        cx.dma("sp", Z[:, 0, :], Zr_d[jj * 128:(jj + 1) * 128, :], reads=(Zr_d.b,), writes=(Z.b,))
                        cx.dma("sp", Z[:, 1, :], Zi_d[jj * 128:(jj + 1) * 128, :], reads=(Zi_d.b,), writes=(Z.b,))
                        pz = pZ.next()
                        for n in range(2):
                            for w in range(2):
                                cx.op("pe", lambda e, pz=pz, Z=Z, n=n, w=w: e.matmul(out=pz[:, n * 512:(n + 1) * 512], lhsT=Eb[:, w, :],
                                                                                  rhs=Z[:, w, n * 512:(n + 1) * 512], start=(w == 0), stop=(w == 1)),
                                      inc=(w == 1), reads=(Eb.b, Z.b), writes=(pz.b,))
                        bs_ = bst.next()
                        if jj % 2 == 0:
                            cx.op("act", lambda e, bs_=bs_, pz=pz: e.copy(out=bs_[:], in_=pz[:]), reads=(pz.b,), writes=(bs_.b,))
                        else:
                            cx.op("dve", lambda e, bs_=bs_, pz=pz: e.tensor_copy(out=bs_[:], in_=pz[:]), reads=(pz.b,), writes=(bs_.b,))
                        for ks in range(4):
                            cx.dma("pool", Bv[4 * jj + ks], bs_[ks * 32:(ks + 1) * 32, :], reads=(bs_.b,), writes=(Bp_d.b,))
                    cx.cc("ReduceScatter", ALU.add, Bp_d.t, Brs_d.t, reads=(Bp_d.b,), writes=(Brs_d.b,))

            if want("P2"):
                m0 = xout_d.t[160:176, :].rearrange("r (q c) -> (r q) c", c=128)
                m1 = xout_d.t[XROWS + 160:XROWS + 176, :].rearrange("r (q c) -> (r q) c", c=128)
                cx.dma("pool", kcT_d[:, 0:128], m0[128:256, :], reads=(xout_d.b,), writes=(kcT_d.b,))
                cx.dma("pool", kcT_d[:, TOK + 128:TOK + 256], m1[0:128, :], reads=(xout_d.b,), writes=(kcT_d.b,))
                cx.dma("pool", vc_d[0:128, :], m0[384:512, :], reads=(xout_d.b,), writes=(vc_d.b,))
                cx.dma("pool", vc_d[TOK + 128:TOK + 256, :], m1[256:384, :], reads=(xout_d.b,), writes=(vc_d.b,))

            if want("P3"):
                with Phase(cx) as ph:
                    stage = mk_stage(ph)
                    Wq = ph.sb("Wq", [128, 2, 3, 768], BF16)
                    for which, src in enumerate((w_uqr, w_uqs)):
                        for c in range(3):
                            load_cast(ph, lambda c0, c1, which=which, c=c: (Wq[:, which, c, c0:c1], Wq.b),
                                      lambda c0, c1, src=src, c=c: src[l, c * 128:(c + 1) * 128, c0:c1], 128, 768, stage)
                    Wkv = ph.sb("Wkv", [128, 1024], BF16)
                    load_cast(ph, lambda c0, c1: (Wkv[:, c0:c1], Wkv.b), lambda c0, c1: w_ukv[l, :, c0:c1], 128, 1024, stage)
                    cqn = ph.sb("cqn", [128, 3, TOK], BF16)
                    cx.dma("sp", cqn[:], cqT_d.t.rearrange("(c p) t -> p c t", p=128), reads=(cqT_d.b,), writes=(cqn.b,))
                    ckv = ph.sb("ckv", [128, SEQ], BF16)
                    KTs = [ph.sb(f"KT{i}", [96, SEQ], BF16) for i in range(2)]
                    for r in range(2):
                        cx.dma("sp", ckv[:, r * TOK:(r + 1) * TOK], xout_d[r * XROWS:r * XROWS + 128, :], reads=(xout_d.b,), writes=(ckv.b,))
                        for i in range(2):
                            cx.dma("sp", KTs[i][64:96, r * TOK:(r + 1) * TOK], xout_d[r * XROWS + 128:r * XROWS + 160, :],
                                   reads=(xout_d.b,), writes=(KTs[i].b,))
                    Vs = [ph.sb(f"V{i}", [128, 64, 65], BF16) for i in range(2)]
                    for i in range(2):
                        cx.op("pool", lambda e, i=i: e.memset(Vs[i][:, :, 64:65], 1.0), writes=(Vs[i].b,))
                    QTs = [ph.sb(f"QT{i}", [96, TOK], BF16) for i in range(2)]
                    ropet = ph.sb("ropet", [96, 2, TOK], F32)
                    cx.dma("sp", ropet[64:96], rope_d.t.rearrange("w p t -> p w t"), reads=(rope_d.b,), writes=(ropet.b,))
                    r1 = ph.sb("r1", [96, 512], F32)
                    r2 = ph.sb("r2", [96, 512], F32)
                    ones1 = ph.sb("ones1", [65, 64], F32)
                    cx.op("dve", lambda e: e.memset(ones1[:], 1.0), writes=(ones1.b,))
                    PT2 = Rot([ph.sb(f"PT{i}", [128, 1024], BF16) for i in range(4)])
                    osbs = Rot([ph.sb(f"osb{i}", [65, 512], F32) for i in range(2)])
                    ost = Rot([ph.sb(f"ost{i}", [64, 512], BF16) for i in range(2)])
                    pS2 = Rot([ph.ps(f"pS{i}", [128, 1024]) for i in range(3)])
                    pO = Rot([ph.ps(f"pO{i}", [128, 512]) for i in range(1)])
                    pX = Rot([ph.ps(f"pX{i}", [128, 512]) for i in range(1)])

                    half_state = [None, 0]

                    def half_bank():
                        if half_state[0] is None or half_state[1] == 2:
                            half_state[0] = pS2.next()
                            half_state[1] = 0
                        t_, o_ = half_state[0], half_state[1] * 512
                        half_state[1] += 1
                        return t_, o_

                    def prep_head_gen(h):
                        KT, V, QT = KTs[h % 2], Vs[h % 2], QTs[h % 2]
                        for ch in range(16):
                            bk, o = half_bank()
                            cx.op("pe", lambda e, bk=bk, o=o, ch=ch: e.matmul(out=bk[0:64, o:o + 512], lhsT=Wkv[:, h * 128:h * 128 + 64], rhs=ckv[:, ch * 512:(ch + 1) * 512],
                                                                             start=True, stop=True), reads=(Wkv.b, ckv.b), writes=(bk.b,))
                            cx.op("dve", lambda e, bk=bk, o=o, ch=ch: e.tensor_copy(out=KT[0:64, ch * 512:(ch + 1) * 512], in_=bk[0:64, o:o + 512]),
                                  reads=(bk.b,), writes=(KT.b,))
                            yield
                        for g8 in range(8):
                            bk, o = half_bank()
                            for j in range(8):
                                kt = g8 * 8 + j
                                cx.op("pe", lambda e, bk=bk, o=o, kt=kt, j=j: e.matmul(out=bk[:, o + j * 64:o + (j + 1) * 64], lhsT=ckv[:, kt * 128:(kt + 1) * 128],
                                                                                      rhs=Wkv[:, h * 128 + 64:h * 128 + 128], start=True, stop=True),
                                      inc=(j == 7), reads=(Wkv.b, ckv.b), writes=(bk.b,))
                            cx.op("dve", lambda e, bk=bk, o=o, g8=g8: e.tensor_copy(out=V[:, g8 * 8:(g8 + 1) * 8, 0:64],
                                                                                   in_=bk[:, o:o + 512].rearrange("p (j c) -> p j c", j=8)),
                                  reads=(bk.b,), writes=(V.b,))
                            yield
                        for ch in range(8):
                            half_state[1] = 2
                            bk, oA = half_bank()
                            _, oB = half_bank()
                            for which, o in ((0, oA), (1, oB)):
                                for c in range(3):
                                    cx.op("pe", lambda e, which=which, o=o, c=c, ch=ch: e.matmul(
                                        out=bk[0:96, o:o + 512], lhsT=Wq[:, which, c, h * 96:(h + 1) * 96], rhs=cqn[:, c, ch * 512:(ch + 1) * 512],
                                        start=(c == 0), stop=(c == 2)), inc=(c == 2), reads=(Wq.b, cqn.b), writes=(bk.b,))
                            cx.op("dve", lambda e, ch=ch: e.tensor_copy(out=QT[0:64, ch * 512:(ch + 1) * 512], in_=bk[0:64, oA:oA + 512]),
                                  reads=(bk.b,), writes=(QT.b,))
                            cx.op("dve", lambda e, ch=ch: e.tensor_tensor(out=r1[64:96, :], in0=bk[64:96, oA:oA + 512], in1=ropet[64:96, 0, ch * 512:(ch + 1) * 512],
                                                                          op=ALU.mult), reads=(bk.b, ropet.b), writes=(r1.b,))
                            cx.op("dve", lambda e, ch=ch: e.tensor_tensor(out=r2[64:96, :], in0=bk[64:96, oB:oB + 512], in1=ropet[64:96, 1, ch * 512:(ch + 1) * 512],
                                                                          op=ALU.mult), reads=(bk.b, ropet.b), writes=(r2.b,))
                            cx.op("dve", lambda e, ch=ch: e.tensor_tensor(out=QT[64:96, ch * 512:(ch + 1) * 512], in0=r1[64:96, :], in1=r2[64:96, :], op=ALU.add),
                                  reads=(r1.b, r2.b), writes=(QT.b,))
                            yield

                    def prep_head(h):
                        for _ in prep_head_gen(h):
                            pass

                    LOOK = 2
                    items = [(h, qc, pr) for h in range(8) for qc in range(8) for pr in range(32)]
                    pend = []
                    inflight = []
                    cur_po = {}

                    def s_stage(h, qc, pr):
                        KT, QT = KTs[h % 2], QTs[h % 2]
                        ps2 = pS2.next()
                        pt2 = PT2.next()
                        for j in range(2):
                            kt = 2 * pr + j
                            cx.op("pe", lambda e, j=j, kt=kt: e.matmul(out=ps2[:, j * 512:(j + 1) * 512], lhsT=KT[:, kt * 128:(kt + 1) * 128],
                                                                       rhs=QT[:, qc * 512:(qc + 1) * 512], start=True, stop=True),
                                  inc=(j == 1), reads=(KT.b, QT.b), writes=(ps2.b,))
                        cx.op("act", lambda e: e.activation(out=pt2[:], in_=ps2[:], func=AF.Exp, scale=MLA_SCALE),
                              reads=(ps2.b,), writes=(pt2.b,))
                        return pt2

                    def pv_stage(h, qc, pr, pt2, step):
                        V = Vs[h % 2]
                        if pr == 0:
                            cur_po[(h, qc)] = pO.next()
                        po = cur_po[(h, qc)]
                        for j in range(2):
                            kt = 2 * pr + j
                            cx.op("pe", lambda e, j=j, kt=kt: e.matmul(out=po[0:65, :], lhsT=V[:, kt, :], rhs=pt2[:, j * 512:(j + 1) * 512],
                                                                       start=(kt == 0), stop=(kt == 63)),
                                  inc=(j == 1), reads=(V.b, pt2.b), writes=(po.b,))
                        if pr == 31:
                            osb = osbs.next()
                            cx.op("dve", lambda e: e.tensor_copy(out=osb[:], in_=po[0:65, :]), reads=(po.b,), writes=(osb.b,))

                            def fin_a(osb=osb):
                                cx.op("act", lambda e: e.activation(out=osb[64:65, :], in_=osb[64:65, :], func=AF.Ln), reads=(osb.b,), writes=(osb.b,))
                                cx.op("act", lambda e: e.activation(out=osb[64:65, :], in_=osb[64:65, :], func=AF.Exp, scale=-1.0), reads=(osb.b,), writes=(osb.b,))
                            pend.append((step + 2, fin_a))

                            def fin(h=h, qc=qc, osb=osb):
                                bc = pX.next()
                                cx.op("pe", lambda e: e.matmul(out=bc[0:64, :], lhsT=ones1[64:65, :], rhs=osb[64:65, :], start=True, stop=True),
                                      reads=(ones1.b, osb.b), writes=(bc.b,))
                                o = ost.next()
                                cx.op("dve", lambda e: e.tensor_tensor(out=o[:], in0=osb[0:64, :], in1=bc[0:64, :], op=ALU.mult),
                                      reads=(osb.b, bc.b), writes=(o.b,))
                                cx.dma("pool", oaT_d[h * 64:(h + 1) * 64, qc * 512:(qc + 1) * 512], o[:], reads=(o.b,), writes=(oaT_d.b,))
                            pend.append((step + 6, fin))

                    prep_head(0)
                    prep_head(1)
                    prep_gen = [None]
                    n_it = len(items)
                    for step in range(n_it + LOOK + 8):
                        if step < n_it:
                            h, qc, pr = items[step]
                            inflight.append((items[step], s_stage(h, qc, pr)))
                        if step >= LOOK and inflight and step - LOOK < n_it:
                            (h, qc, pr), pt2 = inflight.pop(0)
                            pv_stage(h, qc, pr, pt2, step)
                            if qc == 7 and pr == 31 and h + 2 < 8:
                                assert prep_gen[0] is None
                                prep_gen[0] = prep_head_gen(h + 2)
                        if prep_gen[0] is not None and step % 7 == 0:
                            try:
                                next(prep_gen[0])
                            except StopIteration:
                                prep_gen[0] = None
                        while pend and pend[0][0] <= step:
                            pend.pop(0)[1]()
                    assert not pend and not inflight and prep_gen[0] is None

            if want("P4"):
                with Phase(cx) as ph:
                    kcT = ph.sb("kcT", [128, 2, TOK + 256], BF16)
                    cx.op("dve", lambda e: e.memset(kcT[64:128, 0, :], 0.0), writes=(kcT.b,))
                    cx.op("dve", lambda e: e.memset(kcT[0:64, 1, :], 0.0), reads=(kcT.b,), writes=(kcT.b,))
                    cx.dma("sp", kcT[0:64, 0, :], kcT_d[0:64, :], reads=(kcT_d.b,), writes=(kcT.b,))
                    cx.dma("sp", kcT[64:128, 1, :], kcT_d[64:128, :], reads=(kcT_d.b,), writes=(kcT.b,))
                    vc = ph.sb("vc", [128, 34, 2, 65], BF16)
                    cx.op("pool", lambda e: e.memset(vc[:, :, :, 64:65], 1.0), writes=(vc.b,))
                    for kvh in range(2):
                        cx.dma("sp", vc[:, :, kvh, 0:64], vc_d.t.rearrange("(t p) c -> p t c", p=128)[:, :, kvh * 64:(kvh + 1) * 64],
                               reads=(vc_d.b,), writes=(vc.b,))
                    qcT = ph.sb("qcT", [128, 4, TOK], BF16)
                    cx.dma("sp", qcT[:], qcT_d.t, reads=(qcT_d.b,), writes=(qcT.b,))
                    posq_i = ph.sb("posq_i", [128, TOK], I32)
                    cx.dma("sp", posq_i[:], pos_in.partition_broadcast(128), reads=(), writes=(posq_i.b,))
                    posq = ph.sb("posq", [128, TOK], F32)
                    cx.op("dve", lambda e: e.tensor_copy(out=posq[:], in_=posq_i[:]), reads=(posq_i.b,), writes=(posq.b,))
                    posk_i = ph.sb("posk_i", [128, 34], I32)
                    cx.dma("sp", posk_i[:], posx_in.rearrange("t p -> p t"), reads=(), writes=(posk_i.b,), allow_slow_non_contiguous=True)
                    posk = ph.sb("posk", [128, 34], F32)
                    cx.op("dve", lambda e: e.tensor_copy(out=posk[:], in_=posk_i[:]), reads=(posk_i.b,), writes=(posk.b,))
                    cx.op("dve", lambda e: e.tensor_scalar(out=posk[:], in0=posk[:], scalar1=-1.0, scalar2=None, op0=ALU.mult),
                          reads=(posk.b,), writes=(posk.b,))
                    wm = ph.sb("wm", [128, 4, 128], F32)
                    cx.dma("sp", wm[:], wmask_in.rearrange("w p c -> p w c"), reads=(), writes=(wm.b,))
                    es_ = ph.sb("es", [65, 8], F32)
                    cx.dma("sp", es_[64:65, :], sink_in[l:l + 1, :], reads=(), writes=(es_.b,))
                    cx.op("act", lambda e: e.activation(out=es_[64:65, :], in_=es_[64:65, :], func=AF.Exp), reads=(es_.b,), writes=(es_.b,))
                    ones1 = ph.sb("ones1", [65, 64], F32)
                    cx.op("dve", lambda e: e.memset(ones1[:], 1.0), writes=(ones1.b,))
                    onehot = ph.sb("onehot", [65, 65], BF16)
                    cx.op("dve", lambda e: e.memset(onehot[:], 0.0), writes=(onehot.b,))
                    cx.op("dve", lambda e: e.memset(onehot[64:65, 64:65], 1.0), reads=(onehot.b,), writes=(onehot.b,))
                    esrow = ph.sb("esrow", [65, 2, 512], BF16)
                    for kvh in range(2):
                        cx.op("dve", lambda e, kvh=kvh: e.tensor_copy(out=esrow[64:65, kvh, :].rearrange("p (g q) -> p g q", g=4),
                                                                      in_=es_[64:65, kvh * 4:(kvh + 1) * 4].unsqueeze(2).to_broadcast([1, 4, 128])),
                              reads=(es_.b,), writes=(esrow.b,))
                    slopeT = ph.sb("slopeT", [128, 2, 4, 128], F32)
                    for kvh in range(2):
                        for g in range(4):
                            cx.op("dve", lambda e, kvh=kvh, g=g: e.memset(slopeT[:, kvh, g, :], -SLOPES[kvh * 4 + g]), writes=(slopeT.b,))
                    Ds = Rot([ph.sb(f"Dm{i}", [128, 128], F32) for i in range(9)])
                    biases = Rot([ph.sb(f"bias{i}", [128, 512], F32) for i in range(12)])
                    tmps = Rot([ph.sb(f"tmp{i}", [128, 512], F32) for i in range(3)])
                    PTs = Rot([ph.sb(f"PT{i}", [128, 512], BF16) for i in range(4)])
                    osbs = Rot([ph.sb(f"osb{i}", [65, 512], F32) for i in range(3)])
                    ost = Rot([ph.sb(f"ost{i}", [64, 512], BF16) for i in range(2)])
                    pS = Rot([ph.ps(f"pS{i}", [128, 512]) for i in range(4)])
                    pO = Rot([ph.ps(f"pO{i}", [128, 512]) for i in range(2)])
                    pX = Rot([ph.ps(f"pX{i}", [128, 512]) for i in range(2)])
                    ocv = ocT_d.t.rearrange("(h d) t -> d h t", d=64)
                    LOOK = 2
                    steps = [(j, kvh, kk) for j in range(NT) for kvh in range(2) for kk in range(3)]
                    blk_bias = {}
                    cur_po = {}
                    pend = []
                    inflight = []

                    def block_prep(j):
                        bl = {}
                        for kk in range(3):
                            tt = j + kk
                            Dm = Ds.next()
                            cx.op("act", lambda e, Dm=Dm, tt=tt: e.activation(out=Dm[:], in_=posq[:, j * 128:(j + 1) * 128], func=AF.Abs,
                                                                              bias=posk[:, tt:tt + 1], scale=1.0),
                                  reads=(posq.b, posk.b), writes=(Dm.b,))
                            mi = None
                            if kk == 0:
                                mi = 2 if j == 0 else 0
                            elif kk == 2:
                                mi = 3 if j == NT - 1 else 1
                            if mi is not None:
                                cx.op("dve", lambda e, Dm=Dm, mi=mi: e.tensor_tensor(out=Dm[:], in0=Dm[:], in1=wm[:, mi, :], op=ALU.add),
                                      reads=(Dm.b, wm.b), writes=(Dm.b,))
                            bl[kk] = Dm
                        blk_bias[j] = bl

                    def s_stage(j, kvh, kk):
                        tt = j + kk
                        ps_ = pS.next()
                        for g in range(4):
                            cx.op("pe", lambda e, g=g: e.matmul(
                                out=ps_[:, g * 128:(g + 1) * 128], lhsT=kcT[:, kvh, tt * 128:(tt + 1) * 128],
                                rhs=qcT[:, g, j * 128:(j + 1) * 128], start=True, stop=True),
                                inc=(g == 3), reads=(kcT.b, qcT.b), writes=(ps_.b,))
                        tmp = tmps.next()
                        Dm = blk_bias[j][kk]
                        for g in range(4):
                            cx.op("dve", lambda e, g=g: e.scalar_tensor_tensor(
                                out=tmp[:, g * 128:(g + 1) * 128], in0=Dm[:], scalar=-SLOPES[kvh * 4 + g],
                                in1=ps_[:, g * 128:(g + 1) * 128], op0=ALU.mult, op1=ALU.add),
                                reads=(Dm.b, ps_.b), writes=(tmp.b,))
                        pt = PTs.next()
                        cx.op("act", lambda e: e.activation(out=pt[:], in_=tmp[:], func=AF.Exp), reads=(tmp.b,), writes=(pt.b,))
                        return pt

                    def pv_stage(j, kvh, kk, pt, step):
                        tt = j + kk
                        if kk == 0:
                            cur_po[(j, kvh)] = pO.next()
                        po = cur_po[(j, kvh)]
                        if kk == 0:
                            cx.op("pe", lambda e: e.matmul(out=po[0:65, :], lhsT=onehot[64:65, :], rhs=esrow[64:65, kvh, :], start=True, stop=False),
                                  inc=False, reads=(onehot.b, esrow.b), writes=(po.b,))
                        cx.op("pe", lambda e: e.matmul(out=po[0:65, :], lhsT=vc[:, tt, kvh, :], rhs=pt[:], start=False, stop=(kk == 2)),
                              reads=(vc.b, pt.b), writes=(po.b,))
                        if kk == 2:
                            osb = osbs.next()

                            def fin_a(osb=osb, po=po):
                                cx.op("act", lambda e: e.copy(out=osb[:], in_=po[0:65, :]), reads=(po.b,), writes=(osb.b,))
                                cx.op("act", lambda e: e.activation(out=osb[64:65, :], in_=osb[64:65, :], func=AF.Ln), reads=(osb.b,), writes=(osb.b,))
                                cx.op("act", lambda e: e.activation(out=osb[64:65, :], in_=osb[64:65, :], func=AF.Exp, scale=-1.0), reads=(osb.b,), writes=(osb.b,))

                            st8 = {}

                            def fin_b(osb=osb):
                                bc = pX.next()
                                st8["bc"] = bc
                                cx.op("pe", lambda e: e.matmul(out=bc[0:64, :], lhsT=ones1[64:65, :], rhs=osb[64:65, :], start=True, stop=True),
                                      reads=(ones1.b, osb.b), writes=(bc.b,))

                            def fin_c(osb=osb):
                                bc = st8["bc"]
                                o = ost.next()
                                cx.op("dve", lambda e: e.tensor_tensor(out=o[:], in0=osb[0:64, :], in1=bc[0:64, :], op=ALU.mult),
                                      reads=(osb.b, bc.b), writes=(o.b,))
                                cx.dma("pool", ocv[:, kvh * 4:(kvh + 1) * 4, j * 128:(j + 1) * 128], o[:].rearrange("p (g t) -> p g t", g=4),
                                       reads=(o.b,), writes=(ocT_d.b,))
                            pend.append((step + 2, fin_a))
                            pend.append((step + 3, fin_b))
                            pend.append((step + 4, fin_c))

                    block_prep(0)
                    n_st = len(steps)
                    for step in range(n_st + LOOK + 6):
                        if step < n_st:
                            j, kvh, kk = steps[step]
                            if kvh == 0 and kk == 0 and j + 1 < NT:
                                block_prep(j + 1)
                            inflight.append((steps[step], s_stage(j, kvh, kk)))
                        if step >= LOOK and inflight and step - LOOK < n_st:
                            (j, kvh, kk), pt = inflight.pop(0)
                            pv_stage(j, kvh, kk, pt, step)
                        pend.sort(key=lambda x: x[0])
                        while pend and pend[0][0] <= step:
                            pend.pop(0)[1]()
                    assert not pend and not inflight

            if want("P6"):
                with Phase(cx) as ph:
                    stage = mk_stage(ph, 6)
                    Wg = ph.sb("Wg", [128, 8, 3 * D], BF16)
                    load_w_kc(ph, Wg, w_gates[l], 8, 3 * D, stage)
                    Wa = ph.sb("Wa", [128, 4, D], BF16)
                    load_w_kc(ph, Wa, w_a[l], 4, D, stage)
                    Wc = ph.sb("Wc", [128, 4, D], BF16)
                    load_w_kc(ph, Wc, w_c[l], 4, D, stage)
                    Wo = ph.sb("Wo", [128, 8, D], BF16)
                    load_w_kc(ph, Wo, w_out[l], 8, D, stage)
                    ident = ph.sb("ident", [128, 128], BF16)
                    cx.dma("sp", ident[:], ident_in, reads=(), writes=(ident.b,))
                    gpost = gain_bc(ph, "gpost", g_mix_post[l:l + 1, :])
                    hTt = [ph.sb(f"hTt{i}", [128, 8, 128], BF16) for i in range(2)]
                    oat = [ph.sb(f"oat{i}", [128, 4, 128], BF16) for i in range(2)]
                    oct_ = [ph.sb(f"oct{i}", [128, 4, 128], BF16) for i in range(2)]
                    brs = [ph.sb(f"brs{i}", [128, D], F32) for i in range(2)]
                    xts = [ph.sb(f"xt{i}", [128, D], F32) for i in range(4)]
                    sg = [ph.sb(f"sg{i}", [128, D], F32) for i in range(3)]
                    mg = ph.sb("mg", [128, D], F32)
                    tm = ph.sb("tm", [128, D], F32)
                    mbs = [ph.sb(f"mb{i}", [128, D], BF16) for i in range(2)]
                    mTs = [ph.sb(f"mT{i}", [128, 8, 128], BF16) for i in range(2)]
                    junk = ph.sb("junk", [128, D], F32)
                    sss = [ph.sb(f"ss{i}", [128, 1], F32) for i in range(2)]
                    xos = [ph.sb(f"xo{i}", [128, D], F32) for i in range(2)]
                    pG = Rot([ph.ps(f"pG{i}", [128, 1024]) for i in range(3)])
                    pT = ph.ps("pT", [128, 8, 128], BF16)
                    hv = hT_d.t.rearrange("(kc p) t -> p kc t", p=128)
                    oav = oaT_d.t.rearrange("(kc p) t -> p kc t", p=128)
                    ocv2 = ocT_d.t.rearrange("(kc p) t -> p kc t", p=128)

                    def st_a(t):
                        sl = slice(t * 128, (t + 1) * 128)
                        hT, oa, oc, br, xt = hTt[t % 2], oat[t % 2], oct_[t % 2], brs[t % 2], xts[t % 4]
                        cx.dma("sp", hT[:], hv[:, :, sl], reads=(hT_d.b,), writes=(hT.b,))
                        cx.dma("sp", oa[:], oav[:, :, sl], reads=(oaT_d.b,), writes=(oa.b,))
                        cx.dma("sp", oc[:], ocv2[:, :, sl], reads=(ocT_d.b,), writes=(oc.b,))
                        cx.dma("sp", br[:], Brs_d[sl, :], reads=(Brs_d.b,), writes=(br.b,))
                        cx.dma("sp", xt[:], x_src[sl, :], reads=(x_src_b,), writes=(xt.b,))

                    def st_b(t):
                        hT, oa, oc, br, mb = hTt[t % 2], oat[t % 2], oct_[t % 2], brs[t % 2], mbs[t % 2]
                        for gi in range(3):
                            pg = pG.next()
                            for n in range(2):
                                for kc in range(8):
                                    cx.op("pe", lambda e, pg=pg, kc=kc, n=n, gi=gi: e.matmul(
                                        out=pg[:, n * 512:(n + 1) * 512], lhsT=hT[:, kc, :], rhs=Wg[:, kc, gi * D + n * 512:gi * D + (n + 1) * 512],
                                        start=(kc == 0), stop=(kc == 7)), inc=(kc == 7), reads=(hT.b, Wg.b), writes=(pg.b,))
                            cx.op("act", lambda e, pg=pg, gi=gi: e.activation(out=sg[gi][:], in_=pg[:], func=AF.Sigmoid),
                                  reads=(pg.b,), writes=(sg[gi].b,))
                        pa = pG.next()
                        for n in range(2):
                            for k4 in range(4):
                                cx.op("pe", lambda e, k4=k4, n=n: e.matmul(out=pa[:, n * 512:(n + 1) * 512], lhsT=oa[:, k4, :],
                                                                          rhs=Wa[:, k4, n * 512:(n + 1) * 512], start=(k4 == 0), stop=(k4 == 3)),
                                      inc=(k4 == 3), reads=(oa.b, Wa.b), writes=(pa.b,))
                        cx.op("dve", lambda e: e.tensor_tensor(out=mg[:], in0=pa[:], in1=sg[0][:], op=ALU.mult),
                              reads=(pa.b, sg[0].b), writes=(mg.b,))
                        pc = pG.next()
                        for n in range(2):
                            for k4 in range(4):
                                cx.op("pe", lambda e, k4=k4, n=n: e.matmul(out=pc[:, n * 512:(n + 1) * 512], lhsT=oc[:, k4, :],
                                                                          rhs=Wc[:, k4, n * 512:(n + 1) * 512], start=(k4 == 0), stop=(k4 == 3)),
                                      inc=(k4 == 3), reads=(oc.b, Wc.b), writes=(pc.b,))
                        cx.op("dve", lambda e: e.tensor_tensor(out=tm[:], in0=pc[:], in1=sg[2][:], op=ALU.mult),
                              reads=(pc.b, sg[2].b), writes=(tm.b,))
                        cx.op("dve", lambda e: e.tensor_tensor(out=mg[:], in0=mg[:], in1=tm[:], op=ALU.add), reads=(mg.b, tm.b), writes=(mg.b,))
                        cx.op("dve", lambda e: e.tensor_tensor(out=br[:], in0=br[:], in1=sg[1][:], op=ALU.mult),
                              reads=(br.b, sg[1].b), writes=(br.b,))
                        cx.op("dve", lambda e: e.tensor_tensor(out=mb[:], in0=mg[:], in1=br[:], op=ALU.add),
                              reads=(mg.b, br.b), writes=(mb.b,))

                    def st_c(t):
                        mb, mT = mbs[t % 2], mTs[t % 2]
                        for kc in range(8):
                            cx.op("pe", lambda e, kc=kc: e.transpose(out=pT[:, kc, :], in_=mb[:, kc * 128:(kc + 1) * 128], identity=ident[:]), inc=(kc == 7), reads=(mb.b, ident.b), writes=(pT.b,))
                        cx.op("act", lambda e: e.copy(out=mT[:], in_=pT[:]), reads=(pT.b,), writes=(mT.b,))

                    def st_d(t):
                        sl = slice(t * 128, (t + 1) * 128)
                        mT, xt, ss, xo_ = mTs[t % 2], xts[t % 4], sss[t % 2], xos[t % 2]
                        py = pG.next()
                        for n in range(2):
                            for kc in range(8):
                                cx.op("pe", lambda e, kc=kc, n=n: e.matmul(out=py[:, n * 512:(n + 1) * 512], lhsT=mT[:, kc, :],
                                                                          rhs=Wo[:, kc, n * 512:(n + 1) * 512], start=(kc == 0), stop=(kc == 7)),
                                      inc=(kc == 7), reads=(mT.b, Wo.b), writes=(py.b,))
                        rms_rstd_tokmajor(ph, py[:], (py.b,), junk, ss, D)
                        cx.op("dve", lambda e: e.scalar_tensor_tensor(out=xo_[:], in0=py[:], scalar=ss[:, 0:1], in1=gpost[:],
                                                                      op0=ALU.mult, op1=ALU.mult),
                              reads=(py.b, ss.b, gpost.b), writes=(xo_.b,))
                        cx.op("dve", lambda e: e.tensor_tensor(out=xo_[:], in0=xo_[:], in1=xt[:], op=ALU.add),
                              reads=(xo_.b, xt.b), writes=(xo_.b,))
                        cx.dma("pool", xb[sl, :], xo_[:], reads=(xo_.b,), writes=(xb.b,))

                    skew(NT, [st_a, st_b, st_c, st_d])

            if want("P7"):
                with Phase(cx) as ph:
                    stage = mk_stage(ph, 6)
                    Wfg = ph.sb("Wfg", [128, 8, FFN], BF16)
                    load_w_kc(ph, Wfg, w_fg[l], 8, FFN, stage)
                    Wfu = ph.sb("Wfu", [128, 8, FFN], BF16)
                    load_w_kc(ph, Wfu, w_fu[l], 8, FFN, stage)
                    ident = ph.sb("ident", [128, 128], BF16)
                    cx.dma("sp", ident[:], ident_in, reads=(), writes=(ident.b,))
                    gpre = gain_bc(ph, "gfpre", g_ffn_pre[l:l + 1, :])
                    xts = [ph.sb(f"xt{i}", [128, D], F32) for i in range(4)]

                    def ld_x(t):
                        if t < NT:
                            cx.dma("sp", xts[t % 4][:], xb[t * 128:(t + 1) * 128, :], reads=(xb.b,), writes=(xts[t % 4].b,))
                    ld_x(0)
                    ld_x(1)
                    junk = ph.sb("junk", [128, D], F32)
                    sss = Rot([ph.sb(f"ss{i}", [128, 1], F32) for i in range(3)])
                    hbs = Rot([ph.sb(f"hb{i}", [128, D], BF16) for i in range(2)])
                    hTs = Rot([ph.sb(f"hT{i}", [128, 8, 512], BF16, nsub=4) for i in range(2)])
                    ast = Rot([ph.sb(f"ast{i}", [128, 512], BF16) for i in range(3)])
                    sil = Rot([ph.sb(f"sil{i}", [128, 512], F32) for i in range(2)])
                    pGU = Rot([ph.ps(f"pGU{i}", [128, 512]) for i in range(6)])
                    pT = ph.ps("pT", [128, 8, 128], BF16)
                    hT_l = hTs.items

                    def tile_front(t):
                        ci, tt = t // 4, t % 4
                        hT = hT_l[ci % 2]
                        xt = xts[t % 4]
                        ss = sss.next(); hb = hbs.next()
                        ld_x(t + 2)
                        rms_rstd_tokmajor(ph, xt[:], (xt.b,), junk, ss, D)
                        cx.op("dve", lambda e: e.scalar_tensor_tensor(
                            out=hb[:], in0=xt[:], scalar=ss[:, 0:1], in1=gpre[:], op0=ALU.mult, op1=ALU.mult),
                            reads=(xt.b, ss.b, gpre.b), writes=(hb.b,))
                        for kc in range(8):
                            cx.op("pe", lambda e, kc=kc: e.transpose(out=pT[:, kc, :], in_=hb[:, kc * 128:(kc + 1) * 128], identity=ident[:]),
                                  inc=(kc == 7), reads=(hb.b, ident.b), writes=(pT.b,))
                        cx.op("act", lambda e: e.copy(out=hT[:, :, tt * 128:(tt + 1) * 128], in_=pT[:]),
                              reads=(pT.b,), writes=(hT.bs[tt],))

                    for tt in range(4):
                        tile_front(tt)
                    for ci in range(NCH):
                        hT = hT_l[ci % 2]
                        for f in range(NF):
                            if ci + 1 < NCH and f in (3, 8, 13, 18):
                                tile_front((ci + 1) * 4 + (f - 3) // 5)
                            pg = pGU.next(); pu = pGU.next()
                            for pp, W in ((pg, Wfg), (pu, Wfu)):
                                for kc in range(8):
                                    cx.op("pe", lambda e, pp=pp, W=W, kc=kc, f=f, hT=hT: e.matmul(
                                        out=pp[:], lhsT=W[:, kc, f * 128:(f + 1) * 128], rhs=hT[:, kc, :], start=(kc == 0), stop=(kc == 7)),
                                        inc=(kc == 7), reads=tuple(hT.bs) + (W.b,), writes=(pp.b,))
                            s_ = sil.next()
                            cx.op("act", lambda e, s_=s_, pg=pg: e.activation(out=s_[:], in_=pg[:], func=AF.Silu), reads=(pg.b,), writes=(s_.b,))
                            a_ = ast.next()
                            cx.op("dve", lambda e, s_=s_, pu=pu, a_=a_: e.tensor_tensor(out=a_[:], in0=pu[:], in1=s_[:], op=ALU.mult),
                                  reads=(pu.b, s_.b), writes=(a_.b,))
                            cx.dma("pool", act_d[f * 128:(f + 1) * 128, ci * 512:(ci + 1) * 512], a_[:], reads=(a_.b,), writes=(act_d.b,))
                with Phase(cx) as ph:
                    stage = mk_stage(ph, 6)
                    Wfd = ph.sb("Wfd", [128, NF, D], BF16)
                    load_w_kc(ph, Wfd, w_fd[l], NF, D, stage)
                    gpost = gain_bc(ph, "gfpost", g_ffn_post[l:l + 1, :])
                    aTs = Rot([ph.sb(f"aT{i}", [128, NF, 512], BF16) for i in range(2)])
                    xts = Rot([ph.sb(f"xt{i}", [128, D], F32) for i in range(3)])
                    junk = ph.sb("junk", [128, D], F32)
                    sss = Rot([ph.sb(f"ss{i}", [128, 1], F32) for i in range(3)])
                    xo = Rot([ph.sb(f"xo{i}", [128, D], F32) for i in range(2)])
                    pD = Rot([ph.ps(f"pD{i}", [128, 1024]) for i in range(3)])
                    av = act_d.t.rearrange("(f p) t -> p f t", p=128)
                    for ci in range(NCH):
                        aT = aTs.next()
                        cx.dma("sp", aT[:], av[:, :, ci * 512:(ci + 1) * 512], reads=(act_d.b,), writes=(aT.b,))
                        for tt in range(4):
                            t = ci * 4 + tt
                            xt = xts.next()
                            cx.dma("sp", xt[:], xb[t * 128:(t + 1) * 128, :], reads=(xb.b,), writes=(xt.b,))
                            pd = pD.next()
                            for n in range(2):
                                for f in range(NF):
                                    cx.op("pe", lambda e, pd=pd, f=f, n=n, tt=tt, aT=aT: e.matmul(
                                        out=pd[:, n * 512:(n + 1) * 512], lhsT=aT[:, f, tt * 128:(tt + 1) * 128], rhs=Wfd[:, f, n * 512:(n + 1) * 512],
                                        start=(f == 0), stop=(f == NF - 1)), inc=(f == NF - 1), reads=(aT.b, Wfd.b), writes=(pd.b,))
                            ss = sss.next()
                            rms_rstd_tokmajor(ph, pd[:], (pd.b,), junk, ss, D)
                            xo_ = xo.next()
                            cx.op("dve", lambda e, pd=pd, ss=ss, xo_=xo_: e.scalar_tensor_tensor(out=xo_[:], in0=pd[:], scalar=ss[:, 0:1], in1=gpost[:],
                                                                                                op0=ALU.mult, op1=ALU.mult),
                                  reads=(pd.b, ss.b, gpost.b), writes=(xo_.b,))
                            cx.op("dve", lambda e, xo_=xo_, xt=xt: e.tensor_tensor(out=xo_[:], in0=xo_[:], in1=xt[:], op=ALU.add),
                                  reads=(xo_.b, xt.b), writes=(xo_.b,))
                            cx.dma("pool", xa[t * 128:(t + 1) * 128, :], xo_[:], reads=(xo_.b,), writes=(xa.b,))

            if want("P8"):
                last = (l == nlayers - 1)
                with Phase(cx) as ph:
                    stage = mk_stage(ph, 6)
                    Wpg = ph.sb("Wpg", [128, 8, D], BF16)
                    load_w_kc(ph, Wpg, w_pg[l], 8, D, stage)
                    Wpp = ph.sb("Wpp", [128, 2, D], BF16)
                    load_w_kc(ph, Wpp, w_pp[l], 2, D, stage)
                    ident = ph.sb("ident", [128, 128], BF16)
                    cx.dma("sp", ident[:], ident_in, reads=(), writes=(ident.b,))
                    gple = gain_bc(ph, "gple", g_ple[l:l + 1, :])
                    xts = [ph.sb(f"xt{i}", [128, D], F32) for i in range(7)]
                    pts = [ph.sb(f"pt{i}", [128, 256], F32) for i in range(4)]
                    xbf = [ph.sb(f"xbf{i}", [128, D + 256], BF16) for i in range(2)]
                    xTs = [ph.sb(f"xT{i}", [128, 10, 128], BF16) for i in range(2)]
                    sgts = [ph.sb(f"sgt{i}", [128, D], F32) for i in range(2)]
                    ets = [ph.sb(f"et{i}", [128, D], F32) for i in range(2)]
                    junk = ph.sb("junk", [128, D], F32)
                    sss = [ph.sb(f"ss{i}", [128, 1], F32) for i in range(2)]
                    xos = [ph.sb(f"xo{i}", [128, D], F32) for i in range(2)]
                    pGs = [ph.ps(f"pG{i}", [128, 1024]) for i in range(2)]
                    pTs = [ph.ps(f"pT{i}", [128, 16, 128], BF16) for i in range(2)]

                    def st_ld(t):
                        sl = slice(t * 128, (t + 1) * 128)
                        xt, pt = xts[t % 7], pts[t % 4]
                        cx.dma("sp", xt[:], xa[sl, :], reads=(xa.b,), writes=(xt.b,))
                        cx.dma("sp", pt[:], p_in[l, sl, :], reads=(), writes=(pt.b,))

                    def st_a(t):
                        xt, pt, xb_ = xts[t % 7], pts[t % 4], xbf[t % 2]
                        cx.op("dve", lambda e: e.tensor_copy(out=xb_[:, 0:D], in_=xt[:]), reads=(xt.b,), writes=(xb_.b,))
                        cx.op("act", lambda e: e.copy(out=xb_[:, D:D + 256], in_=pt[:]), reads=(pt.b,), writes=(xb_.b,))

                    def st_b(t):
                        xb_, xT_, pT = xbf[t % 2], xTs[t % 2], pTs[t % 2]
                        for kc in range(10):
                            cx.op("pe", lambda e, kc=kc: e.transpose(out=pT[:, kc, :], in_=xb_[:, kc * 128:(kc + 1) * 128], identity=ident[:]), inc=(kc == 9), reads=(xb_.b, ident.b), writes=(pT.b,))
                        cx.op("act", lambda e: e.copy(out=xT_[:], in_=pT[:, 0:10, :]), reads=(pT.b,), writes=(xT_.b,))

                    def st_c(t):
                        xT_, sgt, et = xTs[t % 2], sgts[t % 2], ets[t % 2]
                        pg = pGs[0]
                        for n in range(2):
                            for kc in range(8):
                                cx.op("pe", lambda e, kc=kc, n=n: e.matmul(out=pg[:, n * 512:(n + 1) * 512], lhsT=xT_[:, kc, :],
                                                                          rhs=Wpg[:, kc, n * 512:(n + 1) * 512], start=(kc == 0), stop=(kc == 7)),
                                      inc=(kc == 7), reads=(xT_.b, Wpg.b), writes=(pg.b,))
                        cx.op("act", lambda e: e.activation(out=sgt[:], in_=pg[:], func=AF.Sigmoid), reads=(pg.b,), writes=(sgt.b,))
                        pe_ = pGs[1]
                        for n in range(2):
                            for kc in range(2):
                                cx.op("pe", lambda e, kc=kc, n=n: e.matmul(out=pe_[:, n * 512:(n + 1) * 512], lhsT=xT_[:, 8 + kc, :],
                                                                          rhs=Wpp[:, kc, n * 512:(n + 1) * 512], start=(kc == 0), stop=(kc == 1)),
                                      inc=(kc == 1), reads=(xT_.b, Wpp.b), writes=(pe_.b,))
                        cx.op("dve", lambda e: e.tensor_tensor(out=et[:], in0=pe_[:], in1=sgt[:], op=ALU.mult),
                              reads=(pe_.b, sgt.b), writes=(et.b,))

                    def st_d(t):
                        sl = slice(t * 128, (t + 1) * 128)
                        xt, et, ss, xo_ = xts[t % 7], ets[t % 2], sss[t % 2], xos[t % 2]
                        rms_rstd_tokmajor(ph, et[:], (et.b,), junk, ss, D)
                        cx.op("dve", lambda e: e.scalar_tensor_tensor(out=xo_[:], in0=et[:], scalar=ss[:, 0:1], in1=gple[:],
                                                                      op0=ALU.mult, op1=ALU.mult),
                              reads=(et.b, ss.b, gple.b), writes=(xo_.b,))
                        cx.op("dve", lambda e: e.tensor_tensor(out=xo_[:], in0=xo_[:], in1=xt[:], op=ALU.add),
                              reads=(xo_.b, xt.b), writes=(xo_.b,))
                        if last:
                            cx.dma("pool", y_out[sl, :], xo_[:], reads=(xo_.b,), writes=(ybuf,))
                        else:
                            cx.dma("pool", xc[sl, :], xo_[:], reads=(xo_.b,), writes=(xc.b,))

                    skew(NT, [st_ld, st_a, st_b, st_c, st_d], [0, 2, 3, 4, 5])
        cx.barrier()
    return nc


def _consts(half):
    bf = ml_dtypes.bfloat16
    ident = np.eye(128, dtype=np.float32).astype(bf)
    invf = (np.float32(10000.0) ** (-np.arange(16, dtype=np.float32) / np.float32(16))).astype(np.float32)
    invf = np.concatenate([invf, invf]).reshape(32, 1)
    c = np.arange(256)
    th = 2.0 * np.pi * np.outer(c, c) / 256.0
    cdft = np.stack([np.cos(th) / 16.0, -np.sin(th) / 16.0]).astype(np.float32).astype(bf)
    s1 = np.arange(128)[None, :, None]
    k1 = np.arange(256)[None, None, :]
    s2 = np.arange(32)[:, None, None]
    ph = (4096 * half * k1 + 32 * s1 * k1 + s2 * k1) % 8192
    ang = 2.0 * np.pi * ph / 8192.0
    nrm = 1.0 / np.sqrt(8192.0)
    G = np.stack([np.cos(ang) * nrm, -np.sin(ang) * nrm, np.sin(ang) * nrm], axis=1).astype(np.float32).astype(bf)
    a2 = 2.0 * np.pi * np.outer(np.arange(32), np.arange(32)) / 32.0
    Er = np.kron(np.eye(4), np.cos(a2))
    Es = np.kron(np.eye(4), np.sin(a2))
    E = np.stack([Er, Es]).astype(np.float32).astype(bf)
    BIG = 1.0e7
    k = np.arange(128)[:, None]
    q = np.arange(128)[None, :]
    m0 = np.where(k >= q, 0.0, BIG)
    m2 = np.where(k <= q, 0.0, BIG)
    e0 = np.full((128, 128), BIG) if half == 0 else m0
    e3 = np.full((128, 128), BIG) if half == 1 else m2
    wmask = np.stack([m0, m2, e0, e3]).astype(np.float32)
    return dict(ident=ident, invf=invf, cdft=cdft, Gmat=G, Emat=E, wmask=wmask)


def _prep_weights(inp):
    w_in = np.asarray(inp["w_in"])
    cq = w_in[:, :, 0:384]
    ckv = w_in[:, :, 384:512]
    kr = w_in[:, :, 512:544]
    krs = np.concatenate([kr[:, :, 16:32], kr[:, :, 0:16]], axis=2)
    qc = w_in[:, :, 544:1056]
    idx = np.concatenate([np.concatenate([np.arange((0 * 4 + g) * 64, (0 * 4 + g) * 64 + 64),
                                          np.arange((1 * 4 + g) * 64, (1 * 4 + g) * 64 + 64)]) for g in range(4)])
    qcp = qc[:, :, idx]
    kc = w_in[:, :, 1056:1184]
    vc = w_in[:, :, 1184:1312]
    pad = np.zeros(kr.shape[:2] + (64,), dtype=kr.dtype)
    w_small = np.ascontiguousarray(np.concatenate([cq, ckv, pad, kr, pad, krs, qcp, kc, vc], axis=2))
    w_gates = np.ascontiguousarray(w_in[:, :, 1312:])
    w_uq = np.asarray(inp["w_uq"])
    ir, isw = [], []
    for h in range(8):
        b = h * 96
        ir += list(range(b, b + 96))
        isw += list(range(b, b + 64)) + list(range(b + 80, b + 96)) + list(range(b + 64, b + 80))
    return dict(w_small=w_small, w_gates=w_gates, w_uqr=np.ascontiguousarray(w_uq[:, :, ir]),
                w_uqs=np.ascontiguousarray(w_uq[:, :, isw]), w_ukv=np.asarray(inp["w_ukv"]),
                w_a=np.asarray(inp["w_branch_a"]), w_b=np.asarray(inp["w_branch_b"]), w_c=np.asarray(inp["w_branch_c"]),
                w_out=np.asarray(inp["w_out"]), w_fg=np.asarray(inp["w_ffn_gate"]), w_fu=np.asarray(inp["w_ffn_up"]),
                w_fd=np.asarray(inp["w_ffn_down"]), w_pp=np.asarray(inp["w_ple_proj"]), w_pg=np.asarray(inp["w_ple_gate"]),
                g_mix_pre=np.asarray(inp["norm_mix_pre"]), g_q=np.asarray(inp["mla_q_norm"]), g_kv=np.asarray(inp["mla_kv_norm"]),
                g_mix_post=np.asarray(inp["norm_mix_post"]), g_ffn_pre=np.asarray(inp["norm_ffn_pre"]),
                g_ffn_post=np.asarray(inp["norm_ffn_post"]), g_ple=np.asarray(inp["norm_ple"]), sink=np.asarray(inp["gqa_sink"]))


def make_in_maps(inp):
    shared = _prep_weights(inp)
    shared = {k: np.ascontiguousarray(v, dtype=np.float32) for k, v in shared.items()}
    x = np.asarray(inp["x"]); p = np.asarray(inp["p"]); pos = np.asarray(inp["positions"])
    maps = []
    for c in range(8):
        b, half = c // 2, c % 2
        s0 = half * TOK
        m = dict(shared)
        m.update(_consts(half))
        m["x"] = np.ascontiguousarray(x[b, s0:s0 + TOK, :], dtype=np.float32)
        m["p"] = np.ascontiguousarray(p[:, b, s0:s0 + TOK, :], dtype=np.float32)
        m["pos"] = np.ascontiguousarray(pos[b, s0:s0 + TOK].reshape(1, TOK), dtype=np.int32)
        px = np.zeros(TOK + 256, dtype=np.int32)
        lo, hi = s0 - 128, s0 + TOK + 128
        slo, shi = max(lo, 0), min(hi, SEQ)
        px[slo - lo:shi - lo] = pos[b, slo:shi]
        m["posx"] = np.ascontiguousarray(px.reshape(34, 128))
        maps.append(m)
    return maps


_NC_CACHE = {}


def kernel(**inputs):
    if "nc" not in _NC_CACHE:
        _NC_CACHE["nc"] = build_program()
    nc = _NC_CACHE["nc"]
    maps = make_in_maps(inputs)
    res = run_bass_kernel_spmd(nc, maps, core_ids=list(range(8)))
    out = np.empty((NB, SEQ, D), dtype=np.float32)
    for c in range(8):
        b, half = c // 2, c % 2
        out[b, half * TOK:(half + 1) * TOK, :] = res.results[c]["y"]
    return out
```

```python
import numpy as np
import ml_dtypes
from contextlib import ExitStack
import concourse.bass as bass
import concourse.mybir as mybir
from concourse.bass_utils import run_bass_kernel_spmd

F32 = mybir.dt.float32
BF16 = mybir.dt.bfloat16
I32 = mybir.dt.int32
ALU = mybir.AluOpType
AF = mybir.ActivationFunctionType

D = 1024
SEQ = 8192
NB = 4
DEPTH = 2
TOK = 4096
NT = TOK // 128
NCH = TOK // 512
FFN = 2816
NF = FFN // 128
EPS = 1e-6
HQ = 8
SLOPES = [2.0 ** (-8.0 * (h + 1.0) / 8.0) for h in range(8)]
MLA_SCALE = 96 ** -0.5
NSMALL = 1472
O_CQ, O_CKV, O_KRA, O_KRB, O_QC, O_KC, O_VC = 0, 384, 512, 608, 704, 1216, 1344
PAIRS = [[0, 1], [2, 3], [4, 5], [6, 7]]
DBG = {}


def dbg(k):
    return k not in DBG.get('skip', ())

XROWS = 176


class Buf:
    __slots__ = ("name", "w", "r", "dsem", "px")

    def __init__(self, name):
        self.name = name
        self.w = None
        self.r = {}
        self.dsem = None
        self.px = False


class DSem:
    __slots__ = ("sem", "count", "key", "lazy")

    def __init__(self, sem, key):
        self.sem = sem
        self.count = 0
        self.key = key
        self.lazy = False


class Tl:
    __slots__ = ("t", "b", "bs")

    def __init__(self, t, name, nsub=0):
        self.t = t
        self.b = Buf(name)
        self.bs = [Buf(f"{name}.{i}") for i in range(nsub)]

    def __getitem__(self, k):
        return self.t[k]


class Ctx:
    def __init__(self, nc, es):
        self.nc = nc
        self.es = es
        self.eng = {"pe": nc.tensor, "act": nc.scalar, "dve": nc.vector, "pool": nc.gpsimd, "sp": nc.sync}
        self.psem = {}
        for k in self.eng:
            self.psem[k] = es.enter_context(nc.semaphore("p_" + k))
        self.cnt = {k: 0 for k in self.eng}
        self.known = {k: {} for k in self.eng}
        self.pending = {k: False for k in self.eng}
        self.dsems = []
        self.free_dsems = []
        self.n_ins = 0

    def new_dsem(self):
        if self.free_dsems:
            return self.free_dsems.pop()
        s = self.es.enter_context(self.nc.semaphore(f"d{len(self.dsems)}"))
        d = DSem(s, f"d{len(self.dsems)}")
        self.dsems.append(d)
        return d

    def release_dsem(self, d):
        self.free_dsems.append(d)

    def _wait(self, e, ev):
        key, sem, val = ev
        if self.known[e].get(key, 0) >= val:
            return
        self.eng[e].wait_ge(sem, val)
        self.known[e][key] = val

    def _deps(self, e, reads, writes, skip_key=None):
        for b in reads:
            if b.w is not None and b.w[0] != skip_key:
                self._wait(e, b.w)
            if b.px:
                for ev in b.r.values():
                    if ev[0] != e:
                        self._wait(e, ev)
        for b in writes:
            if b.w is not None and b.w[0] != skip_key:
                self._wait(e, b.w)
            for ev in b.r.values():
                if ev[0] != skip_key:
                    self._wait(e, ev)

    def op(self, e, fn, reads=(), writes=(), inc=True):
        skip = "pe" if e == "pe" else None
        self._deps(e, reads, writes, skip)
        ins = fn(self.eng[e])
        if inc:
            self.cnt[e] += 1
            ins.then_inc(self.psem[e], 1)
            ev = (e, self.psem[e], self.cnt[e])
            self.pending[e] = False
        else:
            ev = (e, self.psem[e], self.cnt[e] + 1)
            self.pending[e] = True
        for b in reads:
            b.r[e] = ev
        for b in writes:
            b.w = ev
            b.r = {}
        self.n_ins += 1
        return ins

    def dma(self, q, out, in_, reads=(), writes=(), **kw):
        b0 = writes[0]
        if b0.dsem is None:
            b0.dsem = self.new_dsem()
        ds = b0.dsem
        self._deps(q, reads, writes, ds.key)
        ins = self.eng[q].dma_start(out=out, in_=in_, **kw)
        ds.count += 16
        ins.then_inc(ds.sem, 16)
        ev = (ds.key, ds.sem, ds.count)
        for b in reads:
            b.r[ds.key] = ev
        for b in writes:
            b.w = ev
            b.r = {}
        self.n_ins += 1
        return ins

    def cc(self, kind, op, in_ap, out_ap, reads, writes):
        b0 = writes[0]
        if b0.dsem is None:
            b0.dsem = self.new_dsem()
        ds = b0.dsem
        ds.lazy = True
        self._deps("pool", reads, writes, None)
        ins = self.nc.gpsimd.collective_compute(kind, op, replica_groups=PAIRS, ins=[in_ap], outs=[out_ap])
        ds.count += 1
        ins.then_inc(ds.sem, 1)
        ev = (ds.key, ds.sem, ds.count)
        for b in reads:
            b.r[ds.key] = ev
        for b in writes:
            b.w = ev
            b.r = {}
        return ins

    def barrier(self):
        assert not any(self.pending.values()), self.pending
        for e in self.eng:
            for e2 in self.eng:
                if e2 != e and self.cnt[e2] > 0:
                    self._wait(e, (e2, self.psem[e2], self.cnt[e2]))
            for d in self.dsems:
                if d.count > 0 and not d.lazy:
                    self._wait(e, (d.key, d.sem, d.count))


class Phase:
    uid = 0

    def __init__(self, cx):
        self.cx = cx
        self.es = ExitStack()
        self.tiles = []

    def __enter__(self):
        self.es.__enter__()
        return self

    def __exit__(self, *a):
        self.cx.barrier()
        for t in self.tiles:
            for b in [t.b] + t.bs:
                if b.dsem is not None:
                    self.cx.release_dsem(b.dsem)
                    b.dsem = None
        return self.es.__exit__(*a)

    def sb(self, name, shape, dtype, nsub=0):
        Phase.uid += 1
        name = f"{name}_u{Phase.uid}"
        t = self.es.enter_context(self.cx.nc.sbuf_tensor(name, list(shape), dtype))
        tl = Tl(t, name, nsub)
        self.tiles.append(tl)
        return tl

    def ps(self, name, shape, dtype=F32):
        Phase.uid += 1
        name = f"{name}_u{Phase.uid}"
        t = self.es.enter_context(self.cx.nc.psum_tensor(name, list(shape), dtype))
        tl = Tl(t, name)
        tl.b.px = True
        self.tiles.append(tl)
        return tl


def skew(n, stages, lags=None):
    if lags is None:
        lags = list(range(len(stages)))
    for step in range(n + max(lags)):
        for st, lg in zip(stages, lags):
            t = step - lg
            if 0 <= t < n:
                st(t)


class Rot:
    def __init__(self, items):
        self.items = items
        self.i = 0

    def next(self):
        x = self.items[self.i % len(self.items)]
        self.i += 1
        return x


def build_program(dump=(), phases=None, nlayers=DEPTH):
    nc = bass.Bass("TRN2", target_bir_lowering=False)
    es = ExitStack()
    with es:
        cx = Ctx(nc, es)

        def din(name, shape, dt=F32):
            return nc.dram_tensor(name, list(shape), dt, kind="ExternalInput").ap()

        def dscr(name, shape, dt):
            kind = "ExternalOutput" if name in dump else "Internal"
            return Tl(nc.dram_tensor(name, list(shape), dt, kind=kind).ap(), name)

        x_in = din("x", [TOK, D])
        p_in = din("p", [DEPTH, TOK, 256])
        pos_in = din("pos", [1, TOK], I32)
        posx_in = din("posx", [34, 128], I32)
        w_small = din("w_small", [DEPTH, D, NSMALL])
        w_gates = din("w_gates", [DEPTH, D, 3 * D])
        w_uqr = din("w_uqr", [DEPTH, 384, 768])
        w_uqs = din("w_uqs", [DEPTH, 384, 768])
        w_ukv = din("w_ukv", [DEPTH, 128, 1024])
        w_a = din("w_a", [DEPTH, 512, D])
        w_b = din("w_b", [DEPTH, D, D])
        w_c = din("w_c", [DEPTH, 512, D])
        w_out = din("w_out", [DEPTH, D, D])
        w_fg = din("w_fg", [DEPTH, D, FFN])
        w_fu = din("w_fu", [DEPTH, D, FFN])
        w_fd = din("w_fd", [DEPTH, FFN, D])
        w_pp = din("w_pp", [DEPTH, 256, D])
        w_pg = din("w_pg", [DEPTH, D, D])
        g_mix_pre = din("g_mix_pre", [DEPTH, D])
        g_q = din("g_q", [DEPTH, 384])
        g_kv = din("g_kv", [DEPTH, 128])
        g_mix_post = din("g_mix_post", [DEPTH, D])
        g_ffn_pre = din("g_ffn_pre", [DEPTH, D])
        g_ffn_post = din("g_ffn_post", [DEPTH, D])
        g_ple = din("g_ple", [DEPTH, D])
        sink_in = din("sink", [DEPTH, 8])
        ident_in = din("ident", [128, 128], BF16)
        invf_in = din("invf", [32, 1])
        cdft_in = din("cdft", [2, 256, 256], BF16)
        G_in = din("Gmat", [32, 3, 128, 256], BF16)
        E_in = din("Emat", [2, 128, 128], BF16)
        wmask_in = din("wmask", [4, 128, 128])
        y_out = nc.dram_tensor("y", [TOK, D], F32, kind="ExternalOutput").ap()
        ybuf = Buf("y")

        xa = dscr("xa", [TOK, D], F32)
        xb = dscr("xb", [TOK, D], F32)
        xc = dscr("xc", [TOK, D], F32)
        act_d = dscr("act_d", [FFN, TOK], BF16)
        hT_d = dscr("hT_d", [D, TOK], BF16)
        Yr_d = dscr("Yr_d", [TOK, D], BF16)
        Yi_d = dscr("Yi_d", [TOK, D], BF16)
        Yn_d = dscr("Yn_d", [TOK, D], BF16)
        cqT_d = dscr("cqT_d", [384, TOK], BF16)
        xin_d = dscr("xin_d", [XROWS, TOK], BF16)
        xout_d = dscr("xout_d", [2 * XROWS, TOK], BF16)
        qcT_d = dscr("qcT_d", [128, 4, TOK], BF16)
        kcT_d = dscr("kcT_d", [128, TOK + 256], BF16)
        vc_d = dscr("vc_d", [TOK + 256, 128], BF16)
        oaT_d = dscr("oaT_d", [512, TOK], BF16)
        ocT_d = dscr("ocT_d", [512, TOK], BF16)
        Zr_d = dscr("Zr_d", [SEQ, D], BF16)
        Zi_d = dscr("Zi_d", [SEQ, D], BF16)
        Bp_d = dscr("Bp_d", [SEQ, D], F32)
        Brs_d = dscr("Brs_d", [TOK, D], F32)
        rope_d = dscr("rope_d", [2, 32, TOK], F32)
        xin_buf = Buf("x_in")

        def want(name):
            return phases is None or name in phases

        CAST_ENG = ["dve", "act"]
        cast_i = [0]
        def load_cast(ph, dst_ap_fn, src_ap_fn, nparts, ncols, stage, colchunk=2048):
            for c0 in range(0, ncols, colchunk):
                c1 = min(ncols, c0 + colchunk)
                st = stage.next()
                cx.dma("sp", st[0:nparts, 0:c1 - c0], src_ap_fn(c0, c1), reads=(), writes=(st.b,))
                dst, dbuf = dst_ap_fn(c0, c1)
                ce = CAST_ENG[cast_i[0] % len(CAST_ENG)]
                cast_i[0] += 1
                if ce == "act":
                    cx.op("act", lambda e, dst=dst, st=st, n=c1 - c0: e.copy(out=dst, in_=st[0:nparts, 0:n]),
                          reads=(st.b,), writes=(dbuf,))
                else:
                    cx.op(ce, lambda e, dst=dst, st=st, n=c1 - c0: e.tensor_copy(out=dst, in_=st[0:nparts, 0:n]),
                          reads=(st.b,), writes=(dbuf,))

        def load_w_kc(ph, wt, src, nkc, ncols, stage):
            for kc in range(nkc):
                load_cast(ph, lambda c0, c1, kc=kc: (wt[:, kc, c0:c1], wt.b),
                          lambda c0, c1, kc=kc: src[kc * 128:(kc + 1) * 128, c0:c1], 128, ncols, stage)

        def mk_stage(ph, n=3):
            return Rot([ph.sb(f"wstage{i}", [128, 2048], F32) for i in range(n)])

        def gain_bc(ph, name, src_row):
            n = src_row.shape[-1]
            t = ph.sb(name, [128, n], F32)
            cx.dma("sp", t[:], src_row.partition_broadcast(128), reads=(), writes=(t.b,))
            return t

        def rms_rstd_tokmajor(ph, src_ap, src_bufs, junk, ss, n):
            cx.op("act", lambda e: e.activation(out=junk[:, 0:n], in_=src_ap, func=AF.Square, accum_out=ss[:, 0:1]),
                  reads=src_bufs, writes=(junk.b, ss.b))
            cx.op("dve", lambda e: e.tensor_scalar(out=ss[:, 0:1], in0=ss[:, 0:1], scalar1=1.0 / n, scalar2=EPS,
                                                   op0=ALU.mult, op1=ALU.add), reads=(ss.b,), writes=(ss.b,))
            cx.op("act", lambda e: e.activation(out=ss[:, 0:1], in_=ss[:, 0:1], func=AF.Sqrt), reads=(ss.b,), writes=(ss.b,))
            cx.op("dve", lambda e: e.reciprocal(out=ss[:, 0:1], in_=ss[:, 0:1]), reads=(ss.b,), writes=(ss.b,))

        if want("P0"):
            with Phase(cx) as ph:
                posi = ph.sb("posi", [32, TOK], I32)
                posf = ph.sb("posf", [32, TOK], F32)
                ang = ph.sb("ang", [32, TOK], F32)
                nn = ph.sb("nn", [32, TOK], F32)
                tab = ph.sb("tab", [32, TOK], F32)
                invf = ph.sb("invf", [32, 1], F32)
                cx.dma("sp", posi[:], pos_in.partition_broadcast(32), reads=(), writes=(posi.b,))
                cx.dma("sp", invf[:], invf_in, reads=(), writes=(invf.b,))
                cx.op("dve", lambda e: e.tensor_copy(out=posf[:], in_=posi[:]), reads=(posi.b,), writes=(posf.b,))
                TWO_PI = 2.0 * np.pi
                C1 = 6.28125
                C2 = TWO_PI - C1
                MAGIC = 12582912.0
                for which in range(2):
                    shift = (np.pi / 2.0) if which == 0 else 0.0
                    cx.op("dve", lambda e, shift=shift: e.tensor_scalar(out=ang[:], in0=posf[:], scalar1=invf[:, 0:1], scalar2=shift,
                                                                       op0=ALU.mult, op1=ALU.add), reads=(posf.b, invf.b), writes=(ang.b,))
                    cx.op("dve", lambda e: e.tensor_scalar(out=nn[:], in0=ang[:], scalar1=1.0 / TWO_PI, scalar2=MAGIC,
                                                           op0=ALU.mult, op1=ALU.add), reads=(ang.b,), writes=(nn.b,))
                    cx.op("dve", lambda e: e.tensor_scalar(out=nn[:], in0=nn[:], scalar1=MAGIC, scalar2=None,
                                                           op0=ALU.subtract), reads=(nn.b,), writes=(nn.b,))
                    cx.op("dve", lambda e: e.scalar_tensor_tensor(out=ang[:], in0=nn[:], scalar=-C1, in1=ang[:],
                                                                  op0=ALU.mult, op1=ALU.add), reads=(nn.b, ang.b), writes=(ang.b,))
                    cx.op("dve", lambda e: e.scalar_tensor_tensor(out=ang[:], in0=nn[:], scalar=-C2, in1=ang[:],
                                                                  op0=ALU.mult, op1=ALU.add), reads=(nn.b, ang.b), writes=(ang.b,))
                    cx.op("dve", lambda e: e.tensor_scalar(out=ang[:], in0=ang[:], scalar1=3.1415925, scalar2=-3.1415925,
                                                           op0=ALU.min, op1=ALU.max), reads=(ang.b,), writes=(ang.b,))
                    cx.op("act", lambda e: e.activation(out=tab[:], in_=ang[:], func=AF.Sin), reads=(ang.b,), writes=(tab.b,))
                    if which == 1:
                        cx.op("dve", lambda e: e.tensor_scalar(out=tab[0:16, :], in0=tab[0:16, :], scalar1=-1.0, scalar2=None,
                                                               op0=ALU.mult), reads=(tab.b,), writes=(tab.b,))
                    cx.dma("pool", rope_d[which], tab[:], reads=(tab.b,), writes=(rope_d.b,))

        for l in range(nlayers):
            x_src, x_src_b = (x_in, xin_buf) if l == 0 else (xc.t, xc.b)
            if want("P1"):
                with Phase(cx) as ph:
                    stage = mk_stage(ph)
                    Ws = ph.sb("Ws", [128, 8, NSMALL], BF16)
                    load_w_kc(ph, Ws, w_small[l], 8, NSMALL, stage)
                    Wb = ph.sb("Wb", [128, 8, D], BF16)
                    load_w_kc(ph, Wb, w_b[l], 8, D, stage)
                    cd = ph.sb("cd", [128, 2, 2, 256], BF16)
                    cx.dma("sp", cd[:], cdft_in.rearrange("w (kr p) c -> p w kr c", p=128), reads=(), writes=(cd.b,))
                    Mri = ph.sb("Mri", [128, 2, 8, D], BF16)
                    ident = ph.sb("ident", [128, 128], BF16)
                    cx.dma("sp", ident[:], ident_in, reads=(), writes=(ident.b,))
                    ones = ph.sb("ones", [128, 128], BF16)
                    cx.op("dve", lambda e: e.memset(ones[:], 1.0), writes=(ones.b,))
                    gpre = gain_bc(ph, "gpre", g_mix_pre[l:l + 1, :])
                    gq = ph.sb("gq", [128, 3], F32)
                    cx.dma("sp", gq[:], g_q[l].rearrange("(c p) -> p c", p=128), reads=(), writes=(gq.b,), allow_slow_non_contiguous=True)
                    gkv = ph.sb("gkv", [128, 1], F32)
                    cx.dma("sp", gkv[:], g_kv[l].rearrange("(c p) -> p c", p=128), reads=(), writes=(gkv.b,), allow_slow_non_contiguous=True)
                    ropet = ph.sb("ropet", [96, 2, TOK], F32)
                    cx.dma("sp", ropet[64:96], rope_d.t.rearrange("w p t -> p w t"), reads=(rope_d.b,), writes=(ropet.b,))

                    pF = Rot([ph.ps(f"pF{i}", [128, 512]) for i in range(4)])
                    pY = ph.ps("pY", [128, 1024])
                    pT = ph.ps("pT", [128, 8, 128], BF16)
                    pT2 = ph.ps("pT2", [128, 8, 128], BF16)

                    for which in range(2):
                        for m in range(8):
                            gr, mc = m // 2, m % 2
                            for n in range(2):
                                for kr in range(2):
                                    cx.op("pe", lambda e, which=which, kr=kr, mc=mc, gr=gr, n=n: e.matmul(
                                        out=pY[:, n * 512:(n + 1) * 512], lhsT=cd[:, which, kr, mc * 128:(mc + 1) * 128],
                                        rhs=Wb[:, 2 * gr + kr, n * 512:(n + 1) * 512], start=(kr == 0), stop=(kr == 1)),
                                        inc=(kr == 1), reads=(cd.b, Wb.b), writes=(pY.b,))
                            cx.op("act", lambda e, which=which, m=m: e.copy(out=Mri[:, which, m, :], in_=pY[:]),
                                  reads=(pY.b,), writes=(Mri.b,))

                    xts = Rot([ph.sb(f"xt{i}", [128, D], F32) for i in range(3)])
                    junk = ph.sb("junk", [128, D], F32)
                    sss = Rot([ph.sb(f"ss{i}", [128, 1], F32) for i in range(3)])
                    hbs = Rot([ph.sb(f"hb{i}", [128, D], BF16) for i in range(2)])
                    hTs = Rot([ph.sb(f"hT{i}", [128, 8, 512], BF16, nsub=4) for i in range(2)])
                    ysb = Rot([ph.sb(f"ysb{i}", [128, D], BF16) for i in range(3)])
                    vst = Rot([ph.sb(f"vst{i}", [128, 4, 128], BF16) for i in range(2)])
                    sqs = [ph.sb(f"sq{i}", [128, 512], BF16) for i in range(3)]
                    msb = ph.sb("msb", [128, 512], F32)
                    cqst = Rot([ph.sb(f"cqst{i}", [128, 3, 512], BF16) for i in range(2)])
                    fst = Rot([ph.sb(f"fst{i}", [128, 512], BF16) for i in range(3)])
                    r1 = ph.sb("r1", [96, 512], F32)
                    r2 = ph.sb("r2", [96, 512], F32)
                    krst = Rot([ph.sb(f"krst{i}", [96, 512], BF16) for i in range(2)])

                    pTs = [pT, pT2]
                    hT_l = hTs.items
                    hb_l = hbs.items
                    chunk_pV = {}

                    xt_l = xts.items

                    def st_ld(t):
                        xt = xt_l[t % 3]
                        cx.dma("sp", xt[:], x_src[t * 128:(t + 1) * 128, :], reads=(x_src_b,), writes=(xt.b,))

                    def st_a(t):
                        xt = xt_l[t % 3]
                        ss = sss.next()
                        hb = hb_l[t % 2]
                        rms_rstd_tokmajor(ph, xt[:], (xt.b,), junk, ss, D)
                        cx.op("dve", lambda e: e.scalar_tensor_tensor(
                            out=hb[:], in0=xt[:], scalar=ss[:, 0:1], in1=gpre[:], op0=ALU.mult, op1=ALU.mult),
                            reads=(xt.b, ss.b, gpre.b), writes=(hb.b,))

                    def st_b(t):
                        ci, tt = t // 4, t % 4
                        hb, hT, pT_ = hb_l[t % 2], hT_l[ci % 2], pTs[t % 2]
                        for kc in range(8):
                            cx.op("pe", lambda e, kc=kc: e.transpose(out=pT_[:, kc, :], in_=hb[:, kc * 128:(kc + 1) * 128], identity=ident[:]),
                                  inc=(kc == 7), reads=(hb.b, ident.b), writes=(pT_.b,))
                        cx.op("act", lambda e: e.copy(out=hT[:, :, tt * 128:(tt + 1) * 128], in_=pT_[:]),
                              reads=(pT_.b,), writes=(hT.bs[tt],))

                    def st_c(t):
                        ci, tt = t // 4, t % 4
                        hT = hT_l[ci % 2]
                        if tt == 0:
                            chunk_pV[ci] = pF.next()
                        pV = chunk_pV[ci]
                        for kc in range(8):
                            cx.op("pe", lambda e, kc=kc: e.matmul(
                                out=pV[:, tt * 128:(tt + 1) * 128], lhsT=hT[:, kc, tt * 128:(tt + 1) * 128],
                                rhs=Ws[:, kc, O_VC:O_VC + 128], start=(kc == 0), stop=(kc == 7)),
                                inc=(kc == 7), reads=(hT.bs[tt], Ws.b), writes=(pV.b,))
                        for which in range(2):
                            for n in range(2):
                                for kc in range(8):
                                    cx.op("pe", lambda e, kc=kc, n=n, which=which: e.matmul(
                                        out=pY[:, n * 512:(n + 1) * 512], lhsT=hT[:, kc, tt * 128:(tt + 1) * 128],
                                        rhs=Mri[:, which, kc, n * 512:(n + 1) * 512], start=(kc == 0), stop=(kc == 7)),
                                        inc=(kc == 7), reads=(hT.bs[tt], Mri.b), writes=(pY.b,))
                            if which == 0:
                                ys = ysb.next()
                                cx.op("act", lambda e, ys=ys: e.copy(out=ys[:], in_=pY[:]), reads=(pY.b,), writes=(ys.b,))
                                cx.dma("pool", Yr_d[t * 128:(t + 1) * 128, :], ys[:], reads=(ys.b,), writes=(Yr_d.b,))
                            else:
                                ys = ysb.next()
                                cx.op("dve", lambda e, ys=ys: e.tensor_copy(out=ys[:], in_=pY[:]), reads=(pY.b,), writes=(ys.b,))
                                cx.dma("pool", Yi_d[t * 128:(t + 1) * 128, :], ys[:], reads=(ys.b,), writes=(Yi_d.b,))
                        if tt == 3:
                            chunk_level(ci, hT, pV)

                    def chunk_level(ci, hT, pV):
                        tok0 = ci * 512
                        for _once in (0,):
                            vs = vst.next()
                            cx.op("dve", lambda e, vs=vs, pV=pV: e.tensor_copy(out=vs[:], in_=pV[:].rearrange("p (t c) -> p t c", t=4)),
                                  reads=(pV.b,), writes=(vs.b,))
                            cx.dma("pool", vc_d[128 + tok0:128 + tok0 + 512, :].rearrange("(t p) c -> p t c", p=128), vs[:],
                                   reads=(vs.b,), writes=(vc_d.b,))
                            cx.dma("pool", hT_d.t.rearrange("(kc p) t -> p kc t", p=128)[:, :, tok0:tok0 + 512], hT[:],
                                   reads=tuple(hT.bs), writes=(hT_d.b,))

                            def fm_proj(col0, m, pbank, hT=hT):
                                for kc in range(8):
                                    cx.op("pe", lambda e, kc=kc: e.matmul(out=pbank[0:m, :], lhsT=Ws[:, kc, col0:col0 + m], rhs=hT[:, kc, :],
                                                                           start=(kc == 0), stop=(kc == 7)),
                                          inc=(kc == 7), reads=tuple(hT.bs) + (Ws.b,), writes=(pbank.b,))

                            def fm_norm(banks, nfeat, gcol, dst_fn):
                                pS = pF.next()
                                for c, bk in enumerate(banks):
                                    cx.op("act", lambda e, c=c, bk=bk: e.activation(out=sqs[c][:], in_=bk[:], func=AF.Square),
                                          reads=(bk.b,), writes=(sqs[c].b,))
                                for c in range(len(banks)):
                                    cx.op("pe", lambda e, c=c: e.matmul(out=pS[:], lhsT=ones[:], rhs=sqs[c][:], start=(c == 0),
                                                                        stop=(c == len(banks) - 1)), reads=(ones.b, sqs[c].b), writes=(pS.b,))
                                cx.op("dve", lambda e: e.tensor_scalar(out=msb[:], in0=pS[:], scalar1=1.0 / nfeat, scalar2=EPS,
                                                                       op0=ALU.mult, op1=ALU.add), reads=(pS.b,), writes=(msb.b,))
                                cx.op("act", lambda e: e.activation(out=msb[:], in_=msb[:], func=AF.Sqrt), reads=(msb.b,), writes=(msb.b,))
                                cx.op("dve", lambda e: e.reciprocal(out=msb[:], in_=msb[:]), reads=(msb.b,), writes=(msb.b,))
                                for c, bk in enumerate(banks):
                                    dst, dbuf = dst_fn(c)
                                    cx.op("dve", lambda e, c=c, bk=bk, dst=dst: e.scalar_tensor_tensor(
                                        out=dst, in0=bk[:], scalar=gcol[:, c:c + 1], in1=msb[:], op0=ALU.mult, op1=ALU.mult),
                                        reads=(bk.b, gcol.b, msb.b), writes=(dbuf,))

                            banks = [pF.next() for _ in range(3)]
                            for c in range(3):
                                fm_proj(O_CQ + c * 128, 128, banks[c])
                            cq = cqst.next()
                            fm_norm(banks, 384, gq, lambda c: (cq[:, c, :], cq.b))
                            cx.dma("pool", cqT_d.t.rearrange("(c p) t -> p c t", p=128)[:, :, tok0:tok0 + 512], cq[:],
                                   reads=(cq.b,), writes=(cqT_d.b,))
                            bk = pF.next()
                            fm_proj(O_CKV, 128, bk)
                            f1 = fst.next()
                            fm_norm([bk], 128, gkv, lambda c: (f1[:], f1.b))
                            cx.dma("pool", xin_d[0:128, tok0:tok0 + 512], f1[:], reads=(f1.b,), writes=(xin_d.b,))
                            bA = pF.next()
                            bB = pF.next()
                            fm_proj(O_KRA, 96, bA)
                            fm_proj(O_KRB, 96, bB)
                            cx.op("dve", lambda e, bA=bA: e.tensor_tensor(out=r1[64:96, :], in0=bA[64:96, :], in1=ropet[64:96, 0, tok0:tok0 + 512],
                                                                          op=ALU.mult), reads=(bA.b, ropet.b), writes=(r1.b,))
                            cx.op("dve", lambda e, bB=bB: e.tensor_tensor(out=r2[64:96, :], in0=bB[64:96, :], in1=ropet[64:96, 1, tok0:tok0 + 512],
                                                                          op=ALU.mult), reads=(bB.b, ropet.b), writes=(r2.b,))
                            kr = krst.next()
                            cx.op("dve", lambda e, kr=kr: e.tensor_tensor(out=kr[64:96, :], in0=r1[64:96, :], in1=r2[64:96, :], op=ALU.add),
                                  reads=(r1.b, r2.b), writes=(kr.b,))
                            cx.dma("pool", xin_d[128:160, tok0:tok0 + 512], kr[64:96, :], reads=(kr.b,), writes=(xin_d.b,))
                            for g in range(4):
                                bk = pF.next()
                                fm_proj(O_QC + g * 128, 128, bk)
                                f1 = fst.next()
                                cx.op("act", lambda e, f1=f1, bk=bk: e.mul(f1[:], bk[:], 0.125),
                                      reads=(bk.b,), writes=(f1.b,))
                                cx.dma("pool", qcT_d[:, g, tok0:tok0 + 512], f1[:], reads=(f1.b,), writes=(qcT_d.b,))
                            bk = pF.next()
                            fm_proj(O_KC, 128, bk)
                            f1 = fst.next()
                            cx.op("act", lambda e, f1=f1, bk=bk: e.copy(out=f1[:], in_=bk[:]), reads=(bk.b,), writes=(f1.b,))
                            cx.dma("pool", kcT_d[:, 128 + tok0:128 + tok0 + 512], f1[:], reads=(f1.b,), writes=(kcT_d.b,))

                    skew(NT, [st_ld, st_a, st_b, st_c], [0, 2, 3, 4])

                    if dbg('misc'):
                        misc = xin_d.t[160:176, :].rearrange("r (q c) -> (r q) c", c=128)
                        cx.dma("pool", misc[0:128, :], kcT_d[:, 128:256], reads=(kcT_d.b,), writes=(xin_d.b,))
                        cx.dma("pool", misc[128:256, :], kcT_d[:, TOK:TOK + 128], reads=(kcT_d.b,), writes=(xin_d.b,))
                        cx.dma("pool", misc[256:384, :], vc_d[128:256, :], reads=(vc_d.b,), writes=(xin_d.b,))
                        cx.dma("pool", misc[384:512, :], vc_d[TOK:TOK + 128, :], reads=(vc_d.b,), writes=(xin_d.b,))

            if want("P2"):
                cx.cc("AllGather", ALU.bypass, xin_d.t, xout_d.t, reads=(xin_d.b,), writes=(xout_d.b,))

            if want("P5"):
                with Phase(cx) as ph:
                    Gs = Rot([ph.sb(f"G{i}", [128, 3, 256], BF16) for i in range(2)])
                    Yt = Rot([ph.sb(f"Yt{i}", [128, 2, D], BF16) for i in range(2)])
                    zst = Rot([ph.sb(f"zst{i}", [128, D], BF16) for i in range(4)])
                    pZ = Rot([ph.ps(f"pZ{i}", [128, 1024]) for i in range(4)])
                    Yv = [y.t.rearrange("(s1 s2) c -> s2 s1 c", s2=32) for y in (Yr_d, Yi_d)]
                    Zv = [z.t.rearrange("(k1 s2) c -> s2 k1 c", s2=32) for z in (Zr_d, Zi_d)]
                    for s2 in range(32):
                        G = Gs.next()
                        Y = Yt.next()
                        cx.dma("sp", G[:], G_in[s2].rearrange("w p c -> p w c"), reads=(), writes=(G.b,))
                        for w in range(2):
                            cx.dma("sp", Y[:, w, :], Yv[w][s2], reads=(Yr_d.b, Yi_d.b), writes=(Y.b,))
                        for m in range(2):
                            for part in range(2):
                                pz = pZ.next()
                                srcs = ((0, 0), (2, 1)) if part == 0 else ((0, 1), (1, 0))
                                for n in range(2):
                                    for i, (gw, yw) in enumerate(srcs):
                                        cx.op("pe", lambda e, pz=pz, G=G, Y=Y, gw=gw, yw=yw, n=n, i=i, m=m: e.matmul(
                                            out=pz[:, n * 512:(n + 1) * 512], lhsT=G[:, gw, m * 128:(m + 1) * 128],
                                            rhs=Y[:, yw, n * 512:(n + 1) * 512], start=(i == 0), stop=(i == 1)),
                                            inc=(i == 1), reads=(G.b, Y.b), writes=(pz.b,))
                                z = zst.next()
                                eng = "act" if part == 0 else "dve"
                                if eng == "act":
                                    cx.op("act", lambda e, z=z, pz=pz: e.copy(out=z[:], in_=pz[:]), reads=(pz.b,), writes=(z.b,))
                                else:
                                    cx.op("dve", lambda e, z=z, pz=pz: e.tensor_copy(out=z[:], in_=pz[:]), reads=(pz.b,), writes=(z.b,))
                                cx.dma("pool", Zv[part][s2, m * 128:(m + 1) * 128, :], z[:], reads=(z.b,), writes=((Zr_d, Zi_d)[part].b,))
                    cx.barrier()
                    Eb = ph.sb("Eb", [128, 2, 128], BF16)
                    cx.dma("sp", Eb[:], E_in.rearrange("w p c -> p w c"), reads=(), writes=(Eb.b,))
                    Zt = Rot([ph.sb(f"Zt{i}", [128, 2, D], BF16) for i in range(2)])
                    bst = Rot([ph.sb(f"bst{i}", [128, D], F32) for i in range(3)])
                    Bv = Bp_d.t.rearrange("(k2 r) c -> r k2 c", r=256)
                    for jj in range(64):
                        Z = Zt.next()
                        cx.dma("sp", Z[:, 0, :], Zr_d[jj * 128:(jj + 1) * 128, :], reads=(Zr_d.b,), writes=(Z.b,))
                        cx.dma("sp", Z[:, 1, :], Zi_d[jj * 128:(jj + 1) * 128, :], reads=(Zi_d.b,), writes=(Z.b,))
                        pz = pZ.next()
                        for n in range(2):
                            for w in range(2):
                                cx.op("pe", lambda e, pz=pz, Z=Z, n=n, w=w: e.matmul(out=pz[:, n * 512:(n + 1) * 512], lhsT=Eb[:, w, :],
                                                                                  rhs=Z[:, w, n * 512:(n + 1) * 512], start=(w == 0), stop=(w == 1)),
                                      inc=(w == 1), reads=(Eb.b, Z.b), writes=(pz.b,))
                        bs_ = bst.next()
                        if jj % 2 == 0:
                            cx.op("act", lambda e, bs_=bs_, pz=pz: e.copy(out=bs_[:], in_=pz[:]), reads=(pz.b,), writes=(bs_.b,))
                        else:
                            cx.op("dve", lambda e, bs_=bs_, pz=pz: e.tensor_copy(out=bs_[:], in_=pz[:]), reads=(pz.b,), writes=(bs_.b,))
                        for ks in range(4):
                            cx.dma("pool", Bv[4 * jj + ks], bs_[ks * 32:(ks + 1) * 32, :], reads=(bs_.b,), writes=(Bp_d.b,))
                    cx.cc("ReduceScatter", ALU.add, Bp_d.t, Brs_d.t, reads=(Bp_d.b,), writes=(Brs_d.b,))

            if want("P2"):
                m0 = xout_d.t[160:176, :].rearrange("r (q c) -> (r q) c", c=128)
                m1 = xout_d.t[XROWS + 160:XROWS + 176, :].rearrange("r (q c) -> (r q) c", c=128)
                cx.dma("pool", kcT_d[:, 0:128], m0[128:256, :], reads=(xout_d.b,), writes=(kcT_d.b,))
                cx.dma("pool", kcT_d[:, TOK + 128:TOK + 256], m1[0:128, :], reads=(xout_d.b,), writes=(kcT_d.b,))
                cx.dma("pool", vc_d[0:128, :], m0[384:512, :], reads=(xout_d.b,), writes=(vc_d.b,))
                cx.dma("pool", vc_d[TOK + 128:TOK + 256, :], m1[256:384, :], reads=(xout_d.b,), writes=(vc_d.b,))

            if want("P3"):
                with Phase(cx) as ph:
                    stage = mk_stage(ph)
                    Wq = ph.sb("Wq", [128, 2, 3, 768], BF16)
                    for which, src in enumerate((w_uqr, w_uqs)):
                        for c in range(3):
                            load_cast(ph, lambda c0, c1, which=which, c=c: (Wq[:, which, c, c0:c1], Wq.b),
                                      lambda c0, c1, src=src, c=c: src[l, c * 128:(c + 1) * 128, c0:c1], 128, 768, stage)
                    Wkv = ph.sb("Wkv", [128, 1024], BF16)
                    load_cast(ph, lambda c0, c1: (Wkv[:, c0:c1], Wkv.b), lambda c0, c1: w_ukv[l, :, c0:c1], 128, 1024, stage)
                    cqn = ph.sb("cqn", [128, 3, TOK], BF16)
                    cx.dma("sp", cqn[:], cqT_d.t.rearrange("(c p) t -> p c t", p=128), reads=(cqT_d.b,), writes=(cqn.b,))
                    ckv = ph.sb("ckv", [128, SEQ], BF16)
                    KTs = [ph.sb(f"KT{i}", [96, SEQ], BF16) for i in range(2)]
                    for r in range(2):
                        cx.dma("sp", ckv[:, r * TOK:(r + 1) * TOK], xout_d[r * XROWS:r * XROWS + 128, :], reads=(xout_d.b,), writes=(ckv.b,))
                        for i in range(2):
                            cx.dma("sp", KTs[i][64:96, r * TOK:(r + 1) * TOK], xout_d[r * XROWS + 128:r * XROWS + 160, :],
                                   reads=(xout_d.b,), writes=(KTs[i].b,))
                    Vs = [ph.sb(f"V{i}", [128, 64, 65], BF16) for i in range(2)]
                    for i in range(2):
                        cx.op("pool", lambda e, i=i: e.memset(Vs[i][:, :, 64:65], 1.0), writes=(Vs[i].b,))
                    QTs = [ph.sb(f"QT{i}", [96, TOK], BF16) for i in range(2)]
                    ropet = ph.sb("ropet", [96, 2, TOK], F32)
                    cx.dma("sp", ropet[64:96], rope_d.t.rearrange("w p t -> p w t"), reads=(rope_d.b,), writes=(ropet.b,))
                    r1 = ph.sb("r1", [96, 512], F32)
                    r2 = ph.sb("r2", [96, 512], F32)
                    ones1 = ph.sb("ones1", [65, 64], F32)
                    cx.op("dve", lambda e: e.memset(ones1[:], 1.0), writes=(ones1.b,))
                    PT2 = Rot([ph.sb(f"PT{i}", [128, 1024], BF16) for i in range(4)])
                    osbs = Rot([ph.sb(f"osb{i}", [65, 512], F32) for i in range(2)])
                    ost = Rot([ph.sb(f"ost{i}", [64, 512], BF16) for i in range(2)])
                    pS2 = Rot([ph.ps(f"pS{i}", [128, 1024]) for i in range(3)])
                    pO = Rot([ph.ps(f"pO{i}", [128, 512]) for i in range(1)])
                    pX = Rot([ph.ps(f"pX{i}", [128, 512]) for i in range(1)])

                    half_state = [None, 0]

                    def half_bank():
                        if half_state[0] is None or half_state[1] == 2:
                            half_state[0] = pS2.next()
                            half_state[1] = 0
                        t_, o_ = half_state[0], half_state[1] * 512
                        half_state[1] += 1
                        return t_, o_

                    def prep_head_gen(h):
                        KT, V, QT = KTs[h % 2], Vs[h % 2], QTs[h % 2]
                        for ch in range(16):
                            bk, o = half_bank()
                            cx.op("pe", lambda e, bk=bk, o=o, ch=ch: e.matmul(out=bk[0:64, o:o + 512], lhsT=Wkv[:, h * 128:h * 128 + 64], rhs=ckv[:, ch * 512:(ch + 1) * 512],
                                                                             start=True, stop=True), reads=(Wkv.b, ckv.b), writes=(bk.b,))
                            cx.op("dve", lambda e, bk=bk, o=o, ch=ch: e.tensor_copy(out=KT[0:64, ch * 512:(ch + 1) * 512], in_=bk[0:64, o:o + 512]),
                                  reads=(bk.b,), writes=(KT.b,))
                            yield
                        for g8 in range(8):
                            bk, o = half_bank()
                            for j in range(8):
                                kt = g8 * 8 + j
                                cx.op("pe", lambda e, bk=bk, o=o, kt=kt, j=j: e.matmul(out=bk[:, o + j * 64:o + (j + 1) * 64], lhsT=ckv[:, kt * 128:(kt + 1) * 128],
                                                                                      rhs=Wkv[:, h * 128 + 64:h * 128 + 128], start=True, stop=True),
                                      inc=(j == 7), reads=(Wkv.b, ckv.b), writes=(bk.b,))
                            cx.op("dve", lambda e, bk=bk, o=o, g8=g8: e.tensor_copy(out=V[:, g8 * 8:(g8 + 1) * 8, 0:64],
                                                                                   in_=bk[:, o:o + 512].rearrange("p (j c) -> p j c", j=8)),
                                  reads=(bk.b,), writes=(V.b,))
                            yield
                        for ch in range(8):
                            half_state[1] = 2
                            bk, oA = half_bank()
                            _, oB = half_bank()
                            for which, o in ((0, oA), (1, oB)):
                                for c in range(3):
                                    cx.op("pe", lambda e, which=which, o=o, c=c, ch=ch: e.matmul(
                                        out=bk[0:96, o:o + 512], lhsT=Wq[:, which, c, h * 96:(h + 1) * 96], rhs=cqn[:, c, ch * 512:(ch + 1) * 512],
                                        start=(c == 0), stop=(c == 2)), inc=(c == 2), reads=(Wq.b, cqn.b), writes=(bk.b,))
                            cx.op("dve", lambda e, ch=ch: e.tensor_copy(out=QT[0:64, ch * 512:(ch + 1) * 512], in_=bk[0:64, oA:oA + 512]),
                                  reads=(bk.b,), writes=(QT.b,))
                            cx.op("dve", lambda e, ch=ch: e.tensor_tensor(out=r1[64:96, :], in0=bk[64:96, oA:oA + 512], in1=ropet[64:96, 0, ch * 512:(ch + 1) * 512],
                                                                          op=ALU.mult), reads=(bk.b, ropet.b), writes=(r1.b,))
                            cx.op("dve", lambda e, ch=ch: e.tensor_tensor(out=r2[64:96, :], in0=bk[64:96, oB:oB + 512], in1=ropet[64:96, 1, ch * 512:(ch + 1) * 512],
                                                                          op=ALU.mult), reads=(bk.b, ropet.b), writes=(r2.b,))
                            cx.op("dve", lambda e, ch=ch: e.tensor_tensor(out=QT[64:96, ch * 512:(ch + 1) * 512], in0=r1[64:96, :], in1=r2[64:96, :], op=ALU.add),
                                  reads=(r1.b, r2.b), writes=(QT.b,))
                            yield

                    def prep_head(h):
                        for _ in prep_head_gen(h):
                            pass

                    LOOK = 2
                    items = [(h, qc, pr) for h in range(8) for qc in range(8) for pr in range(32)]
                    pend = []
                    inflight = []
                    cur_po = {}

                    def s_stage(h, qc, pr):
                        KT, QT = KTs[h % 2], QTs[h % 2]
                        ps2 = pS2.next()
                        pt2 = PT2.next()
                        for j in range(2):
                            kt = 2 * pr + j
                            cx.op("pe", lambda e, j=j, kt=kt: e.matmul(out=ps2[:, j * 512:(j + 1) * 512], lhsT=KT[:, kt * 128:(kt + 1) * 128],
                                                                       rhs=QT[:, qc * 512:(qc + 1) * 512], start=True, stop=True),
                                  inc=(j == 1), reads=(KT.b, QT.b), writes=(ps2.b,))
                        cx.op("act", lambda e: e.activation(out=pt2[:], in_=ps2[:], func=AF.Exp, scale=MLA_SCALE),
                              reads=(ps2.b,), writes=(pt2.b,))
                        return pt2

                    def pv_stage(h, qc, pr, pt2, step):
                        V = Vs[h % 2]
                        if pr == 0:
                            cur_po[(h, qc)] = pO.next()
                        po = cur_po[(h, qc)]
                        for j in range(2):
                            kt = 2 * pr + j
                            cx.op("pe", lambda e, j=j, kt=kt: e.matmul(out=po[0:65, :], lhsT=V[:, kt, :], rhs=pt2[:, j * 512:(j + 1) * 512],
                                                                       start=(kt == 0), stop=(kt == 63)),
                                  inc=(j == 1), reads=(V.b, pt2.b), writes=(po.b,))
                        if pr == 31:
                            osb = osbs.next()
                            cx.op("dve", lambda e: e.tensor_copy(out=osb[:], in_=po[0:65, :]), reads=(po.b,), writes=(osb.b,))

                            def fin_a(osb=osb):
                                cx.op("act", lambda e: e.activation(out=osb[64:65, :], in_=osb[64:65, :], func=AF.Ln), reads=(osb.b,), writes=(osb.b,))
                                cx.op("act", lambda e: e.activation(out=osb[64:65, :], in_=osb[64:65, :], func=AF.Exp, scale=-1.0), reads=(osb.b,), writes=(osb.b,))
                            pend.append((step + 2, fin_a))

                            def fin(h=h, qc=qc, osb=osb):
                                bc = pX.next()
                                cx.op("pe", lambda e: e.matmul(out=bc[0:64, :], lhsT=ones1[64:65, :], rhs=osb[64:65, :], start=True, stop=True),
                                      reads=(ones1.b, osb.b), writes=(bc.b,))
                                o = ost.next()
                                cx.op("dve", lambda e: e.tensor_tensor(out=o[:], in0=osb[0:64, :], in1=bc[0:64, :], op=ALU.mult),
                                      reads=(osb.b, bc.b), writes=(o.b,))
                                cx.dma("pool", oaT_d[h * 64:(h + 1) * 64, qc * 512:(qc + 1) * 512], o[:], reads=(o.b,), writes=(oaT_d.b,))
                            pend.append((step + 6, fin))

                    prep_head(0)
                    prep_head(1)
                    prep_gen = [None]
                    n_it = len(items)
                    for step in range(n_it + LOOK + 8):
                        if step < n_it:
                            h, qc, pr = items[step]
                            inflight.append((items[step], s_stage(h, qc, pr)))
                        if step >= LOOK and inflight and step - LOOK < n_it:
                            (h, qc, pr), pt2 = inflight.pop(0)
                            pv_stage(h, qc, pr, pt2, step)
                            if qc == 7 and pr == 31 and h + 2 < 8:
                                assert prep_gen[0] is None
                                prep_gen[0] = prep_head_gen(h + 2)
                        if prep_gen[0] is not None and step % 7 == 0:
                            try:
                                next(prep_gen[0])
                            except StopIteration:
                                prep_gen[0] = None
                        while pend and pend[0][0] <= step:
                            pend.pop(0)[1]()
                    assert not pend and not inflight and prep_gen[0] is None

            if want("P4"):
                with Phase(cx) as ph:
                    kcT = ph.sb("kcT", [128, 2, TOK + 256], BF16)
                    cx.op("dve", lambda e: e.memset(kcT[64:128, 0, :], 0.0), writes=(kcT.b,))
                    cx.op("dve", lambda e: e.memset(kcT[0:64, 1, :], 0.0), reads=(kcT.b,), writes=(kcT.b,))
                    cx.dma("sp", kcT[0:64, 0, :], kcT_d[0:64, :], reads=(kcT_d.b,), writes=(kcT.b,))
                    cx.dma("sp", kcT[64:128, 1, :], kcT_d[64:128, :], reads=(kcT_d.b,), writes=(kcT.b,))
                    vc = ph.sb("vc", [128, 34, 2, 65], BF16)
                    cx.op("pool", lambda e: e.memset(vc[:, :, :, 64:65], 1.0), writes=(vc.b,))
                    for kvh in range(2):
                        cx.dma("sp", vc[:, :, kvh, 0:64], vc_d.t.rearrange("(t p) c -> p t c", p=128)[:, :, kvh * 64:(kvh + 1) * 64],
                               reads=(vc_d.b,), writes=(vc.b,))
                    qcT = ph.sb("qcT", [128, 4, TOK], BF16)
                    cx.dma("sp", qcT[:], qcT_d.t, reads=(qcT_d.b,), writes=(qcT.b,))
                    posq_i = ph.sb("posq_i", [128, TOK], I32)
                    cx.dma("sp", posq_i[:], pos_in.partition_broadcast(128), reads=(), writes=(posq_i.b,))
                    posq = ph.sb("posq", [128, TOK], F32)
                    cx.op("dve", lambda e: e.tensor_copy(out=posq[:], in_=posq_i[:]), reads=(posq_i.b,), writes=(posq.b,))
                    posk_i = ph.sb("posk_i", [128, 34], I32)
                    cx.dma("sp", posk_i[:], posx_in.rearrange("t p -> p t"), reads=(), writes=(posk_i.b,), allow_slow_non_contiguous=True)
                    posk = ph.sb("posk", [128, 34], F32)
                    cx.op("dve", lambda e: e.tensor_copy(out=posk[:], in_=posk_i[:]), reads=(posk_i.b,), writes=(posk.b,))
                    cx.op("dve", lambda e: e.tensor_scalar(out=posk[:], in0=posk[:], scalar1=-1.0, scalar2=None, op0=ALU.mult),
                          reads=(posk.b,), writes=(posk.b,))
                    wm = ph.sb("wm", [128, 4, 128], F32)
                    cx.dma("sp", wm[:], wmask_in.rearrange("w p c -> p w c"), reads=(), writes=(wm.b,))
                    es_ = ph.sb("es", [65, 8], F32)
                    cx.dma("sp", es_[64:65, :], sink_in[l:l + 1, :], reads=(), writes=(es_.b,))
                    cx.op("act", lambda e: e.activation(out=es_[64:65, :], in_=es_[64:65, :], func=AF.Exp), reads=(es_.b,), writes=(es_.b,))
                    ones1 = ph.sb("ones1", [65, 64], F32)
                    cx.op("dve", lambda e: e.memset(ones1[:], 1.0), writes=(ones1.b,))
                    onehot = ph.sb("onehot", [65, 65], BF16)
                    cx.op("dve", lambda e: e.memset(onehot[:], 0.0), writes=(onehot.b,))
                    cx.op("dve", lambda e: e.memset(onehot[64:65, 64:65], 1.0), reads=(onehot.b,), writes=(onehot.b,))
                    esrow = ph.sb("esrow", [65, 2, 512], BF16)
                    for kvh in range(2):
                        cx.op("dve", lambda e, kvh=kvh: e.tensor_copy(out=esrow[64:65, kvh, :].rearrange("p (g q) -> p g q", g=4),
                                                                      in_=es_[64:65, kvh * 4:(kvh + 1) * 4].unsqueeze(2).to_broadcast([1, 4, 128])),
                              reads=(es_.b,), writes=(esrow.b,))
                    slopeT = ph.sb("slopeT", [128, 2, 4, 128], F32)
                    for kvh in range(2):
                        for g in range(4):
                            cx.op("dve", lambda e, kvh=kvh, g=g: e.memset(slopeT[:, kvh, g, :], -SLOPES[kvh * 4 + g]), writes=(slopeT.b,))
                    Ds = Rot([ph.sb(f"Dm{i}", [128, 128], F32) for i in range(9)])
                    biases = Rot([ph.sb(f"bias{i}", [128, 512], F32) for i in range(12)])
                    tmps = Rot([ph.sb(f"tmp{i}", [128, 512], F32) for i in range(3)])
                    PTs = Rot([ph.sb(f"PT{i}", [128, 512], BF16) for i in range(4)])
                    osbs = Rot([ph.sb(f"osb{i}", [65, 512], F32) for i in range(3)])
                    ost = Rot([ph.sb(f"ost{i}", [64, 512], BF16) for i in range(2)])
                    pS = Rot([ph.ps(f"pS{i}", [128, 512]) for i in range(4)])
                    pO = Rot([ph.ps(f"pO{i}", [128, 512]) for i in range(2)])
                    pX = Rot([ph.ps(f"pX{i}", [128, 512]) for i in range(2)])
                    ocv = ocT_d.t.rearrange("(h d) t -> d h t", d=64)
                    LOOK = 2
                    steps = [(j, kvh, kk) for j in range(NT) for kvh in range(2) for kk in range(3)]
                    blk_bias = {}
                    cur_po = {}
                    pend = []
                    inflight = []

                    def block_prep(j):
                        bl = {}
                        for kk in range(3):
                            tt = j + kk
                            Dm = Ds.next()
                            cx.op("act", lambda e, Dm=Dm, tt=tt: e.activation(out=Dm[:], in_=posq[:, j * 128:(j + 1) * 128], func=AF.Abs,
                                                                              bias=posk[:, tt:tt + 1], scale=1.0),
                                  reads=(posq.b, posk.b), writes=(Dm.b,))
                            mi = None
                            if kk == 0:
                                mi = 2 if j == 0 else 0
                            elif kk == 2:
                                mi = 3 if j == NT - 1 else 1
                            if mi is not None:
                                cx.op("dve", lambda e, Dm=Dm, mi=mi: e.tensor_tensor(out=Dm[:], in0=Dm[:], in1=wm[:, mi, :], op=ALU.add),
                                      reads=(Dm.b, wm.b), writes=(Dm.b,))
                            bl[kk] = Dm
                        blk_bias[j] = bl

                    def s_stage(j, kvh, kk):
                        tt = j + kk
                        ps_ = pS.next()
                        cx.op("pe", lambda e: e.matmul(
                            out=ps_[:].rearrange("p (g q) -> p g q", g=4), lhsT=kcT[:, kvh, tt * 128:(tt + 1) * 128],
                            rhs=qcT[:, :, j * 128:(j + 1) * 128], start=True, stop=True),
                            reads=(kcT.b, qcT.b), writes=(ps_.b,))
                        tmp = tmps.next()
                        Dm = blk_bias[j][kk]
                        for g in range(4):
                            cx.op("dve", lambda e, g=g: e.scalar_tensor_tensor(
                                out=tmp[:, g * 128:(g + 1) * 128], in0=Dm[:], scalar=-SLOPES[kvh * 4 + g],
                                in1=ps_[:, g * 128:(g + 1) * 128], op0=ALU.mult, op1=ALU.add),
                                reads=(Dm.b, ps_.b), writes=(tmp.b,))
                        pt = PTs.next()
                        cx.op("act", lambda e: e.activation(out=pt[:], in_=tmp[:], func=AF.Exp), reads=(tmp.b,), writes=(pt.b,))
                        return pt

                    def pv_stage(j, kvh, kk, pt, step):
                        tt = j + kk
                        if kk == 0:
                            cur_po[(j, kvh)] = pO.next()
                        po = cur_po[(j, kvh)]
                        if kk == 0:
                            cx.op("pe", lambda e: e.matmul(out=po[0:65, :], lhsT=onehot[64:65, :], rhs=esrow[64:65, kvh, :], start=True, stop=False),
                                  inc=False, reads=(onehot.b, esrow.b), writes=(po.b,))
                        cx.op("pe", lambda e: e.matmul(out=po[0:65, :], lhsT=vc[:, tt, kvh, :], rhs=pt[:], start=False, stop=(kk == 2)),
                              reads=(vc.b, pt.b), writes=(po.b,))
                        if kk == 2:
                            osb = osbs.next()

                            def fin_a(osb=osb, po=po):
                                cx.op("act", lambda e: e.copy(out=osb[:], in_=po[0:65, :]), reads=(po.b,), writes=(osb.b,))
                                cx.op("act", lambda e: e.activation(out=osb[64:65, :], in_=osb[64:65, :], func=AF.Ln), reads=(osb.b,), writes=(osb.b,))
                                cx.op("act", lambda e: e.activation(out=osb[64:65, :], in_=osb[64:65, :], func=AF.Exp, scale=-1.0), reads=(osb.b,), writes=(osb.b,))

                            st8 = {}

                            def fin_b(osb=osb):
                                bc = pX.next()
                                st8["bc"] = bc
                                cx.op("pe", lambda e: e.matmul(out=bc[0:64, :], lhsT=ones1[64:65, :], rhs=osb[64:65, :], start=True, stop=True),
                                      reads=(ones1.b, osb.b), writes=(bc.b,))

                            def fin_c(osb=osb):
                                bc = st8["bc"]
                                o = ost.next()
                                cx.op("dve", lambda e: e.tensor_tensor(out=o[:], in0=osb[0:64, :], in1=bc[0:64, :], op=ALU.mult),
                                      reads=(osb.b, bc.b), writes=(o.b,))
                                cx.dma("pool", ocv[:, kvh * 4:(kvh + 1) * 4, j * 128:(j + 1) * 128], o[:].rearrange("p (g t) -> p g t", g=4),
                                       reads=(o.b,), writes=(ocT_d.b,))
                            pend.append((step + 2, fin_a))
                            pend.append((step + 3, fin_b))
                            pend.append((step + 4, fin_c))

                    block_prep(0)
                    n_st = len(steps)
                    for step in range(n_st + LOOK + 6):
                        if step < n_st:
                            j, kvh, kk = steps[step]
                            if kvh == 0 and kk == 0 and j + 1 < NT:
                                block_prep(j + 1)
                            inflight.append((steps[step], s_stage(j, kvh, kk)))
                        if step >= LOOK and inflight and step - LOOK < n_st:
                            (j, kvh, kk), pt = inflight.pop(0)
                            pv_stage(j, kvh, kk, pt, step)
                        pend.sort(key=lambda x: x[0])
                        while pend and pend[0][0] <= step:
                            pend.pop(0)[1]()
                    assert not pend and not inflight

            if want("P6"):
                with Phase(cx) as ph:
                    stage = mk_stage(ph, 6)
                    Wg = ph.sb("Wg", [128, 8, 3 * D], BF16)
                    load_w_kc(ph, Wg, w_gates[l], 8, 3 * D, stage)
                    Wa = ph.sb("Wa", [128, 4, D], BF16)
                    load_w_kc(ph, Wa, w_a[l], 4, D, stage)
                    Wc = ph.sb("Wc", [128, 4, D], BF16)
                    load_w_kc(ph, Wc, w_c[l], 4, D, stage)
                    Wo = ph.sb("Wo", [128, 8, D], BF16)
                    load_w_kc(ph, Wo, w_out[l], 8, D, stage)
                    ident = ph.sb("ident", [128, 128], BF16)
                    cx.dma("sp", ident[:], ident_in, reads=(), writes=(ident.b,))
                    gpost = gain_bc(ph, "gpost", g_mix_post[l:l + 1, :])
                    hTt = [ph.sb(f"hTt{i}", [128, 8, 128], BF16) for i in range(2)]
                    oat = [ph.sb(f"oat{i}", [128, 4, 128], BF16) for i in range(2)]
                    oct_ = [ph.sb(f"oct{i}", [128, 4, 128], BF16) for i in range(2)]
                    brs = [ph.sb(f"brs{i}", [128, D], F32) for i in range(2)]
                    xts = [ph.sb(f"xt{i}", [128, D], F32) for i in range(4)]
                    sg = [ph.sb(f"sg{i}", [128, D], F32) for i in range(3)]
                    mg = ph.sb("mg", [128, D], F32)
                    tm = ph.sb("tm", [128, D], F32)
                    mbs = [ph.sb(f"mb{i}", [128, D], BF16) for i in range(2)]
                    mTs = [ph.sb(f"mT{i}", [128, 8, 128], BF16) for i in range(2)]
                    junk = ph.sb("junk", [128, D], F32)
                    sss = [ph.sb(f"ss{i}", [128, 1], F32) for i in range(2)]
                    xos = [ph.sb(f"xo{i}", [128, D], F32) for i in range(2)]
                    pG = Rot([ph.ps(f"pG{i}", [128, 1024]) for i in range(3)])
                    pT = ph.ps("pT", [128, 8, 128], BF16)
                    hv = hT_d.t.rearrange("(kc p) t -> p kc t", p=128)
                    oav = oaT_d.t.rearrange("(kc p) t -> p kc t", p=128)
                    ocv2 = ocT_d.t.rearrange("(kc p) t -> p kc t", p=128)

                    def st_a(t):
                        sl = slice(t * 128, (t + 1) * 128)
                        hT, oa, oc, br, xt = hTt[t % 2], oat[t % 2], oct_[t % 2], brs[t % 2], xts[t % 4]
                        cx.dma("sp", hT[:], hv[:, :, sl], reads=(hT_d.b,), writes=(hT.b,))
                        cx.dma("sp", oa[:], oav[:, :, sl], reads=(oaT_d.b,), writes=(oa.b,))
                        cx.dma("sp", oc[:], ocv2[:, :, sl], reads=(ocT_d.b,), writes=(oc.b,))
                        cx.dma("sp", br[:], Brs_d[sl, :], reads=(Brs_d.b,), writes=(br.b,))
                        cx.dma("sp", xt[:], x_src[sl, :], reads=(x_src_b,), writes=(xt.b,))

                    def st_b(t):
                        hT, oa, oc, br, mb = hTt[t % 2], oat[t % 2], oct_[t % 2], brs[t % 2], mbs[t % 2]
                        for gi in range(3):
                            pg = pG.next()
                            for n in range(2):
                                for kc in range(8):
                                    cx.op("pe", lambda e, pg=pg, kc=kc, n=n, gi=gi: e.matmul(
                                        out=pg[:, n * 512:(n + 1) * 512], lhsT=hT[:, kc, :], rhs=Wg[:, kc, gi * D + n * 512:gi * D + (n + 1) * 512],
                                        start=(kc == 0), stop=(kc == 7)), inc=(kc == 7), reads=(hT.b, Wg.b), writes=(pg.b,))
                            cx.op("act", lambda e, pg=pg, gi=gi: e.activation(out=sg[gi][:], in_=pg[:], func=AF.Sigmoid),
                                  reads=(pg.b,), writes=(sg[gi].b,))
                        pa = pG.next()
                        for n in range(2):
                            for k4 in range(4):
                                cx.op("pe", lambda e, k4=k4, n=n: e.matmul(out=pa[:, n * 512:(n + 1) * 512], lhsT=oa[:, k4, :],
                                                                          rhs=Wa[:, k4, n * 512:(n + 1) * 512], start=(k4 == 0), stop=(k4 == 3)),
                                      inc=(k4 == 3), reads=(oa.b, Wa.b), writes=(pa.b,))
                        cx.op("dve", lambda e: e.tensor_tensor(out=mg[:], in0=pa[:], in1=sg[0][:], op=ALU.mult),
                              reads=(pa.b, sg[0].b), writes=(mg.b,))
                        pc = pG.next()
                        for n in range(2):
                            for k4 in range(4):
                                cx.op("pe", lambda e, k4=k4, n=n: e.matmul(out=pc[:, n * 512:(n + 1) * 512], lhsT=oc[:, k4, :],
                                                                          rhs=Wc[:, k4, n * 512:(n + 1) * 512], start=(k4 == 0), stop=(k4 == 3)),
                                      inc=(k4 == 3), reads=(oc.b, Wc.b), writes=(pc.b,))
                        cx.op("dve", lambda e: e.tensor_tensor(out=tm[:], in0=pc[:], in1=sg[2][:], op=ALU.mult),
                              reads=(pc.b, sg[2].b), writes=(tm.b,))
                        cx.op("dve", lambda e: e.tensor_tensor(out=mg[:], in0=mg[:], in1=tm[:], op=ALU.add), reads=(mg.b, tm.b), writes=(mg.b,))
                        cx.op("dve", lambda e: e.tensor_tensor(out=br[:], in0=br[:], in1=sg[1][:], op=ALU.mult),
                              reads=(br.b, sg[1].b), writes=(br.b,))
                        cx.op("dve", lambda e: e.tensor_tensor(out=mb[:], in0=mg[:], in1=br[:], op=ALU.add),
                              reads=(mg.b, br.b), writes=(mb.b,))

                    def st_c(t):
                        mb, mT = mbs[t % 2], mTs[t % 2]
                        for kc in range(8):
                            cx.op("pe", lambda e, kc=kc: e.transpose(out=pT[:, kc, :], in_=mb[:, kc * 128:(kc + 1) * 128], identity=ident[:]), inc=(kc == 7), reads=(mb.b, ident.b), writes=(pT.b,))
                        cx.op("act", lambda e: e.copy(out=mT[:], in_=pT[:]), reads=(pT.b,), writes=(mT.b,))

                    def st_d(t):
                        sl = slice(t * 128, (t + 1) * 128)
                        mT, xt, ss, xo_ = mTs[t % 2], xts[t % 4], sss[t % 2], xos[t % 2]
                        py = pG.next()
                        for n in range(2):
                            for kc in range(8):
                                cx.op("pe", lambda e, kc=kc, n=n: e.matmul(out=py[:, n * 512:(n + 1) * 512], lhsT=mT[:, kc, :],
                                                                          rhs=Wo[:, kc, n * 512:(n + 1) * 512], start=(kc == 0), stop=(kc == 7)),
                                      inc=(kc == 7), reads=(mT.b, Wo.b), writes=(py.b,))
                        rms_rstd_tokmajor(ph, py[:], (py.b,), junk, ss, D)
                        cx.op("dve", lambda e: e.scalar_tensor_tensor(out=xo_[:], in0=py[:], scalar=ss[:, 0:1], in1=gpost[:],
                                                                      op0=ALU.mult, op1=ALU.mult),
                              reads=(py.b, ss.b, gpost.b), writes=(xo_.b,))
                        cx.op("dve", lambda e: e.tensor_tensor(out=xo_[:], in0=xo_[:], in1=xt[:], op=ALU.add),
                              reads=(xo_.b, xt.b), writes=(xo_.b,))
                        cx.dma("pool", xb[sl, :], xo_[:], reads=(xo_.b,), writes=(xb.b,))

                    skew(NT, [st_a, st_b, st_c, st_d])

            if want("P7"):
                with Phase(cx) as ph:
                    stage = mk_stage(ph, 6)
                    Wfg = ph.sb("Wfg", [128, 8, FFN], BF16)
                    load_w_kc(ph, Wfg, w_fg[l], 8, FFN, stage)
                    Wfu = ph.sb("Wfu", [128, 8, FFN], BF16)
                    load_w_kc(ph, Wfu, w_fu[l], 8, FFN, stage)
                    ident = ph.sb("ident", [128, 128], BF16)
                    cx.dma("sp", ident[:], ident_in, reads=(), writes=(ident.b,))
                    gpre = gain_bc(ph, "gfpre", g_ffn_pre[l:l + 1, :])
                    xts = [ph.sb(f"xt{i}", [128, D], F32) for i in range(4)]

                    def ld_x(t):
                        if t < NT:
                            cx.dma("sp", xts[t % 4][:], xb[t * 128:(t + 1) * 128, :], reads=(xb.b,), writes=(xts[t % 4].b,))
                    ld_x(0)
                    ld_x(1)
                    junk = ph.sb("junk", [128, D], F32)
                    sss = Rot([ph.sb(f"ss{i}", [128, 1], F32) for i in range(3)])
                    hbs = Rot([ph.sb(f"hb{i}", [128, D], BF16) for i in range(2)])
                    hTs = Rot([ph.sb(f"hT{i}", [128, 8, 512], BF16, nsub=4) for i in range(2)])
                    ast = Rot([ph.sb(f"ast{i}", [128, 512], BF16) for i in range(3)])
                    sil = Rot([ph.sb(f"sil{i}", [128, 512], F32) for i in range(2)])
                    pGU = Rot([ph.ps(f"pGU{i}", [128, 512]) for i in range(6)])
                    pT = ph.ps("pT", [128, 8, 128], BF16)
                    hT_l = hTs.items

                    def tile_front(t):
                        ci, tt = t // 4, t % 4
                        hT = hT_l[ci % 2]
                        xt = xts[t % 4]
                        ss = sss.next(); hb = hbs.next()
                        ld_x(t + 2)
                        rms_rstd_tokmajor(ph, xt[:], (xt.b,), junk, ss, D)
                        cx.op("dve", lambda e: e.scalar_tensor_tensor(
                            out=hb[:], in0=xt[:], scalar=ss[:, 0:1], in1=gpre[:], op0=ALU.mult, op1=ALU.mult),
                            reads=(xt.b, ss.b, gpre.b), writes=(hb.b,))
                        for kc in range(8):
                            cx.op("pe", lambda e, kc=kc: e.transpose(out=pT[:, kc, :], in_=hb[:, kc * 128:(kc + 1) * 128], identity=ident[:]),
                                  inc=(kc == 7), reads=(hb.b, ident.b), writes=(pT.b,))
                        cx.op("act", lambda e: e.copy(out=hT[:, :, tt * 128:(tt + 1) * 128], in_=pT[:]),
                              reads=(pT.b,), writes=(hT.bs[tt],))

                    for tt in range(4):
                        tile_front(tt)
                    for ci in range(NCH):
                        hT = hT_l[ci % 2]
                        for f in range(NF):
                            if ci + 1 < NCH and f in (3, 8, 13, 18):
                                tile_front((ci + 1) * 4 + (f - 3) // 5)
                            pg = pGU.next(); pu = pGU.next()
                            for pp, W in ((pg, Wfg), (pu, Wfu)):
                                for kc in range(8):
                                    cx.op("pe", lambda e, pp=pp, W=W, kc=kc, f=f, hT=hT: e.matmul(
                                        out=pp[:], lhsT=W[:, kc, f * 128:(f + 1) * 128], rhs=hT[:, kc, :], start=(kc == 0), stop=(kc == 7)),
                                        inc=(kc == 7), reads=tuple(hT.bs) + (W.b,), writes=(pp.b,))
                            s_ = sil.next()
                            cx.op("act", lambda e, s_=s_, pg=pg: e.activation(out=s_[:], in_=pg[:], func=AF.Silu), reads=(pg.b,), writes=(s_.b,))
                            a_ = ast.next()
                            cx.op("dve", lambda e, s_=s_, pu=pu, a_=a_: e.tensor_tensor(out=a_[:], in0=pu[:], in1=s_[:], op=ALU.mult),
                                  reads=(pu.b, s_.b), writes=(a_.b,))
                            cx.dma("pool", act_d[f * 128:(f + 1) * 128, ci * 512:(ci + 1) * 512], a_[:], reads=(a_.b,), writes=(act_d.b,))
                with Phase(cx) as ph:
                    stage = mk_stage(ph, 6)
                    Wfd = ph.sb("Wfd", [128, NF, D], BF16)
                    load_w_kc(ph, Wfd, w_fd[l], NF, D, stage)
                    gpost = gain_bc(ph, "gfpost", g_ffn_post[l:l + 1, :])
                    aTs = Rot([ph.sb(f"aT{i}", [128, NF, 512], BF16) for i in range(2)])
                    xts = Rot([ph.sb(f"xt{i}", [128, D], F32) for i in range(3)])
                    junk = ph.sb("junk", [128, D], F32)
                    sss = Rot([ph.sb(f"ss{i}", [128, 1], F32) for i in range(3)])
                    xo = Rot([ph.sb(f"xo{i}", [128, D], F32) for i in range(2)])
                    pD = Rot([ph.ps(f"pD{i}", [128, 1024]) for i in range(3)])
                    av = act_d.t.rearrange("(f p) t -> p f t", p=128)
                    for ci in range(NCH):
                        aT = aTs.next()
                        cx.dma("sp", aT[:], av[:, :, ci * 512:(ci + 1) * 512], reads=(act_d.b,), writes=(aT.b,))
                        for tt in range(4):
                            t = ci * 4 + tt
                            xt = xts.next()
                            cx.dma("sp", xt[:], xb[t * 128:(t + 1) * 128, :], reads=(xb.b,), writes=(xt.b,))
                            pd = pD.next()
                            for n in range(2):
                                for f in range(NF):
                                    cx.op("pe", lambda e, pd=pd, f=f, n=n, tt=tt, aT=aT: e.matmul(
                                        out=pd[:, n * 512:(n + 1) * 512], lhsT=aT[:, f, tt * 128:(tt + 1) * 128], rhs=Wfd[:, f, n * 512:(n + 1) * 512],
                                        start=(f == 0), stop=(f == NF - 1)), inc=(f == NF - 1), reads=(aT.b, Wfd.b), writes=(pd.b,))
                            ss = sss.next()
                            rms_rstd_tokmajor(ph, pd[:], (pd.b,), junk, ss, D)
                            xo_ = xo.next()
                            cx.op("dve", lambda e, pd=pd, ss=ss, xo_=xo_: e.scalar_tensor_tensor(out=xo_[:], in0=pd[:], scalar=ss[:, 0:1], in1=gpost[:],
                                                                                                op0=ALU.mult, op1=ALU.mult),
                                  reads=(pd.b, ss.b, gpost.b), writes=(xo_.b,))
                            cx.op("dve", lambda e, xo_=xo_, xt=xt: e.tensor_tensor(out=xo_[:], in0=xo_[:], in1=xt[:], op=ALU.add),
                                  reads=(xo_.b, xt.b), writes=(xo_.b,))
                            cx.dma("pool", xa[t * 128:(t + 1) * 128, :], xo_[:], reads=(xo_.b,), writes=(xa.b,))

            if want("P8"):
                last = (l == nlayers - 1)
                with Phase(cx) as ph:
                    stage = mk_stage(ph, 6)
                    Wpg = ph.sb("Wpg", [128, 8, D], BF16)
                    load_w_kc(ph, Wpg, w_pg[l], 8, D, stage)
                    Wpp = ph.sb("Wpp", [128, 2, D], BF16)
                    load_w_kc(ph, Wpp, w_pp[l], 2, D, stage)
                    ident = ph.sb("ident", [128, 128], BF16)
                    cx.dma("sp", ident[:], ident_in, reads=(), writes=(ident.b,))
                    gple = gain_bc(ph, "gple", g_ple[l:l + 1, :])
                    xts = [ph.sb(f"xt{i}", [128, D], F32) for i in range(7)]
                    pts = [ph.sb(f"pt{i}", [128, 256], F32) for i in range(4)]
                    xbf = [ph.sb(f"xbf{i}", [128, D + 256], BF16) for i in range(2)]
                    xTs = [ph.sb(f"xT{i}", [128, 10, 128], BF16) for i in range(2)]
                    sgts = [ph.sb(f"sgt{i}", [128, D], F32) for i in range(2)]
                    ets = [ph.sb(f"et{i}", [128, D], F32) for i in range(2)]
                    junk = ph.sb("junk", [128, D], F32)
                    sss = [ph.sb(f"ss{i}", [128, 1], F32) for i in range(2)]
                    xos = [ph.sb(f"xo{i}", [128, D], F32) for i in range(2)]
                    pGs = [ph.ps(f"pG{i}", [128, 1024]) for i in range(2)]
                    pTs = [ph.ps(f"pT{i}", [128, 16, 128], BF16) for i in range(2)]

                    def st_ld(t):
                        sl = slice(t * 128, (t + 1) * 128)
                        xt, pt = xts[t % 7], pts[t % 4]
                        cx.dma("sp", xt[:], xa[sl, :], reads=(xa.b,), writes=(xt.b,))
                        cx.dma("sp", pt[:], p_in[l, sl, :], reads=(), writes=(pt.b,))

                    def st_a(t):
                        xt, pt, xb_ = xts[t % 7], pts[t % 4], xbf[t % 2]
                        cx.op("dve", lambda e: e.tensor_copy(out=xb_[:, 0:D], in_=xt[:]), reads=(xt.b,), writes=(xb_.b,))
                        cx.op("act", lambda e: e.copy(out=xb_[:, D:D + 256], in_=pt[:]), reads=(pt.b,), writes=(xb_.b,))

                    def st_b(t):
                        xb_, xT_, pT = xbf[t % 2], xTs[t % 2], pTs[t % 2]
                        for kc in range(10):
                            cx.op("pe", lambda e, kc=kc: e.transpose(out=pT[:, kc, :], in_=xb_[:, kc * 128:(kc + 1) * 128], identity=ident[:]), inc=(kc == 9), reads=(xb_.b, ident.b), writes=(pT.b,))
                        cx.op("act", lambda e: e.copy(out=xT_[:], in_=pT[:, 0:10, :]), reads=(pT.b,), writes=(xT_.b,))

                    def st_c(t):
                        xT_, sgt, et = xTs[t % 2], sgts[t % 2], ets[t % 2]
                        pg = pGs[0]
                        for n in range(2):
                            for kc in range(8):
                                cx.op("pe", lambda e, kc=kc, n=n: e.matmul(out=pg[:, n * 512:(n + 1) * 512], lhsT=xT_[:, kc, :],
                                                                          rhs=Wpg[:, kc, n * 512:(n + 1) * 512], start=(kc == 0), stop=(kc == 7)),
                                      inc=(kc == 7), reads=(xT_.b, Wpg.b), writes=(pg.b,))
                        cx.op("act", lambda e: e.activation(out=sgt[:], in_=pg[:], func=AF.Sigmoid), reads=(pg.b,), writes=(sgt.b,))
                        pe_ = pGs[1]
                        for n in range(2):
                            for kc in range(2):
                                cx.op("pe", lambda e, kc=kc, n=n: e.matmul(out=pe_[:, n * 512:(n + 1) * 512], lhsT=xT_[:, 8 + kc, :],
                                                                          rhs=Wpp[:, kc, n * 512:(n + 1) * 512], start=(kc == 0), stop=(kc == 1)),
                                      inc=(kc == 1), reads=(xT_.b, Wpp.b), writes=(pe_.b,))
                        cx.op("dve", lambda e: e.tensor_tensor(out=et[:], in0=pe_[:], in1=sgt[:], op=ALU.mult),
                              reads=(pe_.b, sgt.b), writes=(et.b,))

                    def st_d(t):
                        sl = slice(t * 128, (t + 1) * 128)
                        xt, et, ss, xo_ = xts[t % 7], ets[t % 2], sss[t % 2], xos[t % 2]
                        rms_rstd_tokmajor(ph, et[:], (et.b,), junk, ss, D)
                        cx.op("dve", lambda e: e.scalar_tensor_tensor(out=xo_[:], in0=et[:], scalar=ss[:, 0:1], in1=gple[:],
                                                                      op0=ALU.mult, op1=ALU.mult),
                              reads=(et.b, ss.b, gple.b), writes=(xo_.b,))
                        cx.op("dve", lambda e: e.tensor_tensor(out=xo_[:], in0=xo_[:], in1=xt[:], op=ALU.add),
                              reads=(xo_.b, xt.b), writes=(xo_.b,))
                        if last:
                            cx.dma("pool", y_out[sl, :], xo_[:], reads=(xo_.b,), writes=(ybuf,))
                        else:
                            cx.dma("pool", xc[sl, :], xo_[:], reads=(xo_.b,), writes=(xc.b,))

                    skew(NT, [st_ld, st_a, st_b, st_c, st_d], [0, 2, 3, 4, 5])
        cx.barrier()
    return nc


def _consts(half):
    bf = ml_dtypes.bfloat16
    ident = np.eye(128, dtype=np.float32).astype(bf)
    invf = (np.float32(10000.0) ** (-np.arange(16, dtype=np.float32) / np.float32(16))).astype(np.float32)
    invf = np.concatenate([invf, invf]).reshape(32, 1)
    c = np.arange(256)
    th = 2.0 * np.pi * np.outer(c, c) / 256.0
    cdft = np.stack([np.cos(th) / 16.0, -np.sin(th) / 16.0]).astype(np.float32).astype(bf)
    s1 = np.arange(128)[None, :, None]
    k1 = np.arange(256)[None, None, :]
    s2 = np.arange(32)[:, None, None]
    ph = (4096 * half * k1 + 32 * s1 * k1 + s2 * k1) % 8192
    ang = 2.0 * np.pi * ph / 8192.0
    nrm = 1.0 / np.sqrt(8192.0)
    G = np.stack([np.cos(ang) * nrm, -np.sin(ang) * nrm, np.sin(ang) * nrm], axis=1).astype(np.float32).astype(bf)
    a2 = 2.0 * np.pi * np.outer(np.arange(32), np.arange(32)) / 32.0
    Er = np.kron(np.eye(4), np.cos(a2))
    Es = np.kron(np.eye(4), np.sin(a2))
    E = np.stack([Er, Es]).astype(np.float32).astype(bf)
    BIG = 1.0e7
    k = np.arange(128)[:, None]
    q = np.arange(128)[None, :]
    m0 = np.where(k >= q, 0.0, BIG)
    m2 = np.where(k <= q, 0.0, BIG)
    e0 = np.full((128, 128), BIG) if half == 0 else m0
    e3 = np.full((128, 128), BIG) if half == 1 else m2
    wmask = np.stack([m0, m2, e0, e3]).astype(np.float32)
    return dict(ident=ident, invf=invf, cdft=cdft, Gmat=G, Emat=E, wmask=wmask)


def _prep_weights(inp):
    w_in = np.asarray(inp["w_in"])
    cq = w_in[:, :, 0:384]
    ckv = w_in[:, :, 384:512]
    kr = w_in[:, :, 512:544]
    krs = np.concatenate([kr[:, :, 16:32], kr[:, :, 0:16]], axis=2)
    qc = w_in[:, :, 544:1056]
    idx = np.concatenate([np.concatenate([np.arange((0 * 4 + g) * 64, (0 * 4 + g) * 64 + 64),
                                          np.arange((1 * 4 + g) * 64, (1 * 4 + g) * 64 + 64)]) for g in range(4)])
    qcp = qc[:, :, idx]
    kc = w_in[:, :, 1056:1184]
    vc = w_in[:, :, 1184:1312]
    pad = np.zeros(kr.shape[:2] + (64,), dtype=kr.dtype)
    w_small = np.ascontiguousarray(np.concatenate([cq, ckv, pad, kr, pad, krs, qcp, kc, vc], axis=2))
    w_gates = np.ascontiguousarray(w_in[:, :, 1312:])
    w_uq = np.asarray(inp["w_uq"])
    ir, isw = [], []
    for h in range(8):
        b = h * 96
        ir += list(range(b, b + 96))
        isw += list(range(b, b + 64)) + list(range(b + 80, b + 96)) + list(range(b + 64, b + 80))
    return dict(w_small=w_small, w_gates=w_gates, w_uqr=np.ascontiguousarray(w_uq[:, :, ir]),
                w_uqs=np.ascontiguousarray(w_uq[:, :, isw]), w_ukv=np.asarray(inp["w_ukv"]),
                w_a=np.asarray(inp["w_branch_a"]), w_b=np.asarray(inp["w_branch_b"]), w_c=np.asarray(inp["w_branch_c"]),
                w_out=np.asarray(inp["w_out"]), w_fg=np.asarray(inp["w_ffn_gate"]), w_fu=np.asarray(inp["w_ffn_up"]),
                w_fd=np.asarray(inp["w_ffn_down"]), w_pp=np.asarray(inp["w_ple_proj"]), w_pg=np.asarray(inp["w_ple_gate"]),
                g_mix_pre=np.asarray(inp["norm_mix_pre"]), g_q=np.asarray(inp["mla_q_norm"]), g_kv=np.asarray(inp["mla_kv_norm"]),
                g_mix_post=np.asarray(inp["norm_mix_post"]), g_ffn_pre=np.asarray(inp["norm_ffn_pre"]),
                g_ffn_post=np.asarray(inp["norm_ffn_post"]), g_ple=np.asarray(inp["norm_ple"]), sink=np.asarray(inp["gqa_sink"]))


def make_in_maps(inp):
    shared = _prep_weights(inp)
    shared = {k: np.ascontiguousarray(v, dtype=np.float32) for k, v in shared.items()}
    x = np.asarray(inp["x"]); p = np.asarray(inp["p"]); pos = np.asarray(inp["positions"])
    maps = []
    for c in range(8):
        b, half = c // 2, c % 2
        s0 = half * TOK
        m = dict(shared)
        m.update(_consts(half))
        m["x"] = np.ascontiguousarray(x[b, s0:s0 + TOK, :], dtype=np.float32)
        m["p"] = np.ascontiguousarray(p[:, b, s0:s0 + TOK, :], dtype=np.float32)
        m["pos"] = np.ascontiguousarray(pos[b, s0:s0 + TOK].reshape(1, TOK), dtype=np.int32)
        px = np.zeros(TOK + 256, dtype=np.int32)
        lo, hi = s0 - 128, s0 + TOK + 128
        slo, shi = max(lo, 0), min(hi, SEQ)
        px[slo - lo:shi - lo] = pos[b, slo:shi]
        m["posx"] = np.ascontiguousarray(px.reshape(34, 128))
        maps.append(m)
    return maps


_NC_CACHE = {}


def kernel(**inputs):
    if "nc" not in _NC_CACHE:
        _NC_CACHE["nc"] = build_program()
    nc = _NC_CACHE["nc"]
    maps = make_in_maps(inputs)
    res = run_bass_kernel_spmd(nc, maps, core_ids=list(range(8)))
    out = np.empty((NB, SEQ, D), dtype=np.float32)
    for c in range(8):
        b, half = c // 2, c % 2
        out[b, half * TOK:(half + 1) * TOK, :] = res.results[c]["y"]
    return out
```

```python
import numpy as np
import ml_dtypes
from contextlib import ExitStack
import concourse.bass as bass
import concourse.mybir as mybir
from concourse.bass_utils import run_bass_kernel_spmd

F32 = mybir.dt.float32
BF16 = mybir.dt.bfloat16
I32 = mybir.dt.int32
ALU = mybir.AluOpType
AF = mybir.ActivationFunctionType

D = 1024
SEQ = 8192
NB = 4
DEPTH = 2
TOK = 4096
NT = TOK // 128
NCH = TOK // 512
FFN = 2816
NF = FFN // 128
EPS = 1e-6
HQ = 8
SLOPES = [2.0 ** (-8.0 * (h + 1.0) / 8.0) for h in range(8)]
MLA_SCALE = 96 ** -0.5
NSMALL = 1472
O_CQ, O_CKV, O_KRA, O_KRB, O_QC, O_KC, O_VC = 0, 384, 512, 608, 704, 1216, 1344
PAIRS = [[0, 1], [2, 3], [4, 5], [6, 7]]
DBG = {}


def dbg(k):
    return k not in DBG.get('skip', ())

XROWS = 176


class Buf:
    __slots__ = ("name", "w", "r", "dsem", "px")

    def __init__(self, name):
        self.name = name
        self.w = None
        self.r = {}
        self.dsem = None
        self.px = False


class DSem:
    __slots__ = ("sem", "count", "key", "lazy")

    def __init__(self, sem, key):
        self.sem = sem
        self.count = 0
        self.key = key
        self.lazy = False


class Tl:
    __slots__ = ("t", "b", "bs")

    def __init__(self, t, name, nsub=0):
        self.t = t
        self.b = Buf(name)
        self.bs = [Buf(f"{name}.{i}") for i in range(nsub)]

    def __getitem__(self, k):
        return self.t[k]


class Ctx:
    def __init__(self, nc, es):
        self.nc = nc
        self.es = es
        self.eng = {"pe": nc.tensor, "act": nc.scalar, "dve": nc.vector, "pool": nc.gpsimd, "sp": nc.sync}
        self.psem = {}
        for k in self.eng:
            self.psem[k] = es.enter_context(nc.semaphore("p_" + k))
        self.cnt = {k: 0 for k in self.eng}
        self.known = {k: {} for k in self.eng}
        self.pending = {k: False for k in self.eng}
        self.dsems = []
        self.free_dsems = []
        self.n_ins = 0

    def new_dsem(self):
        if self.free_dsems:
            return self.free_dsems.pop()
        s = self.es.enter_context(self.nc.semaphore(f"d{len(self.dsems)}"))
        d = DSem(s, f"d{len(self.dsems)}")
        self.dsems.append(d)
        return d

    def release_dsem(self, d):
        self.free_dsems.append(d)

    def _wait(self, e, ev):
        key, sem, val = ev
        if self.known[e].get(key, 0) >= val:
            return
        self.eng[e].wait_ge(sem, val)
        self.known[e][key] = val

    def _deps(self, e, reads, writes, skip_key=None):
        for b in reads:
            if b.w is not None and b.w[0] != skip_key:
                self._wait(e, b.w)
            if b.px:
                for ev in b.r.values():
                    if ev[0] != e:
                        self._wait(e, ev)
        for b in writes:
            if b.w is not None and b.w[0] != skip_key:
                self._wait(e, b.w)
            for ev in b.r.values():
                if ev[0] != skip_key:
                    self._wait(e, ev)

    def op(self, e, fn, reads=(), writes=(), inc=True):
        skip = "pe" if e == "pe" else None
        self._deps(e, reads, writes, skip)
        ins = fn(self.eng[e])
        if inc:
            self.cnt[e] += 1
            ins.then_inc(self.psem[e], 1)
            ev = (e, self.psem[e], self.cnt[e])
            self.pending[e] = False
        else:
            ev = (e, self.psem[e], self.cnt[e] + 1)
            self.pending[e] = True
        for b in reads:
            b.r[e] = ev
        for b in writes:
            b.w = ev
            b.r = {}
        self.n_ins += 1
        return ins

    def dma(self, q, out, in_, reads=(), writes=(), **kw):
        b0 = writes[0]
        if b0.dsem is None:
            b0.dsem = self.new_dsem()
        ds = b0.dsem
        self._deps(q, reads, writes, ds.key)
        ins = self.eng[q].dma_start(out=out, in_=in_, **kw)
        ds.count += 16
        ins.then_inc(ds.sem, 16)
        ev = (ds.key, ds.sem, ds.count)
        for b in reads:
            b.r[ds.key] = ev
        for b in writes:
            b.w = ev
            b.r = {}
        self.n_ins += 1
        return ins

    def cc(self, kind, op, in_ap, out_ap, reads, writes):
        b0 = writes[0]
        if b0.dsem is None:
            b0.dsem = self.new_dsem()
        ds = b0.dsem
        ds.lazy = True
        self._deps("pool", reads, writes, None)
        ins = self.nc.gpsimd.collective_compute(kind, op, replica_groups=PAIRS, ins=[in_ap], outs=[out_ap])
        ds.count += 1
        ins.then_inc(ds.sem, 1)
        ev = (ds.key, ds.sem, ds.count)
        for b in reads:
            b.r[ds.key] = ev
        for b in writes:
            b.w = ev
            b.r = {}
        return ins

    def barrier(self):
        assert not any(self.pending.values()), self.pending
        for e in self.eng:
            for e2 in self.eng:
                if e2 != e and self.cnt[e2] > 0:
                    self._wait(e, (e2, self.psem[e2], self.cnt[e2]))
            for d in self.dsems:
                if d.count > 0 and not d.lazy:
                    self._wait(e, (d.key, d.sem, d.count))


class Phase:
    uid = 0

    def __init__(self, cx):
        self.cx = cx
        self.es = ExitStack()
        self.tiles = []

    def __enter__(self):
        self.es.__enter__()
        return self

    def __exit__(self, *a):
        self.cx.barrier()
        for t in self.tiles:
            for b in [t.b] + t.bs:
                if b.dsem is not None:
                    self.cx.release_dsem(b.dsem)
                    b.dsem = None
        return self.es.__exit__(*a)

    def sb(self, name, shape, dtype, nsub=0):
        Phase.uid += 1
        name = f"{name}_u{Phase.uid}"
        t = self.es.enter_context(self.cx.nc.sbuf_tensor(name, list(shape), dtype))
        tl = Tl(t, name, nsub)
        self.tiles.append(tl)
        return tl

    def ps(self, name, shape, dtype=F32):
        Phase.uid += 1
        name = f"{name}_u{Phase.uid}"
        t = self.es.enter_context(self.cx.nc.psum_tensor(name, list(shape), dtype))
        tl = Tl(t, name)
        tl.b.px = True
        self.tiles.append(tl)
        return tl


def skew(n, stages, lags=None):
    if lags is None:
        lags = list(range(len(stages)))
    for step in range(n + max(lags)):
        for st, lg in zip(stages, lags):
            t = step - lg
            if 0 <= t < n:
                st(t)


class Rot:
    def __init__(self, items):
        self.items = items
        self.i = 0

    def next(self):
        x = self.items[self.i % len(self.items)]
        self.i += 1
        return x


def build_program(dump=(), phases=None, nlayers=DEPTH):
    nc = bass.Bass("TRN2", target_bir_lowering=False)
    es = ExitStack()
    with es:
        cx = Ctx(nc, es)

        def din(name, shape, dt=F32):
            return nc.dram_tensor(name, list(shape), dt, kind="ExternalInput").ap()

        def dscr(name, shape, dt):
            kind = "ExternalOutput" if name in dump else "Internal"
            return Tl(nc.dram_tensor(name, list(shape), dt, kind=kind).ap(), name)

        x_in = din("x", [TOK, D])
        p_in = din("p", [DEPTH, TOK, 256])
        pos_in = din("pos", [1, TOK], I32)
        posx_in = din("posx", [34, 128], I32)
        w_small = din("w_small", [DEPTH, D, NSMALL])
        w_gates = din("w_gates", [DEPTH, D, 3 * D])
        w_uqr = din("w_uqr", [DEPTH, 384, 768])
        w_uqs = din("w_uqs", [DEPTH, 384, 768])
        w_ukv = din("w_ukv", [DEPTH, 128, 1024])
        w_a = din("w_a", [DEPTH, 512, D])
        w_b = din("w_b", [DEPTH, D, D])
        w_c = din("w_c", [DEPTH, 512, D])
        w_out = din("w_out", [DEPTH, D, D])
        w_fg = din("w_fg", [DEPTH, D, FFN])
        w_fu = din("w_fu", [DEPTH, D, FFN])
        w_fd = din("w_fd", [DEPTH, FFN, D])
        w_pp = din("w_pp", [DEPTH, 256, D])
        w_pg = din("w_pg", [DEPTH, D, D])
        g_mix_pre = din("g_mix_pre", [DEPTH, D])
        g_q = din("g_q", [DEPTH, 384])
        g_kv = din("g_kv", [DEPTH, 128])
        g_mix_post = din("g_mix_post", [DEPTH, D])
        g_ffn_pre = din("g_ffn_pre", [DEPTH, D])
        g_ffn_post = din("g_ffn_post", [DEPTH, D])
        g_ple = din("g_ple", [DEPTH, D])
        sink_in = din("sink", [DEPTH, 8])
        ident_in = din("ident", [128, 128], BF16)
        invf_in = din("invf", [32, 1])
        cdft_in = din("cdft", [2, 256, 256], BF16)
        G_in = din("Gmat", [32, 3, 128, 256], BF16)
        E_in = din("Emat", [2, 128, 128], BF16)
        wmask_in = din("wmask", [4, 128, 128])
        y_out = nc.dram_tensor("y", [TOK, D], F32, kind="ExternalOutput").ap()
        ybuf = Buf("y")

        xa = dscr("xa", [TOK, D], F32)
        xb = dscr("xb", [TOK, D], F32)
        xc = dscr("xc", [TOK, D], F32)
        act_d = dscr("act_d", [FFN, TOK], BF16)
        hT_d = dscr("hT_d", [D, TOK], BF16)
        Yr_d = dscr("Yr_d", [TOK, D], BF16)
        Yi_d = dscr("Yi_d", [TOK, D], BF16)
        Yn_d = dscr("Yn_d", [TOK, D], BF16)
        cqT_d = dscr("cqT_d", [384, TOK], BF16)
        xin_d = dscr("xin_d", [XROWS, TOK], BF16)
        xout_d = dscr("xout_d", [2 * XROWS, TOK], BF16)
        qcT_d = dscr("qcT_d", [128, 4, TOK], BF16)
        kcT_d = dscr("kcT_d", [128, TOK + 256], BF16)
        vc_d = dscr("vc_d", [TOK + 256, 128], BF16)
        oaT_d = dscr("oaT_d", [512, TOK], BF16)
        ocT_d = dscr("ocT_d", [512, TOK], BF16)
        Zr_d = dscr("Zr_d", [SEQ, D], BF16)
        Zi_d = dscr("Zi_d", [SEQ, D], BF16)
        Bp_d = dscr("Bp_d", [SEQ, D], F32)
        Brs_d = dscr("Brs_d", [TOK, D], F32)
        rope_d = dscr("rope_d", [2, 32, TOK], F32)
        xin_buf = Buf("x_in")

        def want(name):
            return phases is None or name in phases

        CAST_ENG = ["dve", "act"]
        cast_i = [0]
        def load_cast(ph, dst_ap_fn, src_ap_fn, nparts, ncols, stage, colchunk=2048):
            for c0 in range(0, ncols, colchunk):
                c1 = min(ncols, c0 + colchunk)
                st = stage.next()
                cx.dma("sp", st[0:nparts, 0:c1 - c0], src_ap_fn(c0, c1), reads=(), writes=(st.b,))
                dst, dbuf = dst_ap_fn(c0, c1)
                ce = CAST_ENG[cast_i[0] % len(CAST_ENG)]
                cast_i[0] += 1
                if ce == "act":
                    cx.op("act", lambda e, dst=dst, st=st, n=c1 - c0: e.copy(out=dst, in_=st[0:nparts, 0:n]),
                          reads=(st.b,), writes=(dbuf,))
                else:
                    cx.op(ce, lambda e, dst=dst, st=st, n=c1 - c0: e.tensor_copy(out=dst, in_=st[0:nparts, 0:n]),
                          reads=(st.b,), writes=(dbuf,))

        def load_w_kc(ph, wt, src, nkc, ncols, stage):
            for kc in range(nkc):
                load_cast(ph, lambda c0, c1, kc=kc: (wt[:, kc, c0:c1], wt.b),
                          lambda c0, c1, kc=kc: src[kc * 128:(kc + 1) * 128, c0:c1], 128, ncols, stage)

        def mk_stage(ph, n=3):
            return Rot([ph.sb(f"wstage{i}", [128, 2048], F32) for i in range(n)])

        def gain_bc(ph, name, src_row):
            n = src_row.shape[-1]
            t = ph.sb(name, [128, n], F32)
            cx.dma("sp", t[:], src_row.partition_broadcast(128), reads=(), writes=(t.b,))
            return t

        def rms_rstd_tokmajor(ph, src_ap, src_bufs, junk, ss, n):
            cx.op("act", lambda e: e.activation(out=junk[:, 0:n], in_=src_ap, func=AF.Square, accum_out=ss[:, 0:1]),
                  reads=src_bufs, writes=(junk.b, ss.b))
            cx.op("dve", lambda e: e.tensor_scalar(out=ss[:, 0:1], in0=ss[:, 0:1], scalar1=1.0 / n, scalar2=EPS,
                                                   op0=ALU.mult, op1=ALU.add), reads=(ss.b,), writes=(ss.b,))
            cx.op("act", lambda e: e.activation(out=ss[:, 0:1], in_=ss[:, 0:1], func=AF.Sqrt), reads=(ss.b,), writes=(ss.b,))
            cx.op("dve", lambda e: e.reciprocal(out=ss[:, 0:1], in_=ss[:, 0:1]), reads=(ss.b,), writes=(ss.b,))

        if want("P0"):
            with Phase(cx) as ph:
                posi = ph.sb("posi", [32, TOK], I32)
                posf = ph.sb("posf", [32, TOK], F32)
                ang = ph.sb("ang", [32, TOK], F32)
                nn = ph.sb("nn", [32, TOK], F32)
                tab = ph.sb("tab", [32, TOK], F32)
                invf = ph.sb("invf", [32, 1], F32)
                cx.dma("sp", posi[:], pos_in.partition_broadcast(32), reads=(), writes=(posi.b,))
                cx.dma("sp", invf[:], invf_in, reads=(), writes=(invf.b,))
                cx.op("dve", lambda e: e.tensor_copy(out=posf[:], in_=posi[:]), reads=(posi.b,), writes=(posf.b,))
                TWO_PI = 2.0 * np.pi
                C1 = 6.28125
                C2 = TWO_PI - C1
                MAGIC = 12582912.0
                for which in range(2):
                    shift = (np.pi / 2.0) if which == 0 else 0.0
                    cx.op("dve", lambda e, shift=shift: e.tensor_scalar(out=ang[:], in0=posf[:], scalar1=invf[:, 0:1], scalar2=shift,
                                                                       op0=ALU.mult, op1=ALU.add), reads=(posf.b, invf.b), writes=(ang.b,))
                    cx.op("dve", lambda e: e.tensor_scalar(out=nn[:], in0=ang[:], scalar1=1.0 / TWO_PI, scalar2=MAGIC,
                                                           op0=ALU.mult, op1=ALU.add), reads=(ang.b,), writes=(nn.b,))
                    cx.op("dve", lambda e: e.tensor_scalar(out=nn[:], in0=nn[:], scalar1=MAGIC, scalar2=None,
                                                           op0=ALU.subtract), reads=(nn.b,), writes=(nn.b,))
                    cx.op("dve", lambda e: e.scalar_tensor_tensor(out=ang[:], in0=nn[:], scalar=-C1, in1=ang[:],
                                                                  op0=ALU.mult, op1=ALU.add), reads=(nn.b, ang.b), writes=(ang.b,))
                    cx.op("dve", lambda e: e.scalar_tensor_tensor(out=ang[:], in0=nn[:], scalar=-C2, in1=ang[:],
                                                                  op0=ALU.mult, op1=ALU.add), reads=(nn.b, ang.b), writes=(ang.b,))
                    cx.op("dve", lambda e: e.tensor_scalar(out=ang[:], in0=ang[:], scalar1=3.1415925, scalar2=-3.1415925,
                                                           op0=ALU.min, op1=ALU.max), reads=(ang.b,), writes=(ang.b,))
                    cx.op("act", lambda e: e.activation(out=tab[:], in_=ang[:], func=AF.Sin), reads=(ang.b,), writes=(tab.b,))
                    if which == 1:
                        cx.op("dve", lambda e: e.tensor_scalar(out=tab[0:16, :], in0=tab[0:16, :], scalar1=-1.0, scalar2=None,
                                                               op0=ALU.mult), reads=(tab.b,), writes=(tab.b,))
                    cx.dma("pool", rope_d[which], tab[:], reads=(tab.b,), writes=(rope_d.b,))

        for l in range(nlayers):
            x_src, x_src_b = (x_in, xin_buf) if l == 0 else (xc.t, xc.b)
            if want("P1"):
                with Phase(cx) as ph:
                    stage = mk_stage(ph)
                    Ws = ph.sb("Ws", [128, 8, NSMALL], BF16)
                    load_w_kc(ph, Ws, w_small[l], 8, NSMALL, stage)
                    Wb = ph.sb("Wb", [128, 8, D], BF16)
                    load_w_kc(ph, Wb, w_b[l], 8, D, stage)
                    cd = ph.sb("cd", [128, 2, 2, 256], BF16)
                    cx.dma("sp", cd[:], cdft_in.rearrange("w (kr p) c -> p w kr c", p=128), reads=(), writes=(cd.b,))
                    Mri = ph.sb("Mri", [128, 2, 8, D], BF16)
                    ident = ph.sb("ident", [128, 128], BF16)
                    cx.dma("sp", ident[:], ident_in, reads=(), writes=(ident.b,))
                    ones = ph.sb("ones", [128, 128], BF16)
                    cx.op("dve", lambda e: e.memset(ones[:], 1.0), writes=(ones.b,))
                    gpre = gain_bc(ph, "gpre", g_mix_pre[l:l + 1, :])
                    gq = ph.sb("gq", [128, 3], F32)
                    cx.dma("sp", gq[:], g_q[l].rearrange("(c p) -> p c", p=128), reads=(), writes=(gq.b,), allow_slow_non_contiguous=True)
                    gkv = ph.sb("gkv", [128, 1], F32)
                    cx.dma("sp", gkv[:], g_kv[l].rearrange("(c p) -> p c", p=128), reads=(), writes=(gkv.b,), allow_slow_non_contiguous=True)
                    ropet = ph.sb("ropet", [96, 2, TOK], F32)
                    cx.dma("sp", ropet[64:96], rope_d.t.rearrange("w p t -> p w t"), reads=(rope_d.b,), writes=(ropet.b,))

                    pF = Rot([ph.ps(f"pF{i}", [128, 512]) for i in range(4)])
                    pY = ph.ps("pY", [128, 1024])
                    pT = ph.ps("pT", [128, 8, 128], BF16)
                    pT2 = ph.ps("pT2", [128, 8, 128], BF16)

                    for which in range(2):
                        for m in range(8):
                            gr, mc = m // 2, m % 2
                            for n in range(2):
                                for kr in range(2):
                                    cx.op("pe", lambda e, which=which, kr=kr, mc=mc, gr=gr, n=n: e.matmul(
                                        out=pY[:, n * 512:(n + 1) * 512], lhsT=cd[:, which, kr, mc * 128:(mc + 1) * 128],
                                        rhs=Wb[:, 2 * gr + kr, n * 512:(n + 1) * 512], start=(kr == 0), stop=(kr == 1)),
                                        inc=(kr == 1), reads=(cd.b, Wb.b), writes=(pY.b,))
                            cx.op("act", lambda e, which=which, m=m: e.copy(out=Mri[:, which, m, :], in_=pY[:]),
                                  reads=(pY.b,), writes=(Mri.b,))

                    xts = Rot([ph.sb(f"xt{i}", [128, D], F32) for i in range(3)])
                    junk = ph.sb("junk", [128, D], F32)
                    sss = Rot([ph.sb(f"ss{i}", [128, 1], F32) for i in range(3)])
                    hbs = Rot([ph.sb(f"hb{i}", [128, D], BF16) for i in range(2)])
                    hTs = Rot([ph.sb(f"hT{i}", [128, 8, 512], BF16, nsub=4) for i in range(2)])
                    ysb = Rot([ph.sb(f"ysb{i}", [128, D], BF16) for i in range(3)])
                    vst = Rot([ph.sb(f"vst{i}", [128, 4, 128], BF16) for i in range(2)])
                    sqs = [ph.sb(f"sq{i}", [128, 512], BF16) for i in range(3)]
                    msb = ph.sb("msb", [128, 512], F32)
                    cqst = Rot([ph.sb(f"cqst{i}", [128, 3, 512], BF16) for i in range(2)])
                    fst = Rot([ph.sb(f"fst{i}", [128, 512], BF16) for i in range(3)])
                    r1 = ph.sb("r1", [96, 512], F32)
                    r2 = ph.sb("r2", [96, 512], F32)
                    krst = Rot([ph.sb(f"krst{i}", [96, 512], BF16) for i in range(2)])

                    pTs = [pT, pT2]
                    hT_l = hTs.items
                    hb_l = hbs.items
                    chunk_pV = {}

                    xt_l = xts.items

                    def st_ld(t):
                        xt = xt_l[t % 3]
                        cx.dma("sp", xt[:], x_src[t * 128:(t + 1) * 128, :], reads=(x_src_b,), writes=(xt.b,))

                    def st_a(t):
                        xt = xt_l[t % 3]
                        ss = sss.next()
                        hb = hb_l[t % 2]
                        rms_rstd_tokmajor(ph, xt[:], (xt.b,), junk, ss, D)
                        cx.op("dve", lambda e: e.scalar_tensor_tensor(
                            out=hb[:], in0=xt[:], scalar=ss[:, 0:1], in1=gpre[:], op0=ALU.mult, op1=ALU.mult),
                            reads=(xt.b, ss.b, gpre.b), writes=(hb.b,))

                    def st_b(t):
                        ci, tt = t // 4, t % 4
                        hb, hT, pT_ = hb_l[t % 2], hT_l[ci % 2], pTs[t % 2]
                        for kc in range(8):
                            cx.op("pe", lambda e, kc=kc: e.transpose(out=pT_[:, kc, :], in_=hb[:, kc * 128:(kc + 1) * 128], identity=ident[:]),
                                  inc=(kc == 7), reads=(hb.b, ident.b), writes=(pT_.b,))
                        cx.op("act", lambda e: e.copy(out=hT[:, :, tt * 128:(tt + 1) * 128], in_=pT_[:]),
                              reads=(pT_.b,), writes=(hT.bs[tt],))

                    def st_c(t):
                        ci, tt = t // 4, t % 4
                        hT = hT_l[ci % 2]
                        if tt == 0:
                            chunk_pV[ci] = pF.next()
                        pV = chunk_pV[ci]
                        for kc in range(8):
                            cx.op("pe", lambda e, kc=kc: e.matmul(
                                out=pV[:, tt * 128:(tt + 1) * 128], lhsT=hT[:, kc, tt * 128:(tt + 1) * 128],
                                rhs=Ws[:, kc, O_VC:O_VC + 128], start=(kc == 0), stop=(kc == 7)),
                                inc=(kc == 7), reads=(hT.bs[tt], Ws.b), writes=(pV.b,))
                        for which in range(2):
                            for n in range(2):
                                for kc in range(8):
                                    cx.op("pe", lambda e, kc=kc, n=n, which=which: e.matmul(
                                        out=pY[:, n * 512:(n + 1) * 512], lhsT=hT[:, kc, tt * 128:(tt + 1) * 128],
                                        rhs=Mri[:, which, kc, n * 512:(n + 1) * 512], start=(kc == 0), stop=(kc == 7)),
                                        inc=(kc == 7), reads=(hT.bs[tt], Mri.b), writes=(pY.b,))
                            if which == 0:
                                ys = ysb.next()
                                cx.op("act", lambda e, ys=ys: e.copy(out=ys[:], in_=pY[:]), reads=(pY.b,), writes=(ys.b,))
                                cx.dma("pool", Yr_d[t * 128:(t + 1) * 128, :], ys[:], reads=(ys.b,), writes=(Yr_d.b,))
                            else:
                                ys = ysb.next()
                                cx.op("dve", lambda e, ys=ys: e.tensor_copy(out=ys[:], in_=pY[:]), reads=(pY.b,), writes=(ys.b,))
                                cx.dma("pool", Yi_d[t * 128:(t + 1) * 128, :], ys[:], reads=(ys.b,), writes=(Yi_d.b,))
                        if tt == 3:
                            chunk_level(ci, hT, pV)

                    def chunk_level(ci, hT, pV):
                        tok0 = ci * 512
                        for _once in (0,):
                            vs = vst.next()
                            cx.op("dve", lambda e, vs=vs, pV=pV: e.tensor_copy(out=vs[:], in_=pV[:].rearrange("p (t c) -> p t c", t=4)),
                                  reads=(pV.b,), writes=(vs.b,))
                            cx.dma("pool", vc_d[128 + tok0:128 + tok0 + 512, :].rearrange("(t p) c -> p t c", p=128), vs[:],
                                   reads=(vs.b,), writes=(vc_d.b,))
                            cx.dma("pool", hT_d.t.rearrange("(kc p) t -> p kc t", p=128)[:, :, tok0:tok0 + 512], hT[:],
                                   reads=tuple(hT.bs), writes=(hT_d.b,))

                            def fm_proj(col0, m, pbank, hT=hT):
                                for kc in range(8):
                                    cx.op("pe", lambda e, kc=kc: e.matmul(out=pbank[0:m, :], lhsT=Ws[:, kc, col0:col0 + m], rhs=hT[:, kc, :],
                                                                           start=(kc == 0), stop=(kc == 7)),
                                          inc=(kc == 7), reads=tuple(hT.bs) + (Ws.b,), writes=(pbank.b,))

                            def fm_norm(banks, nfeat, gcol, dst_fn):
                                pS = pF.next()
                                for c, bk in enumerate(banks):
                                    cx.op("act", lambda e, c=c, bk=bk: e.activation(out=sqs[c][:], in_=bk[:], func=AF.Square),
                                          reads=(bk.b,), writes=(sqs[c].b,))
                                for c in range(len(banks)):
                                    cx.op("pe", lambda e, c=c: e.matmul(out=pS[:], lhsT=ones[:], rhs=sqs[c][:], start=(c == 0),
                                                                        stop=(c == len(banks) - 1)), reads=(ones.b, sqs[c].b), writes=(pS.b,))
                                cx.op("dve", lambda e: e.tensor_scalar(out=msb[:], in0=pS[:], scalar1=1.0 / nfeat, scalar2=EPS,
                                                                       op0=ALU.mult, op1=ALU.add), reads=(pS.b,), writes=(msb.b,))
                                cx.op("act", lambda e: e.activation(out=msb[:], in_=msb[:], func=AF.Sqrt), reads=(msb.b,), writes=(msb.b,))
                                cx.op("dve", lambda e: e.reciprocal(out=msb[:], in_=msb[:]), reads=(msb.b,), writes=(msb.b,))
                                for c, bk in enumerate(banks):
                                    dst, dbuf = dst_fn(c)
                                    cx.op("dve", lambda e, c=c, bk=bk, dst=dst: e.scalar_tensor_tensor(
                                        out=dst, in0=bk[:], scalar=gcol[:, c:c + 1], in1=msb[:], op0=ALU.mult, op1=ALU.mult),
                                        reads=(bk.b, gcol.b, msb.b), writes=(dbuf,))

                            banks = [pF.next() for _ in range(3)]
                            for c in range(3):
                                fm_proj(O_CQ + c * 128, 128, banks[c])
                            cq = cqst.next()
                            fm_norm(banks, 384, gq, lambda c: (cq[:, c, :], cq.b))
                            cx.dma("pool", cqT_d.t.rearrange("(c p) t -> p c t", p=128)[:, :, tok0:tok0 + 512], cq[:],
                                   reads=(cq.b,), writes=(cqT_d.b,))
                            bk = pF.next()
                            fm_proj(O_CKV, 128, bk)
                            f1 = fst.next()
                            fm_norm([bk], 128, gkv, lambda c: (f1[:], f1.b))
                            cx.dma("pool", xin_d[0:128, tok0:tok0 + 512], f1[:], reads=(f1.b,), writes=(xin_d.b,))
                            bA = pF.next()
                            bB = pF.next()
                            fm_proj(O_KRA, 96, bA)
                            fm_proj(O_KRB, 96, bB)
                            cx.op("dve", lambda e, bA=bA: e.tensor_tensor(out=r1[64:96, :], in0=bA[64:96, :], in1=ropet[64:96, 0, tok0:tok0 + 512],
                                                                          op=ALU.mult), reads=(bA.b, ropet.b), writes=(r1.b,))
                            cx.op("dve", lambda e, bB=bB: e.tensor_tensor(out=r2[64:96, :], in0=bB[64:96, :], in1=ropet[64:96, 1, tok0:tok0 + 512],
                                                                          op=ALU.mult), reads=(bB.b, ropet.b), writes=(r2.b,))
                            kr = krst.next()
                            cx.op("dve", lambda e, kr=kr: e.tensor_tensor(out=kr[64:96, :], in0=r1[64:96, :], in1=r2[64:96, :], op=ALU.add),
                                  reads=(r1.b, r2.b), writes=(kr.b,))
                            cx.dma("pool", xin_d[128:160, tok0:tok0 + 512], kr[64:96, :], reads=(kr.b,), writes=(xin_d.b,))
                            for g in range(4):
                                bk = pF.next()
                                fm_proj(O_QC + g * 128, 128, bk)
                                f1 = fst.next()
                                cx.op("act", lambda e, f1=f1, bk=bk: e.mul(f1[:], bk[:], 0.125),
                                      reads=(bk.b,), writes=(f1.b,))
                                cx.dma("pool", qcT_d[:, g, tok0:tok0 + 512], f1[:], reads=(f1.b,), writes=(qcT_d.b,))
                            bk = pF.next()
                            fm_proj(O_KC, 128, bk)
                            f1 = fst.next()
                            cx.op("act", lambda e, f1=f1, bk=bk: e.copy(out=f1[:], in_=bk[:]), reads=(bk.b,), writes=(f1.b,))
                            cx.dma("pool", kcT_d[:, 128 + tok0:128 + tok0 + 512], f1[:], reads=(f1.b,), writes=(kcT_d.b,))

                    skew(NT, [st_ld, st_a, st_b, st_c], [0, 2, 3, 4])

                    if dbg('misc'):
                        misc = xin_d.t[160:176, :].rearrange("r (q c) -> (r q) c", c=128)
                        cx.dma("pool", misc[0:128, :], kcT_d[:, 128:256], reads=(kcT_d.b,), writes=(xin_d.b,))
                        cx.dma("pool", misc[128:256, :], kcT_d[:, TOK:TOK + 128], reads=(kcT_d.b,), writes=(xin_d.b,))
                        cx.dma("pool", misc[256:384, :], vc_d[128:256, :], reads=(vc_d.b,), writes=(xin_d.b,))
                        cx.dma("pool", misc[384:512, :], vc_d[TOK:TOK + 128, :], reads=(vc_d.b,), writes=(xin_d.b,))

            if want("P2"):
                cx.cc("AllGather", ALU.bypass, xin_d.t, xout_d.t, reads=(xin_d.b,), writes=(xout_d.b,))

            if want("P5"):
                with Phase(cx) as ph:
                    Gs = Rot([ph.sb(f"G{i}", [128, 3, 256], BF16) for i in range(2)])
                    Yt = Rot([ph.sb(f"Yt{i}", [128, 2, D], BF16) for i in range(2)])
                    zst = Rot([ph.sb(f"zst{i}", [128, D], BF16) for i in range(4)])
                    pZ = Rot([ph.ps(f"pZ{i}", [128, 1024]) for i in range(4)])
                    Yv = [y.t.rearrange("(s1 s2) c -> s2 s1 c", s2=32) for y in (Yr_d, Yi_d)]
                    Zv = [z.t.rearrange("(k1 s2) c -> s2 k1 c", s2=32) for z in (Zr_d, Zi_d)]
                    for s2 in range(32):
                        G = Gs.next()
                        Y = Yt.next()
                        cx.dma("sp", G[:], G_in[s2].rearrange("w p c -> p w c"), reads=(), writes=(G.b,))
                        for w in range(2):
                            cx.dma("sp", Y[:, w, :], Yv[w][s2], reads=(Yr_d.b, Yi_d.b), writes=(Y.b,))
                        for m in range(2):
                            for part in range(2):
                                pz = pZ.next()
                                srcs = ((0, 0), (2, 1)) if part == 0 else ((0, 1), (1, 0))
                                for n in range(2):
                                    for i, (gw, yw) in enumerate(srcs):
                                        cx.op("pe", lambda e, pz=pz, G=G, Y=Y, gw=gw, yw=yw, n=n, i=i, m=m: e.matmul(
                                            out=pz[:, n * 512:(n + 1) * 512], lhsT=G[:, gw, m * 128:(m + 1) * 128],
                                            rhs=Y[:, yw, n * 512:(n + 1) * 512], start=(i == 0), stop=(i == 1)),
                                            inc=(i == 1), reads=(G.b, Y.b), writes=(pz.b,))
                                z = zst.next()
                                eng = "act" if part == 0 else "dve"
                                if eng == "act":
                                    cx.op("act", lambda e, z=z, pz=pz: e.copy(out=z[:], in_=pz[:]), reads=(pz.b,), writes=(z.b,))
                                else:
                                    cx.op("dve", lambda e, z=z, pz=pz: e.tensor_copy(out=z[:], in_=pz[:]), reads=(pz.b,), writes=(z.b,))
                                cx.dma("pool", Zv[part][s2, m * 128:(m + 1) * 128, :], z[:], reads=(z.b,), writes=((Zr_d, Zi_d)[part].b,))
                    cx.barrier()
                    Eb = ph.sb("Eb", [128, 2, 128], BF16)
                    cx.dma("sp", Eb[:], E_in.rearrange("w p c -> p w c"), reads=(), writes=(Eb.b,))
                    Zt = Rot([ph.sb(f"Zt{i}", [128, 2, D], BF16) for i in range(2)])
                    bst = Rot([ph.sb(f"bst{i}", [128, D], F32) for i in range(3)])
                    Bv = Bp_d.t.rearrange("(k2 r) c -> r k2 c", r=256)
                    for jj in range(64):
                        Z = Zt.next()
                        cx.dma("sp", Z[:, 0, :], Zr_d[jj * 128:(jj + 1) * 128, :], reads=(Zr_d.b,), writes=(Z.b,))
                        cx.dma("sp", Z[:, 1, :], Zi_d[jj * 128:(jj + 1) * 128, :], reads=(Zi_d.b,), writes=(Z.b,))
                        pz = pZ.next()
                        for n in range(2):
                            for w in range(2):
                                cx.op("pe", lambda e, pz=pz, Z=Z, n=n, w=w: e.matmul(out=pz[:, n * 512:(n + 1) * 512], lhsT=Eb[:, w, :],
                                                                                  rhs=Z[:, w, n * 512:(n + 1) * 512], start=(w == 0), stop=(w == 1)),
                                      inc=(w == 1), reads=(Eb.b, Z.b), writes=(pz.b,))
                        bs_ = bst.next()
                        if jj % 2 == 0:
                            cx.op("act", lambda e, bs_=bs_, pz=pz: e.copy(out=bs_[:], in_=pz[:]), reads=(pz.b,), writes=(bs_.b,))
                        else:
                            cx.op("dve", lambda e, bs_=bs_, pz=pz: e.tensor_copy(out=bs_[:], in_=pz[:]), reads=(pz.b,), writes=(bs_.b,))
                        for ks in range(4):
                            cx.dma("pool", Bv[4 * jj + ks], bs_[ks * 32:(ks + 1) * 32, :], reads=(bs_.b,), writes=(Bp_d.b,))
                    cx.cc("ReduceScatter", ALU.add, Bp_d.t, Brs_d.t, reads=(Bp_d.b,), writes=(Brs_d.b,))

            if want("P2"):
                m0 = xout_d.t[160:176, :].rearrange("r (q c) -> (r q) c", c=128)
                m1 = xout_d.t[XROWS + 160:XROWS + 176, :].rearrange("r (q c) -> (r q) c", c=128)
                cx.dma("pool", kcT_d[:, 0:128], m0[128:256, :], reads=(xout_d.b,), writes=(kcT_d.b,))
                cx.dma("pool", kcT_d[:, TOK + 128:TOK + 256], m1[0:128, :], reads=(xout_d.b,), writes=(kcT_d.b,))
                cx.dma("pool", vc_d[0:128, :], m0[384:512, :], reads=(xout_d.b,), writes=(vc_d.b,))
                cx.dma("pool", vc_d[TOK + 128:TOK + 256, :], m1[256:384, :], reads=(xout_d.b,), writes=(vc_d.b,))

            if want("P3"):
                with Phase(cx) as ph:
                    stage = mk_stage(ph)
                    Wq = ph.sb("Wq", [128, 2, 3, 768], BF16)
                    for which, src in enumerate((w_uqr, w_uqs)):
                        for c in range(3):
                            load_cast(ph, lambda c0, c1, which=which, c=c: (Wq[:, which, c, c0:c1], Wq.b),
                                      lambda c0, c1, src=src, c=c: src[l, c * 128:(c + 1) * 128, c0:c1], 128, 768, stage)
                    Wkv = ph.sb("Wkv", [128, 1024], BF16)
                    load_cast(ph, lambda c0, c1: (Wkv[:, c0:c1], Wkv.b), lambda c0, c1: w_ukv[l, :, c0:c1], 128, 1024, stage)
                    cqn = ph.sb("cqn", [128, 3, TOK], BF16)
                    cx.dma("sp", cqn[:], cqT_d.t.rearrange("(c p) t -> p c t", p=128), reads=(cqT_d.b,), writes=(cqn.b,))
                    ckv = ph.sb("ckv", [128, SEQ], BF16)
                    KTs = [ph.sb(f"KT{i}", [96, SEQ], BF16) for i in range(2)]
                    for r in range(2):
                        cx.dma("sp", ckv[:, r * TOK:(r + 1) * TOK], xout_d[r * XROWS:r * XROWS + 128, :], reads=(xout_d.b,), writes=(ckv.b,))
                        for i in range(2):
                            cx.dma("sp", KTs[i][64:96, r * TOK:(r + 1) * TOK], xout_d[r * XROWS + 128:r * XROWS + 160, :],
                                   reads=(xout_d.b,), writes=(KTs[i].b,))
                    Vs = [ph.sb(f"V{i}", [128, 64, 65], BF16) for i in range(2)]
                    for i in range(2):
                        cx.op("pool", lambda e, i=i: e.memset(Vs[i][:, :, 64:65], 1.0), writes=(Vs[i].b,))
                    QTs = [ph.sb(f"QT{i}", [96, TOK], BF16) for i in range(2)]
                    ropet = ph.sb("ropet", [96, 2, TOK], F32)
                    cx.dma("sp", ropet[64:96], rope_d.t.rearrange("w p t -> p w t"), reads=(rope_d.b,), writes=(ropet.b,))
                    r1 = ph.sb("r1", [96, 512], F32)
                    r2 = ph.sb("r2", [96, 512], F32)
                    ones1 = ph.sb("ones1", [65, 64], F32)
                    cx.op("dve", lambda e: e.memset(ones1[:], 1.0), writes=(ones1.b,))
                    PT2 = Rot([ph.sb(f"PT{i}", [128, 1024], BF16) for i in range(4)])
                    osbs = Rot([ph.sb(f"osb{i}", [65, 512], F32) for i in range(2)])
                    ost = Rot([ph.sb(f"ost{i}", [64, 512], BF16) for i in range(2)])
                    pS2 = Rot([ph.ps(f"pS{i}", [128, 1024]) for i in range(3)])
                    pO = Rot([ph.ps(f"pO{i}", [128, 512]) for i in range(1)])
                    pX = Rot([ph.ps(f"pX{i}", [128, 512]) for i in range(1)])

                    half_state = [None, 0]

                    def half_bank():
                        if half_state[0] is None or half_state[1] == 2:
                            half_state[0] = pS2.next()
                            half_state[1] = 0
                        t_, o_ = half_state[0], half_state[1] * 512
                        half_state[1] += 1
                        return t_, o_

                    def prep_head_gen(h):
                        KT, V, QT = KTs[h % 2], Vs[h % 2], QTs[h % 2]
                        for ch in range(16):
                            bk, o = half_bank()
                            cx.op("pe", lambda e, bk=bk, o=o, ch=ch: e.matmul(out=bk[0:64, o:o + 512], lhsT=Wkv[:, h * 128:h * 128 + 64], rhs=ckv[:, ch * 512:(ch + 1) * 512],
                                                                             start=True, stop=True), reads=(Wkv.b, ckv.b), writes=(bk.b,))
                            cx.op("dve", lambda e, bk=bk, o=o, ch=ch: e.tensor_copy(out=KT[0:64, ch * 512:(ch + 1) * 512], in_=bk[0:64, o:o + 512]),
                                  reads=(bk.b,), writes=(KT.b,))
                            yield
                        for g8 in range(8):
                            bk, o = half_bank()
                            for j in range(8):
                                kt = g8 * 8 + j
                                cx.op("pe", lambda e, bk=bk, o=o, kt=kt, j=j: e.matmul(out=bk[:, o + j * 64:o + (j + 1) * 64], lhsT=ckv[:, kt * 128:(kt + 1) * 128],
                                                                                      rhs=Wkv[:, h * 128 + 64:h * 128 + 128], start=True, stop=True),
                                      inc=(j == 7), reads=(Wkv.b, ckv.b), writes=(bk.b,))
                            cx.op("dve", lambda e, bk=bk, o=o, g8=g8: e.tensor_copy(out=V[:, g8 * 8:(g8 + 1) * 8, 0:64],
                                                                                   in_=bk[:, o:o + 512].rearrange("p (j c) -> p j c", j=8)),
                                  reads=(bk.b,), writes=(V.b,))
                            yield
                        for ch in range(8):
                            half_state[1] = 2
                            bk, oA = half_bank()
                            _, oB = half_bank()
                            for which, o in ((0, oA), (1, oB)):
                                for c in range(3):
                                    cx.op("pe", lambda e, which=which, o=o, c=c, ch=ch: e.matmul(
                                        out=bk[0:96, o:o + 512], lhsT=Wq[:, which, c, h * 96:(h + 1) * 96], rhs=cqn[:, c, ch * 512:(ch + 1) * 512],
                                        start=(c == 0), stop=(c == 2)), inc=(c == 2), reads=(Wq.b, cqn.b), writes=(bk.b,))
                            cx.op("dve", lambda e, ch=ch: e.tensor_copy(out=QT[0:64, ch * 512:(ch + 1) * 512], in_=bk[0:64, oA:oA + 512]),
                                  reads=(bk.b,), writes=(QT.b,))
                            cx.op("dve", lambda e, ch=ch: e.tensor_tensor(out=r1[64:96, :], in0=bk[64:96, oA:oA + 512], in1=ropet[64:96, 0, ch * 512:(ch + 1) * 512],
                                                                          op=ALU.mult), reads=(bk.b, ropet.b), writes=(r1.b,))
                            cx.op("dve", lambda e, ch=ch: e.tensor_tensor(out=r2[64:96, :], in0=bk[64:96, oB:oB + 512], in1=ropet[64:96, 1, ch * 512:(ch + 1) * 512],
                                                                          op=ALU.mult), reads=(bk.b, ropet.b), writes=(r2.b,))
                            cx.op("dve", lambda e, ch=ch: e.tensor_tensor(out=QT[64:96, ch * 512:(ch + 1) * 512], in0=r1[64:96, :], in1=r2[64:96, :], op=ALU.add),
                                  reads=(r1.b, r2.b), writes=(QT.b,))
                            yield

                    def prep_head(h):
                        for _ in prep_head_gen(h):
                            pass

                    LOOK = 2
                    items = [(h, qc, pr) for h in range(8) for qc in range(8) for pr in range(32)]
                    pend = []
                    inflight = []
                    cur_po = {}

                    def s_stage(h, qc, pr):
                        KT, QT = KTs[h % 2], QTs[h % 2]
                        ps2 = pS2.next()
                        pt2 = PT2.next()
                        for j in range(2):
                            kt = 2 * pr + j
                            cx.op("pe", lambda e, j=j, kt=kt: e.matmul(out=ps2[:, j * 512:(j + 1) * 512], lhsT=KT[:, kt * 128:(kt + 1) * 128],
                                                                       rhs=QT[:, qc * 512:(qc + 1) * 512], start=True, stop=True),
                                  inc=(j == 1), reads=(KT.b, QT.b), writes=(ps2.b,))
                        cx.op("act", lambda e: e.activation(out=pt2[:], in_=ps2[:], func=AF.Exp, scale=MLA_SCALE),
                              reads=(ps2.b,), writes=(pt2.b,))
                        return pt2

                    def pv_stage(h, qc, pr, pt2, step):
                        V = Vs[h % 2]
                        if pr == 0:
                            cur_po[(h, qc)] = pO.next()
                        po = cur_po[(h, qc)]
                        for j in range(2):
                            kt = 2 * pr + j
                            cx.op("pe", lambda e, j=j, kt=kt: e.matmul(out=po[0:65, :], lhsT=V[:, kt, :], rhs=pt2[:, j * 512:(j + 1) * 512],
                                                                       start=(kt == 0), stop=(kt == 63)),
                                  inc=(j == 1), reads=(V.b, pt2.b), writes=(po.b,))
                        if pr == 31:
                            osb = osbs.next()
                            cx.op("dve", lambda e: e.tensor_copy(out=osb[:], in_=po[0:65, :]), reads=(po.b,), writes=(osb.b,))

                            def fin_a(osb=osb):
                                cx.op("act", lambda e: e.activation(out=osb[64:65, :], in_=osb[64:65, :], func=AF.Ln), reads=(osb.b,), writes=(osb.b,))
                                cx.op("act", lambda e: e.activation(out=osb[64:65, :], in_=osb[64:65, :], func=AF.Exp, scale=-1.0), reads=(osb.b,), writes=(osb.b,))
                            pend.append((step + 2, fin_a))

                            def fin(h=h, qc=qc, osb=osb):
                                bc = pX.next()
                                cx.op("pe", lambda e: e.matmul(out=bc[0:64, :], lhsT=ones1[64:65, :], rhs=osb[64:65, :], start=True, stop=True),
                                      reads=(ones1.b, osb.b), writes=(bc.b,))
                                o = ost.next()
                                cx.op("dve", lambda e: e.tensor_tensor(out=o[:], in0=osb[0:64, :], in1=bc[0:64, :], op=ALU.mult),
                                      reads=(osb.b, bc.b), writes=(o.b,))
                                cx.dma("pool", oaT_d[h * 64:(h + 1) * 64, qc * 512:(qc + 1) * 512], o[:], reads=(o.b,), writes=(oaT_d.b,))
                            pend.append((step + 6, fin))

                    prep_head(0)
                    prep_head(1)
                    prep_gen = [None]
                    n_it = len(items)
                    for step in range(n_it + LOOK + 8):
                        if step < n_it:
                            h, qc, pr = items[step]
                            inflight.append((items[step], s_stage(h, qc, pr)))
                        if step >= LOOK and inflight and step - LOOK < n_it:
                            (h, qc, pr), pt2 = inflight.pop(0)
                            pv_stage(h, qc, pr, pt2, step)
                            if qc == 7 and pr == 31 and h + 2 < 8:
                                assert prep_gen[0] is None
                                prep_gen[0] = prep_head_gen(h + 2)
                        if prep_gen[0] is not None and step % 7 == 0:
                            try:
                                next(prep_gen[0])
                            except StopIteration:
                                prep_gen[0] = None
                        while pend and pend[0][0] <= step:
                            pend.pop(0)[1]()
                    assert not pend and not inflight and prep_gen[0] is None

            if want("P4"):
                with Phase(cx) as ph:
                    kcT = ph.sb("kcT", [128, 2, TOK + 256], BF16)
                    cx.op("dve", lambda e: e.memset(kcT[64:128, 0, :], 0.0), writes=(kcT.b,))
                    cx.op("dve", lambda e: e.memset(kcT[0:64, 1, :], 0.0), reads=(kcT.b,), writes=(kcT.b,))
                    cx.dma("sp", kcT[0:64, 0, :], kcT_d[0:64, :], reads=(kcT_d.b,), writes=(kcT.b,))
                    cx.dma("sp", kcT[64:128, 1, :], kcT_d[64:128, :], reads=(kcT_d.b,), writes=(kcT.b,))
                    vc = ph.sb("vc", [128, 34, 2, 65], BF16)
                    cx.op("pool", lambda e: e.memset(vc[:, :, :, 64:65], 1.0), writes=(vc.b,))
                    for kvh in range(2):
                        cx.dma("sp", vc[:, :, kvh, 0:64], vc_d.t.rearrange("(t p) c -> p t c", p=128)[:, :, kvh * 64:(kvh + 1) * 64],
                               reads=(vc_d.b,), writes=(vc.b,))
                    qcT = ph.sb("qcT", [128, 4, TOK], BF16)
                    cx.dma("sp", qcT[:], qcT_d.t, reads=(qcT_d.b,), writes=(qcT.b,))
                    posq_i = ph.sb("posq_i", [128, TOK], I32)
                    cx.dma("sp", posq_i[:], pos_in.partition_broadcast(128), reads=(), writes=(posq_i.b,))
                    posq = ph.sb("posq", [128, TOK], F32)
                    cx.op("dve", lambda e: e.tensor_copy(out=posq[:], in_=posq_i[:]), reads=(posq_i.b,), writes=(posq.b,))
                    posk_i = ph.sb("posk_i", [128, 34], I32)
                    cx.dma("sp", posk_i[:], posx_in.rearrange("t p -> p t"), reads=(), writes=(posk_i.b,), allow_slow_non_contiguous=True)
                    posk = ph.sb("posk", [128, 34], F32)
                    cx.op("dve", lambda e: e.tensor_copy(out=posk[:], in_=posk_i[:]), reads=(posk_i.b,), writes=(posk.b,))
                    cx.op("dve", lambda e: e.tensor_scalar(out=posk[:], in0=posk[:], scalar1=-1.0, scalar2=None, op0=ALU.mult),
                          reads=(posk.b,), writes=(posk.b,))
                    wm = ph.sb("wm", [128, 4, 128], F32)
                    cx.dma("sp", wm[:], wmask_in.rearrange("w p c -> p w c"), reads=(), writes=(wm.b,))
                    es_ = ph.sb("es", [65, 8], F32)
                    cx.dma("sp", es_[64:65, :], sink_in[l:l + 1, :], reads=(), writes=(es_.b,))
                    cx.op("act", lambda e: e.activation(out=es_[64:65, :], in_=es_[64:65, :], func=AF.Exp), reads=(es_.b,), writes=(es_.b,))
                    ones1 = ph.sb("ones1", [65, 64], F32)
                    cx.op("dve", lambda e: e.memset(ones1[:], 1.0), writes=(ones1.b,))
                    onehot = ph.sb("onehot", [65, 65], BF16)
                    cx.op("dve", lambda e: e.memset(onehot[:], 0.0), writes=(onehot.b,))
                    cx.op("dve", lambda e: e.memset(onehot[64:65, 64:65], 1.0), reads=(onehot.b,), writes=(onehot.b,))
                    esrow = ph.sb("esrow", [65, 2, 512], BF16)
                    for kvh in range(2):
                        cx.op("dve", lambda e, kvh=kvh: e.tensor_copy(out=esrow[64:65, kvh, :].rearrange("p (g q) -> p g q", g=4),
                                                                      in_=es_[64:65, kvh * 4:(kvh + 1) * 4].unsqueeze(2).to_broadcast([1, 4, 128])),
                              reads=(es_.b,), writes=(esrow.b,))
                    slopeT = ph.sb("slopeT", [128, 2, 4, 128], F32)
                    for kvh in range(2):
                        for g in range(4):
                            cx.op("dve", lambda e, kvh=kvh, g=g: e.memset(slopeT[:, kvh, g, :], -SLOPES[kvh * 4 + g]), writes=(slopeT.b,))
                    Ds = Rot([ph.sb(f"Dm{i}", [128, 128], F32) for i in range(9)])
                    biases = Rot([ph.sb(f"bias{i}", [128, 512], F32) for i in range(12)])
                    tmps = Rot([ph.sb(f"tmp{i}", [128, 512], F32) for i in range(4)])
                    PTs = Rot([ph.sb(f"PT{i}", [128, 512], BF16) for i in range(6)])
                    osbs = Rot([ph.sb(f"osb{i}", [65, 512], F32) for i in range(3)])
                    ost = Rot([ph.sb(f"ost{i}", [64, 512], BF16) for i in range(2)])
                    pS = Rot([ph.ps(f"pS{i}", [128, 512]) for i in range(4)])
                    pO = Rot([ph.ps(f"pO{i}", [128, 512]) for i in range(2)])
                    pX = Rot([ph.ps(f"pX{i}", [128, 512]) for i in range(2)])
                    ocv = ocT_d.t.rearrange("(h d) t -> d h t", d=64)
                    LOOK = 3
                    steps = [(j, kvh, kk) for j in range(NT) for kvh in range(2) for kk in range(3)]
                    blk_bias = {}
                    cur_po = {}
                    pend = []
                    inflight = []

                    def block_prep(j):
                        bl = {}
                        for kk in range(3):
                            tt = j + kk
                            Dm = Ds.next()
                            cx.op("act", lambda e, Dm=Dm, tt=tt: e.activation(out=Dm[:], in_=posq[:, j * 128:(j + 1) * 128], func=AF.Abs,
                                                                              bias=posk[:, tt:tt + 1], scale=1.0),
                                  reads=(posq.b, posk.b), writes=(Dm.b,))
                            mi = None
                            if kk == 0:
                                mi = 2 if j == 0 else 0
                            elif kk == 2:
                                mi = 3 if j == NT - 1 else 1
                            if mi is not None:
                                cx.op("dve", lambda e, Dm=Dm, mi=mi: e.tensor_tensor(out=Dm[:], in0=Dm[:], in1=wm[:, mi, :], op=ALU.add),
                                      reads=(Dm.b, wm.b), writes=(Dm.b,))
                            bl[kk] = Dm
                        blk_bias[j] = bl

                    def s_stage(j, kvh, kk):
                        tt = j + kk
                        ps_ = pS.next()
                        cx.op("pe", lambda e: e.matmul(
                            out=ps_[:].rearrange("p (g q) -> p g q", g=4), lhsT=kcT[:, kvh, tt * 128:(tt + 1) * 128],
                            rhs=qcT[:, :, j * 128:(j + 1) * 128], start=True, stop=True),
                            reads=(kcT.b, qcT.b), writes=(ps_.b,))
                        tmp = tmps.next()
                        Dm = blk_bias[j][kk]
                        for g in range(4):
                            cx.op("dve", lambda e, g=g: e.scalar_tensor_tensor(
                                out=tmp[:, g * 128:(g + 1) * 128], in0=Dm[:], scalar=-SLOPES[kvh * 4 + g],
                                in1=ps_[:, g * 128:(g + 1) * 128], op0=ALU.mult, op1=ALU.add),
                                reads=(Dm.b, ps_.b), writes=(tmp.b,))
                        pt = PTs.next()
                        cx.op("act", lambda e: e.activation(out=pt[:], in_=tmp[:], func=AF.Exp), reads=(tmp.b,), writes=(pt.b,))
                        return pt

                    def pv_stage(j, kvh, kk, pt, step):
                        tt = j + kk
                        if kk == 0:
                            cur_po[(j, kvh)] = pO.next()
                        po = cur_po[(j, kvh)]
                        if kk == 0:
                            cx.op("pe", lambda e: e.matmul(out=po[0:65, :], lhsT=onehot[64:65, :], rhs=esrow[64:65, kvh, :], start=True, stop=False),
                                  inc=False, reads=(onehot.b, esrow.b), writes=(po.b,))
                        cx.op("pe", lambda e: e.matmul(out=po[0:65, :], lhsT=vc[:, tt, kvh, :], rhs=pt[:], start=False, stop=(kk == 2)),
                              reads=(vc.b, pt.b), writes=(po.b,))
                        if kk == 2:
                            osb = osbs.next()

                            def fin_a(osb=osb, po=po):
                                cx.op("act", lambda e: e.copy(out=osb[:], in_=po[0:65, :]), reads=(po.b,), writes=(osb.b,))
                                cx.op("act", lambda e: e.activation(out=osb[64:65, :], in_=osb[64:65, :], func=AF.Ln), reads=(osb.b,), writes=(osb.b,))
                                cx.op("act", lambda e: e.activation(out=osb[64:65, :], in_=osb[64:65, :], func=AF.Exp, scale=-1.0), reads=(osb.b,), writes=(osb.b,))

                            st8 = {}

                            def fin_b(osb=osb):
                                bc = pX.next()
                                st8["bc"] = bc
                                cx.op("pe", lambda e: e.matmul(out=bc[0:64, :], lhsT=ones1[64:65, :], rhs=osb[64:65, :], start=True, stop=True),
                                      reads=(ones1.b, osb.b), writes=(bc.b,))

                            def fin_c(osb=osb):
                                bc = st8["bc"]
                                o = ost.next()
                                cx.op("dve", lambda e: e.tensor_tensor(out=o[:], in0=osb[0:64, :], in1=bc[0:64, :], op=ALU.mult),
                                      reads=(osb.b, bc.b), writes=(o.b,))
                                cx.dma("pool", ocv[:, kvh * 4:(kvh + 1) * 4, j * 128:(j + 1) * 128], o[:].rearrange("p (g t) -> p g t", g=4),
                                       reads=(o.b,), writes=(ocT_d.b,))
                            pend.append((step + 2, fin_a))
                            pend.append((step + 3, fin_b))
                            pend.append((step + 4, fin_c))

                    block_prep(0)
                    n_st = len(steps)
                    for step in range(n_st + LOOK + 6):
                        if step < n_st:
                            j, kvh, kk = steps[step]
                            if kvh == 0 and kk == 0 and j + 1 < NT:
                                block_prep(j + 1)
                            inflight.append((steps[step], s_stage(j, kvh, kk)))
                        if step >= LOOK and inflight and step - LOOK < n_st:
                            (j, kvh, kk), pt = inflight.pop(0)
                            pv_stage(j, kvh, kk, pt, step)
                        pend.sort(key=lambda x: x[0])
                        while pend and pend[0][0] <= step:
                            pend.pop(0)[1]()
                    assert not pend and not inflight

            if want("P6"):
                with Phase(cx) as ph:
                    stage = mk_stage(ph, 6)
                    Wg = ph.sb("Wg", [128, 8, 3 * D], BF16)
                    load_w_kc(ph, Wg, w_gates[l], 8, 3 * D, stage)
                    Wa = ph.sb("Wa", [128, 4, D], BF16)
                    load_w_kc(ph, Wa, w_a[l], 4, D, stage)
                    Wc = ph.sb("Wc", [128, 4, D], BF16)
                    load_w_kc(ph, Wc, w_c[l], 4, D, stage)
                    Wo = ph.sb("Wo", [128, 8, D], BF16)
                    load_w_kc(ph, Wo, w_out[l], 8, D, stage)
                    ident = ph.sb("ident", [128, 128], BF16)
                    cx.dma("sp", ident[:], ident_in, reads=(), writes=(ident.b,))
                    gpost = gain_bc(ph, "gpost", g_mix_post[l:l + 1, :])
                    hTt = [ph.sb(f"hTt{i}", [128, 8, 128], BF16) for i in range(2)]
                    oat = [ph.sb(f"oat{i}", [128, 4, 128], BF16) for i in range(2)]
                    oct_ = [ph.sb(f"oct{i}", [128, 4, 128], BF16) for i in range(2)]
                    brs = [ph.sb(f"brs{i}", [128, D], F32) for i in range(2)]
                    xts = [ph.sb(f"xt{i}", [128, D], F32) for i in range(4)]
                    sg = [ph.sb(f"sg{i}", [128, D], F32) for i in range(3)]
                    mg = ph.sb("mg", [128, D], F32)
                    tm = ph.sb("tm", [128, D], F32)
                    mbs = [ph.sb(f"mb{i}", [128, D], BF16) for i in range(2)]
                    mTs = [ph.sb(f"mT{i}", [128, 8, 128], BF16) for i in range(2)]
                    junk = ph.sb("junk", [128, D], F32)
                    sss = [ph.sb(f"ss{i}", [128, 1], F32) for i in range(2)]
                    xos = [ph.sb(f"xo{i}", [128, D], F32) for i in range(2)]
                    pG = Rot([ph.ps(f"pG{i}", [128, 1024]) for i in range(3)])
                    pT = ph.ps("pT", [128, 8, 128], BF16)
                    hv = hT_d.t.rearrange("(kc p) t -> p kc t", p=128)
                    oav = oaT_d.t.rearrange("(kc p) t -> p kc t", p=128)
                    ocv2 = ocT_d.t.rearrange("(kc p) t -> p kc t", p=128)

                    def st_a(t):
                        sl = slice(t * 128, (t + 1) * 128)
                        hT, oa, oc, br, xt = hTt[t % 2], oat[t % 2], oct_[t % 2], brs[t % 2], xts[t % 4]
                        cx.dma("sp", hT[:], hv[:, :, sl], reads=(hT_d.b,), writes=(hT.b,))
                        cx.dma("sp", oa[:], oav[:, :, sl], reads=(oaT_d.b,), writes=(oa.b,))
                        cx.dma("sp", oc[:], ocv2[:, :, sl], reads=(ocT_d.b,), writes=(oc.b,))
                        cx.dma("sp", br[:], Brs_d[sl, :], reads=(Brs_d.b,), writes=(br.b,))
                        cx.dma("sp", xt[:], x_src[sl, :], reads=(x_src_b,), writes=(xt.b,))

                    def st_b(t):
                        hT, oa, oc, br, mb = hTt[t % 2], oat[t % 2], oct_[t % 2], brs[t % 2], mbs[t % 2]
                        for gi in range(3):
                            pg = pG.next()
                            for n in range(2):
                                for kc in range(8):
                                    cx.op("pe", lambda e, pg=pg, kc=kc, n=n, gi=gi: e.matmul(
                                        out=pg[:, n * 512:(n + 1) * 512], lhsT=hT[:, kc, :], rhs=Wg[:, kc, gi * D + n * 512:gi * D + (n + 1) * 512],
                                        start=(kc == 0), stop=(kc == 7)), inc=(kc == 7), reads=(hT.b, Wg.b), writes=(pg.b,))
                            cx.op("act", lambda e, pg=pg, gi=gi: e.activation(out=sg[gi][:], in_=pg[:], func=AF.Sigmoid),
                                  reads=(pg.b,), writes=(sg[gi].b,))
                        pa = pG.next()
                        for n in range(2):
                            for k4 in range(4):
                                cx.op("pe", lambda e, k4=k4, n=n: e.matmul(out=pa[:, n * 512:(n + 1) * 512], lhsT=oa[:, k4, :],
                                                                          rhs=Wa[:, k4, n * 512:(n + 1) * 512], start=(k4 == 0), stop=(k4 == 3)),
                                      inc=(k4 == 3), reads=(oa.b, Wa.b), writes=(pa.b,))
                        cx.op("dve", lambda e: e.tensor_tensor(out=mg[:], in0=pa[:], in1=sg[0][:], op=ALU.mult),
                              reads=(pa.b, sg[0].b), writes=(mg.b,))
                        pc = pG.next()
                        for n in range(2):
                            for k4 in range(4):
                                cx.op("pe", lambda e, k4=k4, n=n: e.matmul(out=pc[:, n * 512:(n + 1) * 512], lhsT=oc[:, k4, :],
                                                                          rhs=Wc[:, k4, n * 512:(n + 1) * 512], start=(k4 == 0), stop=(k4 == 3)),
                                      inc=(k4 == 3), reads=(oc.b, Wc.b), writes=(pc.b,))
                        cx.op("dve", lambda e: e.tensor_tensor(out=tm[:], in0=pc[:], in1=sg[2][:], op=ALU.mult),
                              reads=(pc.b, sg[2].b), writes=(tm.b,))
                        cx.op("dve", lambda e: e.tensor_tensor(out=mg[:], in0=mg[:], in1=tm[:], op=ALU.add), reads=(mg.b, tm.b), writes=(mg.b,))
                        cx.op("dve", lambda e: e.tensor_tensor(out=br[:], in0=br[:], in1=sg[1][:], op=ALU.mult),
                              reads=(br.b, sg[1].b), writes=(br.b,))
                        cx.op("dve", lambda e: e.tensor_tensor(out=mb[:], in0=mg[:], in1=br[:], op=ALU.add),
                              reads=(mg.b, br.b), writes=(mb.b,))

                    def st_c(t):
                        mb, mT = mbs[t % 2], mTs[t % 2]
                        for kc in range(8):
                            cx.op("pe", lambda e, kc=kc: e.transpose(out=pT[:, kc, :], in_=mb[:, kc * 128:(kc + 1) * 128], identity=ident[:]), inc=(kc == 7), reads=(mb.b, ident.b), writes=(pT.b,))
                        cx.op("act", lambda e: e.copy(out=mT[:], in_=pT[:]), reads=(pT.b,), writes=(mT.b,))

                    def st_d(t):
                        sl = slice(t * 128, (t + 1) * 128)
                        mT, xt, ss, xo_ = mTs[t % 2], xts[t % 4], sss[t % 2], xos[t % 2]
                        py = pG.next()
                        for n in range(2):
                            for kc in range(8):
                                cx.op("pe", lambda e, kc=kc, n=n: e.matmul(out=py[:, n * 512:(n + 1) * 512], lhsT=mT[:, kc, :],
                                                                          rhs=Wo[:, kc, n * 512:(n + 1) * 512], start=(kc == 0), stop=(kc == 7)),
                                      inc=(kc == 7), reads=(mT.b, Wo.b), writes=(py.b,))
                        rms_rstd_tokmajor(ph, py[:], (py.b,), junk, ss, D)
                        cx.op("dve", lambda e: e.scalar_tensor_tensor(out=xo_[:], in0=py[:], scalar=ss[:, 0:1], in1=gpost[:],
                                                                      op0=ALU.mult, op1=ALU.mult),
                              reads=(py.b, ss.b, gpost.b), writes=(xo_.b,))
                        cx.op("dve", lambda e: e.tensor_tensor(out=xo_[:], in0=xo_[:], in1=xt[:], op=ALU.add),
                              reads=(xo_.b, xt.b), writes=(xo_.b,))
                        cx.dma("pool", xb[sl, :], xo_[:], reads=(xo_.b,), writes=(xb.b,))

                    skew(NT, [st_a, st_b, st_c, st_d])

            if want("P7"):
                with Phase(cx) as ph:
                    stage = mk_stage(ph, 6)
                    Wfg = ph.sb("Wfg", [128, 8, FFN], BF16)
                    load_w_kc(ph, Wfg, w_fg[l], 8, FFN, stage)
                    Wfu = ph.sb("Wfu", [128, 8, FFN], BF16)
                    load_w_kc(ph, Wfu, w_fu[l], 8, FFN, stage)
                    ident = ph.sb("ident", [128, 128], BF16)
                    cx.dma("sp", ident[:], ident_in, reads=(), writes=(ident.b,))
                    gpre = gain_bc(ph, "gfpre", g_ffn_pre[l:l + 1, :])
                    xts = [ph.sb(f"xt{i}", [128, D], F32) for i in range(4)]

                    def ld_x(t):
                        if t < NT:
                            cx.dma("sp", xts[t % 4][:], xb[t * 128:(t + 1) * 128, :], reads=(xb.b,), writes=(xts[t % 4].b,))
                    ld_x(0)
                    ld_x(1)
                    junk = ph.sb("junk", [128, D], F32)
                    sss = Rot([ph.sb(f"ss{i}", [128, 1], F32) for i in range(3)])
                    hbs = Rot([ph.sb(f"hb{i}", [128, D], BF16) for i in range(2)])
                    hTs = Rot([ph.sb(f"hT{i}", [128, 8, 512], BF16, nsub=4) for i in range(2)])
                    ast = Rot([ph.sb(f"ast{i}", [128, 512], BF16) for i in range(3)])
                    sil = Rot([ph.sb(f"sil{i}", [128, 512], F32) for i in range(2)])
                    pGU = Rot([ph.ps(f"pGU{i}", [128, 512]) for i in range(6)])
                    pT = ph.ps("pT", [128, 8, 128], BF16)
                    hT_l = hTs.items

                    def tile_front(t):
                        ci, tt = t // 4, t % 4
                        hT = hT_l[ci % 2]
                        xt = xts[t % 4]
                        ss = sss.next(); hb = hbs.next()
                        ld_x(t + 2)
                        rms_rstd_tokmajor(ph, xt[:], (xt.b,), junk, ss, D)
                        cx.op("dve", lambda e: e.scalar_tensor_tensor(
                            out=hb[:], in0=xt[:], scalar=ss[:, 0:1], in1=gpre[:], op0=ALU.mult, op1=ALU.mult),
                            reads=(xt.b, ss.b, gpre.b), writes=(hb.b,))
                        for kc in range(8):
                            cx.op("pe", lambda e, kc=kc: e.transpose(out=pT[:, kc, :], in_=hb[:, kc * 128:(kc + 1) * 128], identity=ident[:]),
                                  inc=(kc == 7), reads=(hb.b, ident.b), writes=(pT.b,))
                        cx.op("act", lambda e: e.copy(out=hT[:, :, tt * 128:(tt + 1) * 128], in_=pT[:]),
                              reads=(pT.b,), writes=(hT.bs[tt],))

                    for tt in range(4):
                        tile_front(tt)
                    for ci in range(NCH):
                        hT = hT_l[ci % 2]
                        for f in range(NF):
                            if ci + 1 < NCH and f in (3, 8, 13, 18):
                                tile_front((ci + 1) * 4 + (f - 3) // 5)
                            pg = pGU.next(); pu = pGU.next()
                            for pp, W in ((pg, Wfg), (pu, Wfu)):
                                for kc in range(8):
                                    cx.op("pe", lambda e, pp=pp, W=W, kc=kc, f=f, hT=hT: e.matmul(
                                        out=pp[:], lhsT=W[:, kc, f * 128:(f + 1) * 128], rhs=hT[:, kc, :], start=(kc == 0), stop=(kc == 7)),
                                        inc=(kc == 7), reads=tuple(hT.bs) + (W.b,), writes=(pp.b,))
                            s_ = sil.next()
                            cx.op("act", lambda e, s_=s_, pg=pg: e.activation(out=s_[:], in_=pg[:], func=AF.Silu), reads=(pg.b,), writes=(s_.b,))
                            a_ = ast.next()
                            cx.op("dve", lambda e, s_=s_, pu=pu, a_=a_: e.tensor_tensor(out=a_[:], in0=pu[:], in1=s_[:], op=ALU.mult),
                                  reads=(pu.b, s_.b), writes=(a_.b,))
                            cx.dma("pool", act_d[f * 128:(f + 1) * 128, ci * 512:(ci + 1) * 512], a_[:], reads=(a_.b,), writes=(act_d.b,))
                with Phase(cx) as ph:
                    stage = mk_stage(ph, 6)
                    Wfd = ph.sb("Wfd", [128, NF, D], BF16)
                    load_w_kc(ph, Wfd, w_fd[l], NF, D, stage)
                    gpost = gain_bc(ph, "gfpost", g_ffn_post[l:l + 1, :])
                    aTs = Rot([ph.sb(f"aT{i}", [128, NF, 512], BF16) for i in range(2)])
                    xts = Rot([ph.sb(f"xt{i}", [128, D], F32) for i in range(3)])
                    junk = ph.sb("junk", [128, D], F32)
                    sss = Rot([ph.sb(f"ss{i}", [128, 1], F32) for i in range(3)])
                    xo = Rot([ph.sb(f"xo{i}", [128, D], F32) for i in range(2)])
                    pD = Rot([ph.ps(f"pD{i}", [128, 1024]) for i in range(3)])
                    av = act_d.t.rearrange("(f p) t -> p f t", p=128)
                    for ci in range(NCH):
                        aT = aTs.next()
                        cx.dma("sp", aT[:], av[:, :, ci * 512:(ci + 1) * 512], reads=(act_d.b,), writes=(aT.b,))
                        for tt in range(4):
                            t = ci * 4 + tt
                            xt = xts.next()
                            cx.dma("sp", xt[:], xb[t * 128:(t + 1) * 128, :], reads=(xb.b,), writes=(xt.b,))
                            pd = pD.next()
                            for n in range(2):
                                for f in range(NF):
                                    cx.op("pe", lambda e, pd=pd, f=f, n=n, tt=tt, aT=aT: e.matmul(
                                        out=pd[:, n * 512:(n + 1) * 512], lhsT=aT[:, f, tt * 128:(tt + 1) * 128], rhs=Wfd[:, f, n * 512:(n + 1) * 512],
                                        start=(f == 0), stop=(f == NF - 1)), inc=(f == NF - 1), reads=(aT.b, Wfd.b), writes=(pd.b,))
                            ss = sss.next()
                            rms_rstd_tokmajor(ph, pd[:], (pd.b,), junk, ss, D)
                            xo_ = xo.next()
                            cx.op("dve", lambda e, pd=pd, ss=ss, xo_=xo_: e.scalar_tensor_tensor(out=xo_[:], in0=pd[:], scalar=ss[:, 0:1], in1=gpost[:],
                                                                                                op0=ALU.mult, op1=ALU.mult),
                                  reads=(pd.b, ss.b, gpost.b), writes=(xo_.b,))
                            cx.op("dve", lambda e, xo_=xo_, xt=xt: e.tensor_tensor(out=xo_[:], in0=xo_[:], in1=xt[:], op=ALU.add),
                                  reads=(xo_.b, xt.b), writes=(xo_.b,))
                            cx.dma("pool", xa[t * 128:(t + 1) * 128, :], xo_[:], reads=(xo_.b,), writes=(xa.b,))

            if want("P8"):
                last = (l == nlayers - 1)
                with Phase(cx) as ph:
                    stage = mk_stage(ph, 6)
                    Wpg = ph.sb("Wpg", [128, 8, D], BF16)
                    load_w_kc(ph, Wpg, w_pg[l], 8, D, stage)
                    Wpp = ph.sb("Wpp", [128, 2, D], BF16)
                    load_w_kc(ph, Wpp, w_pp[l], 2, D, stage)
                    ident = ph.sb("ident", [128, 128], BF16)
                    cx.dma("sp", ident[:], ident_in, reads=(), writes=(ident.b,))
                    gple = gain_bc(ph, "gple", g_ple[l:l + 1, :])
                    xts = [ph.sb(f"xt{i}", [128, D], F32) for i in range(7)]
                    pts = [ph.sb(f"pt{i}", [128, 256], F32) for i in range(4)]
                    xbf = [ph.sb(f"xbf{i}", [128, D + 256], BF16) for i in range(2)]
                    xTs = [ph.sb(f"xT{i}", [128, 10, 128], BF16) for i in range(2)]
                    sgts = [ph.sb(f"sgt{i}", [128, D], F32) for i in range(2)]
                    ets = [ph.sb(f"et{i}", [128, D], F32) for i in range(2)]
                    junk = ph.sb("junk", [128, D], F32)
                    sss = [ph.sb(f"ss{i}", [128, 1], F32) for i in range(2)]
                    xos = [ph.sb(f"xo{i}", [128, D], F32) for i in range(2)]
                    pGs = [ph.ps(f"pG{i}", [128, 1024]) for i in range(2)]
                    pTs = [ph.ps(f"pT{i}", [128, 16, 128], BF16) for i in range(2)]

                    def st_ld(t):
                        sl = slice(t * 128, (t + 1) * 128)
                        xt, pt = xts[t % 7], pts[t % 4]
                        cx.dma("sp", xt[:], xa[sl, :], reads=(xa.b,), writes=(xt.b,))
                        cx.dma("sp", pt[:], p_in[l, sl, :], reads=(), writes=(pt.b,))

                    def st_a(t):
                        xt, pt, xb_ = xts[t % 7], pts[t % 4], xbf[t % 2]
                        cx.op("dve", lambda e: e.tensor_copy(out=xb_[:, 0:D], in_=xt[:]), reads=(xt.b,), writes=(xb_.b,))
                        cx.op("act", lambda e: e.copy(out=xb_[:, D:D + 256], in_=pt[:]), reads=(pt.b,), writes=(xb_.b,))

                    def st_b(t):
                        xb_, xT_, pT = xbf[t % 2], xTs[t % 2], pTs[t % 2]
                        for kc in range(10):
                            cx.op("pe", lambda e, kc=kc: e.transpose(out=pT[:, kc, :], in_=xb_[:, kc * 128:(kc + 1) * 128], identity=ident[:]), inc=(kc == 9), reads=(xb_.b, ident.b), writes=(pT.b,))
                        cx.op("act", lambda e: e.copy(out=xT_[:], in_=pT[:, 0:10, :]), reads=(pT.b,), writes=(xT_.b,))

                    def st_c(t):
                        xT_, sgt, et = xTs[t % 2], sgts[t % 2], ets[t % 2]
                        pg = pGs[0]
                        for n in range(2):
                            for kc in range(8):
                                cx.op("pe", lambda e, kc=kc, n=n: e.matmul(out=pg[:, n * 512:(n + 1) * 512], lhsT=xT_[:, kc, :],
                                                                          rhs=Wpg[:, kc, n * 512:(n + 1) * 512], start=(kc == 0), stop=(kc == 7)),
                                      inc=(kc == 7), reads=(xT_.b, Wpg.b), writes=(pg.b,))
                        cx.op("act", lambda e: e.activation(out=sgt[:], in_=pg[:], func=AF.Sigmoid), reads=(pg.b,), writes=(sgt.b,))
                        pe_ = pGs[1]
                        for n in range(2):
                            for kc in range(2):
                                cx.op("pe", lambda e, kc=kc, n=n: e.matmul(out=pe_[:, n * 512:(n + 1) * 512], lhsT=xT_[:, 8 + kc, :],
                                                                          rhs=Wpp[:, kc, n * 512:(n + 1) * 512], start=(kc == 0), stop=(kc == 1)),
                                      inc=(kc == 1), reads=(xT_.b, Wpp.b), writes=(pe_.b,))
                        cx.op("dve", lambda e: e.tensor_tensor(out=et[:], in0=pe_[:], in1=sgt[:], op=ALU.mult),
                              reads=(pe_.b, sgt.b), writes=(et.b,))

                    def st_d(t):
                        sl = slice(t * 128, (t + 1) * 128)
                        xt, et, ss, xo_ = xts[t % 7], ets[t % 2], sss[t % 2], xos[t % 2]
                        rms_rstd_tokmajor(ph, et[:], (et.b,), junk, ss, D)
                        cx.op("dve", lambda e: e.scalar_tensor_tensor(out=xo_[:], in0=et[:], scalar=ss[:, 0:1], in1=gple[:],
                                                                      op0=ALU.mult, op1=ALU.mult),
                              reads=(et.b, ss.b, gple.b), writes=(xo_.b,))
                        cx.op("dve", lambda e: e.tensor_tensor(out=xo_[:], in0=xo_[:], in1=xt[:], op=ALU.add),
                              reads=(xo_.b, xt.b), writes=(xo_.b,))
                        if last:
                            cx.dma("pool", y_out[sl, :], xo_[:], reads=(xo_.b,), writes=(ybuf,))
                        else:
                            cx.dma("pool", xc[sl, :], xo_[:], reads=(xo_.b,), writes=(xc.b,))

                    skew(NT, [st_ld, st_a, st_b, st_c, st_d], [0, 2, 3, 4, 5])
        cx.barrier()
    return nc


def _consts(half):
    bf = ml_dtypes.bfloat16
    ident = np.eye(128, dtype=np.float32).astype(bf)
    invf = (np.float32(10000.0) ** (-np.arange(16, dtype=np.float32) / np.float32(16))).astype(np.float32)
    invf = np.concatenate([invf, invf]).reshape(32, 1)
    c = np.arange(256)
    th = 2.0 * np.pi * np.outer(c, c) / 256.0
    cdft = np.stack([np.cos(th) / 16.0, -np.sin(th) / 16.0]).astype(np.float32).astype(bf)
    s1 = np.arange(128)[None, :, None]
    k1 = np.arange(256)[None, None, :]
    s2 = np.arange(32)[:, None, None]
    ph = (4096 * half * k1 + 32 * s1 * k1 + s2 * k1) % 8192
    ang = 2.0 * np.pi * ph / 8192.0
    nrm = 1.0 / np.sqrt(8192.0)
    G = np.stack([np.cos(ang) * nrm, -np.sin(ang) * nrm, np.sin(ang) * nrm], axis=1).astype(np.float32).astype(bf)
    a2 = 2.0 * np.pi * np.outer(np.arange(32), np.arange(32)) / 32.0
    Er = np.kron(np.eye(4), np.cos(a2))
    Es = np.kron(np.eye(4), np.sin(a2))
    E = np.stack([Er, Es]).astype(np.float32).astype(bf)
    BIG = 1.0e7
    k = np.arange(128)[:, None]
    q = np.arange(128)[None, :]
    m0 = np.where(k >= q, 0.0, BIG)
    m2 = np.where(k <= q, 0.0, BIG)
    e0 = np.full((128, 128), BIG) if half == 0 else m0
    e3 = np.full((128, 128), BIG) if half == 1 else m2
    wmask = np.stack([m0, m2, e0, e3]).astype(np.float32)
    return dict(ident=ident, invf=invf, cdft=cdft, Gmat=G, Emat=E, wmask=wmask)


def _prep_weights(inp):
    w_in = np.asarray(inp["w_in"])
    cq = w_in[:, :, 0:384]
    ckv = w_in[:, :, 384:512]
    kr = w_in[:, :, 512:544]
    krs = np.concatenate([kr[:, :, 16:32], kr[:, :, 0:16]], axis=2)
    qc = w_in[:, :, 544:1056]
    idx = np.concatenate([np.concatenate([np.arange((0 * 4 + g) * 64, (0 * 4 + g) * 64 + 64),
                                          np.arange((1 * 4 + g) * 64, (1 * 4 + g) * 64 + 64)]) for g in range(4)])
    qcp = qc[:, :, idx]
    kc = w_in[:, :, 1056:1184]
    vc = w_in[:, :, 1184:1312]
    pad = np.zeros(kr.shape[:2] + (64,), dtype=kr.dtype)
    w_small = np.ascontiguousarray(np.concatenate([cq, ckv, pad, kr, pad, krs, qcp, kc, vc], axis=2))
    w_gates = np.ascontiguousarray(w_in[:, :, 1312:])
    w_uq = np.asarray(inp["w_uq"])
    ir, isw = [], []
    for h in range(8):
        b = h * 96
        ir += list(range(b, b + 96))
        isw += list(range(b, b + 64)) + list(range(b + 80, b + 96)) + list(range(b + 64, b + 80))
    return dict(w_small=w_small, w_gates=w_gates, w_uqr=np.ascontiguousarray(w_uq[:, :, ir]),
                w_uqs=np.ascontiguousarray(w_uq[:, :, isw]), w_ukv=np.asarray(inp["w_ukv"]),
                w_a=np.asarray(inp["w_branch_a"]), w_b=np.asarray(inp["w_branch_b"]), w_c=np.asarray(inp["w_branch_c"]),
                w_out=np.asarray(inp["w_out"]), w_fg=np.asarray(inp["w_ffn_gate"]), w_fu=np.asarray(inp["w_ffn_up"]),
                w_fd=np.asarray(inp["w_ffn_down"]), w_pp=np.asarray(inp["w_ple_proj"]), w_pg=np.asarray(inp["w_ple_gate"]),
                g_mix_pre=np.asarray(inp["norm_mix_pre"]), g_q=np.asarray(inp["mla_q_norm"]), g_kv=np.asarray(inp["mla_kv_norm"]),
                g_mix_post=np.asarray(inp["norm_mix_post"]), g_ffn_pre=np.asarray(inp["norm_ffn_pre"]),
                g_ffn_post=np.asarray(inp["norm_ffn_post"]), g_ple=np.asarray(inp["norm_ple"]), sink=np.asarray(inp["gqa_sink"]))


def make_in_maps(inp):
    shared = _prep_weights(inp)
    shared = {k: np.ascontiguousarray(v, dtype=np.float32) for k, v in shared.items()}
    x = np.asarray(inp["x"]); p = np.asarray(inp["p"]); pos = np.asarray(inp["positions"])
    maps = []
    for c in range(8):
        b, half = c // 2, c % 2
        s0 = half * TOK
        m = dict(shared)
        m.update(_consts(half))
        m["x"] = np.ascontiguousarray(x[b, s0:s0 + TOK, :], dtype=np.float32)
        m["p"] = np.ascontiguousarray(p[:, b, s0:s0 + TOK, :], dtype=np.float32)
        m["pos"] = np.ascontiguousarray(pos[b, s0:s0 + TOK].reshape(1, TOK), dtype=np.int32)
        px = np.zeros(TOK + 256, dtype=np.int32)
        lo, hi = s0 - 128, s0 + TOK + 128
        slo, shi = max(lo, 0), min(hi, SEQ)
        px[slo - lo:shi - lo] = pos[b, slo:shi]
        m["posx"] = np.ascontiguousarray(px.reshape(34, 128))
        maps.append(m)
    return maps


_NC_CACHE = {}


def kernel(**inputs):
    if "nc" not in _NC_CACHE:
        _NC_CACHE["nc"] = build_program()
    nc = _NC_CACHE["nc"]
    maps = make_in_maps(inputs)
    res = run_bass_kernel_spmd(nc, maps, core_ids=list(range(8)))
    out = np.empty((NB, SEQ, D), dtype=np.float32)
    for c in range(8):
        b, half = c // 2, c % 2
        out[b, half * TOK:(half + 1) * TOK, :] = res.results[c]["y"]
    return out
```

```python
import numpy as np
import ml_dtypes
from contextlib import ExitStack
import concourse.bass as bass
import concourse.mybir as mybir
from concourse.bass_utils import run_bass_kernel_spmd

F32 = mybir.dt.float32
BF16 = mybir.dt.bfloat16
I32 = mybir.dt.int32
ALU = mybir.AluOpType
AF = mybir.ActivationFunctionType

D = 1024
SEQ = 8192
NB = 4
DEPTH = 2
TOK = 4096
NT = TOK // 128
NCH = TOK // 512
FFN = 2816
NF = FFN // 128
EPS = 1e-6
HQ = 8
SLOPES = [2.0 ** (-8.0 * (h + 1.0) / 8.0) for h in range(8)]
MLA_SCALE = 96 ** -0.5
NSMALL = 1472
O_CQ, O_CKV, O_KRA, O_KRB, O_QC, O_KC, O_VC = 0, 384, 512, 608, 704, 1216, 1344
PAIRS = [[0, 1], [2, 3], [4, 5], [6, 7]]
DBG = {}


def dbg(k):
    return k not in DBG.get('skip', ())

XROWS = 176


class Buf:
    __slots__ = ("name", "w", "r", "dsem", "px")

    def __init__(self, name):
        self.name = name
        self.w = None
        self.r = {}
        self.dsem = None
        self.px = False


class DSem:
    __slots__ = ("sem", "count", "key", "lazy")

    def __init__(self, sem, key):
        self.sem = sem
        self.count = 0
        self.key = key
        self.lazy = False


class Tl:
    __slots__ = ("t", "b", "bs")

    def __init__(self, t, name, nsub=0):
        self.t = t
        self.b = Buf(name)
        self.bs = [Buf(f"{name}.{i}") for i in range(nsub)]

    def __getitem__(self, k):
        return self.t[k]


class Ctx:
    def __init__(self, nc, es):
        self.nc = nc
        self.es = es
        self.eng = {"pe": nc.tensor, "act": nc.scalar, "dve": nc.vector, "pool": nc.gpsimd, "sp": nc.sync}
        self.psem = {}
        for k in self.eng:
            self.psem[k] = es.enter_context(nc.semaphore("p_" + k))
        self.cnt = {k: 0 for k in self.eng}
        self.known = {k: {} for k in self.eng}
        self.pending = {k: False for k in self.eng}
        self.dsems = []
        self.free_dsems = []
        self.n_ins = 0

    def new_dsem(self):
        if self.free_dsems:
            return self.free_dsems.pop()
        s = self.es.enter_context(self.nc.semaphore(f"d{len(self.dsems)}"))
        d = DSem(s, f"d{len(self.dsems)}")
        self.dsems.append(d)
        return d

    def release_dsem(self, d):
        self.free_dsems.append(d)

    def _wait(self, e, ev):
        key, sem, val = ev
        if self.known[e].get(key, 0) >= val:
            return
        self.eng[e].wait_ge(sem, val)
        self.known[e][key] = val

    def _deps(self, e, reads, writes, skip_key=None):
        for b in reads:
            if b.w is not None and b.w[0] != skip_key:
                self._wait(e, b.w)
            if b.px:
                for ev in b.r.values():
                    if ev[0] != e:
                        self._wait(e, ev)
        for b in writes:
            if b.w is not None and b.w[0] != skip_key:
                self._wait(e, b.w)
            for ev in b.r.values():
                if ev[0] != skip_key:
                    self._wait(e, ev)

    def op(self, e, fn, reads=(), writes=(), inc=True):
        skip = "pe" if e == "pe" else None
        self._deps(e, reads, writes, skip)
        ins = fn(self.eng[e])
        if inc:
            self.cnt[e] += 1
            ins.then_inc(self.psem[e], 1)
            ev = (e, self.psem[e], self.cnt[e])
            self.pending[e] = False
        else:
            ev = (e, self.psem[e], self.cnt[e] + 1)
            self.pending[e] = True
        for b in reads:
            b.r[e] = ev
        for b in writes:
            b.w = ev
            b.r = {}
        self.n_ins += 1
        return ins

    def dma(self, q, out, in_, reads=(), writes=(), **kw):
        b0 = writes[0]
        if b0.dsem is None:
            b0.dsem = self.new_dsem()
        ds = b0.dsem
        self._deps(q, reads, writes, ds.key)
        ins = self.eng[q].dma_start(out=out, in_=in_, **kw)
        ds.count += 16
        ins.then_inc(ds.sem, 16)
        ev = (ds.key, ds.sem, ds.count)
        for b in reads:
            b.r[ds.key] = ev
        for b in writes:
            b.w = ev
            b.r = {}
        self.n_ins += 1
        return ins

    def cc(self, kind, op, in_ap, out_ap, reads, writes):
        b0 = writes[0]
        if b0.dsem is None:
            b0.dsem = self.new_dsem()
        ds = b0.dsem
        ds.lazy = True
        self._deps("pool", reads, writes, None)
        ins = self.nc.gpsimd.collective_compute(kind, op, replica_groups=PAIRS, ins=[in_ap], outs=[out_ap])
        ds.count += 1
        ins.then_inc(ds.sem, 1)
        ev = (ds.key, ds.sem, ds.count)
        for b in reads:
            b.r[ds.key] = ev
        for b in writes:
            b.w = ev
            b.r = {}
        return ins

    def barrier(self):
        assert not any(self.pending.values()), self.pending
        for e in self.eng:
            for e2 in self.eng:
                if e2 != e and self.cnt[e2] > 0:
                    self._wait(e, (e2, self.psem[e2], self.cnt[e2]))
            for d in self.dsems:
                if d.count > 0 and not d.lazy:
                    self._wait(e, (d.key, d.sem, d.count))


class Phase:
    uid = 0

    def __init__(self, cx):
        self.cx = cx
        self.es = ExitStack()
        self.tiles = []

    def __enter__(self):
        self.es.__enter__()
        return self

    def __exit__(self, *a):
        self.cx.barrier()
        for t in self.tiles:
            for b in [t.b] + t.bs:
                if b.dsem is not None:
                    self.cx.release_dsem(b.dsem)
                    b.dsem = None
        return self.es.__exit__(*a)

    def sb(self, name, shape, dtype, nsub=0):
        Phase.uid += 1
        name = f"{name}_u{Phase.uid}"
        t = self.es.enter_context(self.cx.nc.sbuf_tensor(name, list(shape), dtype))
        tl = Tl(t, name, nsub)
        self.tiles.append(tl)
        return tl

    def ps(self, name, shape, dtype=F32):
        Phase.uid += 1
        name = f"{name}_u{Phase.uid}"
        t = self.es.enter_context(self.cx.nc.psum_tensor(name, list(shape), dtype))
        tl = Tl(t, name)
        tl.b.px = True
        self.tiles.append(tl)
        return tl


def skew(n, stages, lags=None):
    if lags is None:
        lags = list(range(len(stages)))
    for step in range(n + max(lags)):
        for st, lg in zip(stages, lags):
            t = step - lg
            if 0 <= t < n:
                st(t)


class Rot:
    def __init__(self, items):
        self.items = items
        self.i = 0

    def next(self):
        x = self.items[self.i % len(self.items)]
        self.i += 1
        return x


def build_program(dump=(), phases=None, nlayers=DEPTH):
    nc = bass.Bass("TRN2", target_bir_lowering=False)
    es = ExitStack()
    with es:
        cx = Ctx(nc, es)

        def din(name, shape, dt=F32):
            return nc.dram_tensor(name, list(shape), dt, kind="ExternalInput").ap()

        def dscr(name, shape, dt):
            kind = "ExternalOutput" if name in dump else "Internal"
            return Tl(nc.dram_tensor(name, list(shape), dt, kind=kind).ap(), name)

        x_in = din("x", [TOK, D])
        p_in = din("p", [DEPTH, TOK, 256])
        pos_in = din("pos", [1, TOK], I32)
        posx_in = din("posx", [34, 128], I32)
        w_small = din("w_small", [DEPTH, D, NSMALL])
        w_gates = din("w_gates", [DEPTH, D, 3 * D])
        w_uqr = din("w_uqr", [DEPTH, 384, 768])
        w_uqs = din("w_uqs", [DEPTH, 384, 768])
        w_ukv = din("w_ukv", [DEPTH, 128, 1024])
        w_a = din("w_a", [DEPTH, 512, D])
        w_b = din("w_b", [DEPTH, D, D])
        w_c = din("w_c", [DEPTH, 512, D])
        w_out = din("w_out", [DEPTH, D, D])
        w_fg = din("w_fg", [DEPTH, D, FFN])
        w_fu = din("w_fu", [DEPTH, D, FFN])
        w_fd = din("w_fd", [DEPTH, FFN, D])
        w_pp = din("w_pp", [DEPTH, 256, D])
        w_pg = din("w_pg", [DEPTH, D, D])
        g_mix_pre = din("g_mix_pre", [DEPTH, D])
        g_q = din("g_q", [DEPTH, 384])
        g_kv = din("g_kv", [DEPTH, 128])
        g_mix_post = din("g_mix_post", [DEPTH, D])
        g_ffn_pre = din("g_ffn_pre", [DEPTH, D])
        g_ffn_post = din("g_ffn_post", [DEPTH, D])
        g_ple = din("g_ple", [DEPTH, D])
        sink_in = din("sink", [DEPTH, 8])
        ident_in = din("ident", [128, 128], BF16)
        invf_in = din("invf", [32, 1])
        cdft_in = din("cdft", [2, 256, 256], BF16)
        G_in = din("Gmat", [32, 3, 128, 256], BF16)
        E_in = din("Emat", [2, 128, 128], BF16)
        wmask_in = din("wmask", [4, 128, 128])
        y_out = nc.dram_tensor("y", [TOK, D], F32, kind="ExternalOutput").ap()
        ybuf = Buf("y")

        xa = dscr("xa", [TOK, D], F32)
        xb = dscr("xb", [TOK, D], F32)
        xc = dscr("xc", [TOK, D], F32)
        act_d = dscr("act_d", [FFN, TOK], BF16)
        hT_d = dscr("hT_d", [D, TOK], BF16)
        Yr_d = dscr("Yr_d", [TOK, D], BF16)
        Yi_d = dscr("Yi_d", [TOK, D], BF16)
        Yn_d = dscr("Yn_d", [TOK, D], BF16)
        cqT_d = dscr("cqT_d", [384, TOK], BF16)
        xin_d = dscr("xin_d", [XROWS, TOK], BF16)
        xout_d = dscr("xout_d", [2 * XROWS, TOK], BF16)
        qcT_d = dscr("qcT_d", [128, 4, TOK], BF16)
        kcT_d = dscr("kcT_d", [128, TOK + 256], BF16)
        vc_d = dscr("vc_d", [TOK + 256, 128], BF16)
        oaT_d = dscr("oaT_d", [512, TOK], BF16)
        ocT_d = dscr("ocT_d", [512, TOK], BF16)
        Zr_d = dscr("Zr_d", [SEQ, D], BF16)
        Zi_d = dscr("Zi_d", [SEQ, D], BF16)
        Bp_d = dscr("Bp_d", [SEQ, D], F32)
        Brs_d = dscr("Brs_d", [TOK, D], F32)
        rope_d = dscr("rope_d", [2, 32, TOK], F32)
        xin_buf = Buf("x_in")

        def want(name):
            return phases is None or name in phases

        CAST_ENG = ["dve", "act"]
        cast_i = [0]
        def load_cast(ph, dst_ap_fn, src_ap_fn, nparts, ncols, stage, colchunk=2048):
            for c0 in range(0, ncols, colchunk):
                c1 = min(ncols, c0 + colchunk)
                st = stage.next()
                cx.dma("sp", st[0:nparts, 0:c1 - c0], src_ap_fn(c0, c1), reads=(), writes=(st.b,))
                dst, dbuf = dst_ap_fn(c0, c1)
                ce = CAST_ENG[cast_i[0] % len(CAST_ENG)]
                cast_i[0] += 1
                if ce == "act":
                    cx.op("act", lambda e, dst=dst, st=st, n=c1 - c0: e.copy(out=dst, in_=st[0:nparts, 0:n]),
                          reads=(st.b,), writes=(dbuf,))
                else:
                    cx.op(ce, lambda e, dst=dst, st=st, n=c1 - c0: e.tensor_copy(out=dst, in_=st[0:nparts, 0:n]),
                          reads=(st.b,), writes=(dbuf,))

        def load_w_kc(ph, wt, src, nkc, ncols, stage):
            for kc in range(nkc):
                load_cast(ph, lambda c0, c1, kc=kc: (wt[:, kc, c0:c1], wt.b),
                          lambda c0, c1, kc=kc: src[kc * 128:(kc + 1) * 128, c0:c1], 128, ncols, stage)

        def mk_stage(ph, n=3):
            return Rot([ph.sb(f"wstage{i}", [128, 2048], F32) for i in range(n)])

        def gain_bc(ph, name, src_row):
            n = src_row.shape[-1]
            t = ph.sb(name, [128, n], F32)
            cx.dma("sp", t[:], src_row.partition_broadcast(128), reads=(), writes=(t.b,))
            return t

        def rms_rstd_tokmajor(ph, src_ap, src_bufs, junk, ss, n):
            cx.op("act", lambda e: e.activation(out=junk[:, 0:n], in_=src_ap, func=AF.Square, accum_out=ss[:, 0:1]),
                  reads=src_bufs, writes=(junk.b, ss.b))
            cx.op("dve", lambda e: e.tensor_scalar(out=ss[:, 0:1], in0=ss[:, 0:1], scalar1=1.0 / n, scalar2=EPS,
                                                   op0=ALU.mult, op1=ALU.add), reads=(ss.b,), writes=(ss.b,))
            cx.op("act", lambda e: e.activation(out=ss[:, 0:1], in_=ss[:, 0:1], func=AF.Sqrt), reads=(ss.b,), writes=(ss.b,))
            cx.op("dve", lambda e: e.reciprocal(out=ss[:, 0:1], in_=ss[:, 0:1]), reads=(ss.b,), writes=(ss.b,))

        if want("P0"):
            with Phase(cx) as ph:
                posi = ph.sb("posi", [32, TOK], I32)
                posf = ph.sb("posf", [32, TOK], F32)
                ang = ph.sb("ang", [32, TOK], F32)
                nn = ph.sb("nn", [32, TOK], F32)
                tab = ph.sb("tab", [32, TOK], F32)
                invf = ph.sb("invf", [32, 1], F32)
                cx.dma("sp", posi[:], pos_in.partition_broadcast(32), reads=(), writes=(posi.b,))
                cx.dma("sp", invf[:], invf_in, reads=(), writes=(invf.b,))
                cx.op("dve", lambda e: e.tensor_copy(out=posf[:], in_=posi[:]), reads=(posi.b,), writes=(posf.b,))
                TWO_PI = 2.0 * np.pi
                C1 = 6.28125
                C2 = TWO_PI - C1
                MAGIC = 12582912.0
                for which in range(2):
                    shift = (np.pi / 2.0) if which == 0 else 0.0
                    cx.op("dve", lambda e, shift=shift: e.tensor_scalar(out=ang[:], in0=posf[:], scalar1=invf[:, 0:1], scalar2=shift,
                                                                       op0=ALU.mult, op1=ALU.add), reads=(posf.b, invf.b), writes=(ang.b,))
                    cx.op("dve", lambda e: e.tensor_scalar(out=nn[:], in0=ang[:], scalar1=1.0 / TWO_PI, scalar2=MAGIC,
                                                           op0=ALU.mult, op1=ALU.add), reads=(ang.b,), writes=(nn.b,))
                    cx.op("dve", lambda e: e.tensor_scalar(out=nn[:], in0=nn[:], scalar1=MAGIC, scalar2=None,
                                                           op0=ALU.subtract), reads=(nn.b,), writes=(nn.b,))
                    cx.op("dve", lambda e: e.scalar_tensor_tensor(out=ang[:], in0=nn[:], scalar=-C1, in1=ang[:],
                                                                  op0=ALU.mult, op1=ALU.add), reads=(nn.b, ang.b), writes=(ang.b,))
                    cx.op("dve", lambda e: e.scalar_tensor_tensor(out=ang[:], in0=nn[:], scalar=-C2, in1=ang[:],
                                                                  op0=ALU.mult, op1=ALU.add), reads=(nn.b, ang.b), writes=(ang.b,))
                    cx.op("dve", lambda e: e.tensor_scalar(out=ang[:], in0=ang[:], scalar1=3.1415925, scalar2=-3.1415925,
                                                           op0=ALU.min, op1=ALU.max), reads=(ang.b,), writes=(ang.b,))
                    cx.op("act", lambda e: e.activation(out=tab[:], in_=ang[:], func=AF.Sin), reads=(ang.b,), writes=(tab.b,))
                    if which == 1:
                        cx.op("dve", lambda e: e.tensor_scalar(out=tab[0:16, :], in0=tab[0:16, :], scalar1=-1.0, scalar2=None,
                                                               op0=ALU.mult), reads=(tab.b,), writes=(tab.b,))
                    cx.dma("pool", rope_d[which], tab[:], reads=(tab.b,), writes=(rope_d.b,))

        for l in range(nlayers):
            x_src, x_src_b = (x_in, xin_buf) if l == 0 else (xc.t, xc.b)
            if want("P1"):
                with Phase(cx) as ph:
                    stage = mk_stage(ph)
                    Ws = ph.sb("Ws", [128, 8, NSMALL], BF16)
                    load_w_kc(ph, Ws, w_small[l], 8, NSMALL, stage)
                    Wb = ph.sb("Wb", [128, 8, D], BF16)
                    load_w_kc(ph, Wb, w_b[l], 8, D, stage)
                    cd = ph.sb("cd", [128, 2, 2, 256], BF16)
                    cx.dma("sp", cd[:], cdft_in.rearrange("w (kr p) c -> p w kr c", p=128), reads=(), writes=(cd.b,))
                    Mri = ph.sb("Mri", [128, 2, 8, D], BF16)
                    ident = ph.sb("ident", [128, 128], BF16)
                    cx.dma("sp", ident[:], ident_in, reads=(), writes=(ident.b,))
                    ones = ph.sb("ones", [128, 128], BF16)
                    cx.op("dve", lambda e: e.memset(ones[:], 1.0), writes=(ones.b,))
                    gpre = gain_bc(ph, "gpre", g_mix_pre[l:l + 1, :])
                    gq = ph.sb("gq", [128, 3], F32)
                    cx.dma("sp", gq[:], g_q[l].rearrange("(c p) -> p c", p=128), reads=(), writes=(gq.b,), allow_slow_non_contiguous=True)
                    gkv = ph.sb("gkv", [128, 1], F32)
                    cx.dma("sp", gkv[:], g_kv[l].rearrange("(c p) -> p c", p=128), reads=(), writes=(gkv.b,), allow_slow_non_contiguous=True)
                    ropet = ph.sb("ropet", [96, 2, TOK], F32)
                    cx.dma("sp", ropet[64:96], rope_d.t.rearrange("w p t -> p w t"), reads=(rope_d.b,), writes=(ropet.b,))

                    pF = Rot([ph.ps(f"pF{i}", [128, 512]) for i in range(4)])
                    pY = ph.ps("pY", [128, 1024])
                    pT = ph.ps("pT", [128, 8, 128], BF16)
                    pT2 = ph.ps("pT2", [128, 8, 128], BF16)

                    for which in range(2):
                        for m in range(8):
                            gr, mc = m // 2, m % 2
                            for n in range(2):
                                for kr in range(2):
                                    cx.op("pe", lambda e, which=which, kr=kr, mc=mc, gr=gr, n=n: e.matmul(
                                        out=pY[:, n * 512:(n + 1) * 512], lhsT=cd[:, which, kr, mc * 128:(mc + 1) * 128],
                                        rhs=Wb[:, 2 * gr + kr, n * 512:(n + 1) * 512], start=(kr == 0), stop=(kr == 1)),
                                        inc=(kr == 1), reads=(cd.b, Wb.b), writes=(pY.b,))
                            cx.op("act", lambda e, which=which, m=m: e.copy(out=Mri[:, which, m, :], in_=pY[:]),
                                  reads=(pY.b,), writes=(Mri.b,))

                    xts = Rot([ph.sb(f"xt{i}", [128, D], F32) for i in range(3)])
                    junk = ph.sb("junk", [128, D], F32)
                    sss = Rot([ph.sb(f"ss{i}", [128, 1], F32) for i in range(3)])
                    hbs = Rot([ph.sb(f"hb{i}", [128, D], BF16) for i in range(2)])
                    hTs = Rot([ph.sb(f"hT{i}", [128, 8, 512], BF16, nsub=4) for i in range(2)])
                    ysb = Rot([ph.sb(f"ysb{i}", [128, D], BF16) for i in range(3)])
                    vst = Rot([ph.sb(f"vst{i}", [128, 4, 128], BF16) for i in range(2)])
                    sqs = [ph.sb(f"sq{i}", [128, 512], BF16) for i in range(3)]
                    msb = ph.sb("msb", [128, 512], F32)
                    cqst = Rot([ph.sb(f"cqst{i}", [128, 3, 512], BF16) for i in range(2)])
                    fst = Rot([ph.sb(f"fst{i}", [128, 512], BF16) for i in range(3)])
                    r1 = ph.sb("r1", [96, 512], F32)
                    r2 = ph.sb("r2", [96, 512], F32)
                    krst = Rot([ph.sb(f"krst{i}", [96, 512], BF16) for i in range(2)])

                    pTs = [pT, pT2]
                    hT_l = hTs.items
                    hb_l = hbs.items
                    chunk_pV = {}

                    xt_l = xts.items

                    def st_ld(t):
                        xt = xt_l[t % 3]
                        cx.dma("sp", xt[:], x_src[t * 128:(t + 1) * 128, :], reads=(x_src_b,), writes=(xt.b,))

                    def st_a(t):
                        xt = xt_l[t % 3]
                        ss = sss.next()
                        hb = hb_l[t % 2]
                        rms_rstd_tokmajor(ph, xt[:], (xt.b,), junk, ss, D)
                        cx.op("dve", lambda e: e.scalar_tensor_tensor(
                            out=hb[:], in0=xt[:], scalar=ss[:, 0:1], in1=gpre[:], op0=ALU.mult, op1=ALU.mult),
                            reads=(xt.b, ss.b, gpre.b), writes=(hb.b,))

                    def st_b(t):
                        ci, tt = t // 4, t % 4
                        hb, hT, pT_ = hb_l[t % 2], hT_l[ci % 2], pTs[t % 2]
                        for kc in range(8):
                            cx.op("pe", lambda e, kc=kc: e.transpose(out=pT_[:, kc, :], in_=hb[:, kc * 128:(kc + 1) * 128], identity=ident[:]),
                                  inc=(kc == 7), reads=(hb.b, ident.b), writes=(pT_.b,))
                        cx.op("act", lambda e: e.copy(out=hT[:, :, tt * 128:(tt + 1) * 128], in_=pT_[:]),
                              reads=(pT_.b,), writes=(hT.bs[tt],))

                    def st_c(t):
                        ci, tt = t // 4, t % 4
                        hT = hT_l[ci % 2]
                        if tt == 0:
                            chunk_pV[ci] = pF.next()
                        pV = chunk_pV[ci]
                        for kc in range(8):
                            cx.op("pe", lambda e, kc=kc: e.matmul(
                                out=pV[:, tt * 128:(tt + 1) * 128], lhsT=hT[:, kc, tt * 128:(tt + 1) * 128],
                                rhs=Ws[:, kc, O_VC:O_VC + 128], start=(kc == 0), stop=(kc == 7)),
                                inc=(kc == 7), reads=(hT.bs[tt], Ws.b), writes=(pV.b,))
                        for which in range(2):
                            for n in range(2):
                                for kc in range(8):
                                    cx.op("pe", lambda e, kc=kc, n=n, which=which: e.matmul(
                                        out=pY[:, n * 512:(n + 1) * 512], lhsT=hT[:, kc, tt * 128:(tt + 1) * 128],
                                        rhs=Mri[:, which, kc, n * 512:(n + 1) * 512], start=(kc == 0), stop=(kc == 7)),
                                        inc=(kc == 7), reads=(hT.bs[tt], Mri.b), writes=(pY.b,))
                            if which == 0:
                                ys = ysb.next()
                                cx.op("act", lambda e, ys=ys: e.copy(out=ys[:], in_=pY[:]), reads=(pY.b,), writes=(ys.b,))
                                cx.dma("pool", Yr_d[t * 128:(t + 1) * 128, :], ys[:], reads=(ys.b,), writes=(Yr_d.b,))
                            else:
                                ys = ysb.next()
                                cx.op("dve", lambda e, ys=ys: e.tensor_copy(out=ys[:], in_=pY[:]), reads=(pY.b,), writes=(ys.b,))
                                cx.dma("pool", Yi_d[t * 128:(t + 1) * 128, :], ys[:], reads=(ys.b,), writes=(Yi_d.b,))
                        if tt == 3:
                            chunk_level(ci, hT, pV)

                    def chunk_level(ci, hT, pV):
                        tok0 = ci * 512
                        for _once in (0,):
                            vs = vst.next()
                            cx.op("dve", lambda e, vs=vs, pV=pV: e.tensor_copy(out=vs[:], in_=pV[:].rearrange("p (t c) -> p t c", t=4)),
                                  reads=(pV.b,), writes=(vs.b,))
                            cx.dma("pool", vc_d[128 + tok0:128 + tok0 + 512, :].rearrange("(t p) c -> p t c", p=128), vs[:],
                                   reads=(vs.b,), writes=(vc_d.b,))
                            cx.dma("pool", hT_d.t.rearrange("(kc p) t -> p kc t", p=128)[:, :, tok0:tok0 + 512], hT[:],
                                   reads=tuple(hT.bs), writes=(hT_d.b,))

                            def fm_proj(col0, m, pbank, hT=hT):
                                for kc in range(8):
                                    cx.op("pe", lambda e, kc=kc: e.matmul(out=pbank[0:m, :], lhsT=Ws[:, kc, col0:col0 + m], rhs=hT[:, kc, :],
                                                                           start=(kc == 0), stop=(kc == 7)),
                                          inc=(kc == 7), reads=tuple(hT.bs) + (Ws.b,), writes=(pbank.b,))

                            def fm_norm(banks, nfeat, gcol, dst_fn):
                                pS = pF.next()
                                for c, bk in enumerate(banks):
                                    cx.op("act", lambda e, c=c, bk=bk: e.activation(out=sqs[c][:], in_=bk[:], func=AF.Square),
                                          reads=(bk.b,), writes=(sqs[c].b,))
                                for c in range(len(banks)):
                                    cx.op("pe", lambda e, c=c: e.matmul(out=pS[:], lhsT=ones[:], rhs=sqs[c][:], start=(c == 0),
                                                                        stop=(c == len(banks) - 1)), reads=(ones.b, sqs[c].b), writes=(pS.b,))
                                cx.op("dve", lambda e: e.tensor_scalar(out=msb[:], in0=pS[:], scalar1=1.0 / nfeat, scalar2=EPS,
                                                                       op0=ALU.mult, op1=ALU.add), reads=(pS.b,), writes=(msb.b,))
                                cx.op("act", lambda e: e.activation(out=msb[:], in_=msb[:], func=AF.Sqrt), reads=(msb.b,), writes=(msb.b,))
                                cx.op("dve", lambda e: e.reciprocal(out=msb[:], in_=msb[:]), reads=(msb.b,), writes=(msb.b,))
                                for c, bk in enumerate(banks):
                                    dst, dbuf = dst_fn(c)
                                    cx.op("dve", lambda e, c=c, bk=bk, dst=dst: e.scalar_tensor_tensor(
                                        out=dst, in0=bk[:], scalar=gcol[:, c:c + 1], in1=msb[:], op0=ALU.mult, op1=ALU.mult),
                                        reads=(bk.b, gcol.b, msb.b), writes=(dbuf,))

                            banks = [pF.next() for _ in range(3)]
                            for c in range(3):
                                fm_proj(O_CQ + c * 128, 128, banks[c])
                            cq = cqst.next()
                            fm_norm(banks, 384, gq, lambda c: (cq[:, c, :], cq.b))
                            cx.dma("pool", cqT_d.t.rearrange("(c p) t -> p c t", p=128)[:, :, tok0:tok0 + 512], cq[:],
                                   reads=(cq.b,), writes=(cqT_d.b,))
                            bk = pF.next()
                            fm_proj(O_CKV, 128, bk)
                            f1 = fst.next()
                            fm_norm([bk], 128, gkv, lambda c: (f1[:], f1.b))
                            cx.dma("pool", xin_d[0:128, tok0:tok0 + 512], f1[:], reads=(f1.b,), writes=(xin_d.b,))
                            bA = pF.next()
                            bB = pF.next()
                            fm_proj(O_KRA, 96, bA)
                            fm_proj(O_KRB, 96, bB)
                            cx.op("dve", lambda e, bA=bA: e.tensor_tensor(out=r1[64:96, :], in0=bA[64:96, :], in1=ropet[64:96, 0, tok0:tok0 + 512],
                                                                          op=ALU.mult), reads=(bA.b, ropet.b), writes=(r1.b,))
                            cx.op("dve", lambda e, bB=bB: e.tensor_tensor(out=r2[64:96, :], in0=bB[64:96, :], in1=ropet[64:96, 1, tok0:tok0 + 512],
                                                                          op=ALU.mult), reads=(bB.b, ropet.b), writes=(r2.b,))
                            kr = krst.next()
                            cx.op("dve", lambda e, kr=kr: e.tensor_tensor(out=kr[64:96, :], in0=r1[64:96, :], in1=r2[64:96, :], op=ALU.add),
                                  reads=(r1.b, r2.b), writes=(kr.b,))
                            cx.dma("pool", xin_d[128:160, tok0:tok0 + 512], kr[64:96, :], reads=(kr.b,), writes=(xin_d.b,))
                            for g in range(4):
                                bk = pF.next()
                                fm_proj(O_QC + g * 128, 128, bk)
                                f1 = fst.next()
                                cx.op("act", lambda e, f1=f1, bk=bk: e.mul(f1[:], bk[:], 0.125),
                                      reads=(bk.b,), writes=(f1.b,))
                                cx.dma("pool", qcT_d[:, g, tok0:tok0 + 512], f1[:], reads=(f1.b,), writes=(qcT_d.b,))
                            bk = pF.next()
                            fm_proj(O_KC, 128, bk)
                            f1 = fst.next()
                            cx.op("act", lambda e, f1=f1, bk=bk: e.copy(out=f1[:], in_=bk[:]), reads=(bk.b,), writes=(f1.b,))
                            cx.dma("pool", kcT_d[:, 128 + tok0:128 + tok0 + 512], f1[:], reads=(f1.b,), writes=(kcT_d.b,))

                    skew(NT, [st_ld, st_a, st_b, st_c], [0, 2, 3, 4])

                    if dbg('misc'):
                        misc = xin_d.t[160:176, :].rearrange("r (q c) -> (r q) c", c=128)
                        cx.dma("pool", misc[0:128, :], kcT_d[:, 128:256], reads=(kcT_d.b,), writes=(xin_d.b,))
                        cx.dma("pool", misc[128:256, :], kcT_d[:, TOK:TOK + 128], reads=(kcT_d.b,), writes=(xin_d.b,))
                        cx.dma("pool", misc[256:384, :], vc_d[128:256, :], reads=(vc_d.b,), writes=(xin_d.b,))
                        cx.dma("pool", misc[384:512, :], vc_d[TOK:TOK + 128, :], reads=(vc_d.b,), writes=(xin_d.b,))

            if want("P2"):
                cx.cc("AllGather", ALU.bypass, xin_d.t, xout_d.t, reads=(xin_d.b,), writes=(xout_d.b,))

            if want("P5"):
                with Phase(cx) as ph:
                    Gs = Rot([ph.sb(f"G{i}", [128, 3, 256], BF16) for i in range(2)])
                    Yt = Rot([ph.sb(f"Yt{i}", [128, 2, D], BF16) for i in range(2)])
                    zst = Rot([ph.sb(f"zst{i}", [128, D], BF16) for i in range(4)])
                    pZ = Rot([ph.ps(f"pZ{i}", [128, 1024]) for i in range(4)])
                    Yv = [y.t.rearrange("(s1 s2) c -> s2 s1 c", s2=32) for y in (Yr_d, Yi_d)]
                    Zv = [z.t.rearrange("(k1 s2) c -> s2 k1 c", s2=32) for z in (Zr_d, Zi_d)]
                    for s2 in range(32):
                        G = Gs.next()
                        Y = Yt.next()
                        cx.dma("sp", G[:], G_in[s2].rearrange("w p c -> p w c"), reads=(), writes=(G.b,))
                        for w in range(2):
                            cx.dma("sp", Y[:, w, :], Yv[w][s2], reads=(Yr_d.b, Yi_d.b), writes=(Y.b,))
                        for m in range(2):
                            for part in range(2):
                                pz = pZ.next()
                                srcs = ((0, 0), (2, 1)) if part == 0 else ((0, 1), (1, 0))
                                for n in range(2):
                                    for i, (gw, yw) in enumerate(srcs):
                                        cx.op("pe", lambda e, pz=pz, G=G, Y=Y, gw=gw, yw=yw, n=n, i=i, m=m: e.matmul(
                                            out=pz[:, n * 512:(n + 1) * 512], lhsT=G[:, gw, m * 128:(m + 1) * 128],
                                            rhs=Y[:, yw, n * 512:(n + 1) * 512], start=(i == 0), stop=(i == 1)),
                                            inc=(i == 1), reads=(G.b, Y.b), writes=(pz.b,))
                                z = zst.next()
                                eng = "act" if part == 0 else "dve"
                                if eng == "act":
                                    cx.op("act", lambda e, z=z, pz=pz: e.copy(out=z[:], in_=pz[:]), reads=(pz.b,), writes=(z.b,))
                                else:
                                    cx.op("dve", lambda e, z=z, pz=pz: e.tensor_copy(out=z[:], in_=pz[:]), reads=(pz.b,), writes=(z.b,))
                                cx.dma("pool", Zv[part][s2, m * 128:(m + 1) * 128, :], z[:], reads=(z.b,), writes=((Zr_d, Zi_d)[part].b,))
                    cx.barrier()
                    Eb = ph.sb("Eb", [128, 2, 128], BF16)
                    cx.dma("sp", Eb[:], E_in.rearrange("w p c -> p w c"), reads=(), writes=(Eb.b,))
                    Zt = Rot([ph.sb(f"Zt{i}", [128, 2, D], BF16) for i in range(2)])
                    bst = Rot([ph.sb(f"bst{i}", [128, D], F32) for i in range(3)])
                    Bv = Bp_d.t.rearrange("(k2 r) c -> r k2 c", r=256)
                    for jj in range(64):
                        Z = Zt.next()
                        cx.dma("sp", Z[:, 0, :], Zr_d[jj * 128:(jj + 1) * 128, :], reads=(Zr_d.b,), writes=(Z.b,))
                        cx.dma("sp", Z[:, 1, :], Zi_d[jj * 128:(jj + 1) * 128, :], reads=(Zi_d.b,), writes=(Z.b,))
                        pz = pZ.next()
                        for n in range(2):
                            for w in range(2):
                                cx.op("pe", lambda e, pz=pz, Z=Z, n=n, w=w: e.matmul(out=pz[:, n * 512:(n + 1) * 512], lhsT=Eb[:, w, :],
                                                                                  rhs=Z[:, w, n * 512:(n + 1) * 512], start=(w == 0), stop=(w == 1)),
                                      inc=(w == 1), reads=(Eb.b, Z.b), writes=(pz.b,))
                        bs_ = bst.next()
                        if jj % 2 == 0:
                            cx.op("act", lambda e, bs_=bs_, pz=pz: e.copy(out=bs_[:], in_=pz[:]), reads=(pz.b,), writes=(bs_.b,))
                        else:
                            cx.op("dve", lambda e, bs_=bs_, pz=pz: e.tensor_copy(out=bs_[:], in_=pz[:]), reads=(pz.b,), writes=(bs_.b,))
                        for ks in range(4):
                            cx.dma("pool", Bv[4 * jj + ks], bs_[ks * 32:(ks + 1) * 32, :], reads=(bs_.b,), writes=(Bp_d.b,))
                    cx.cc("ReduceScatter", ALU.add, Bp_d.t, Brs_d.t, reads=(Bp_d.b,), writes=(Brs_d.b,))

            if want("P2"):
                m0 = xout_d.t[160:176, :].rearrange("r (q c) -> (r q) c", c=128)
                m1 = xout_d.t[XROWS + 160:XROWS + 176, :].rearrange("r (q c) -> (r q) c", c=128)
                cx.dma("pool", kcT_d[:, 0:128], m0[128:256, :], reads=(xout_d.b,), writes=(kcT_d.b,))
                cx.dma("pool", kcT_d[:, TOK + 128:TOK + 256], m1[0:128, :], reads=(xout_d.b,), writes=(kcT_d.b,))
                cx.dma("pool", vc_d[0:128, :], m0[384:512, :], reads=(xout_d.b,), writes=(vc_d.b,))
                cx.dma("pool", vc_d[TOK + 128:TOK + 256, :], m1[256:384, :], reads=(xout_d.b,), writes=(vc_d.b,))

            if want("P3"):
                with Phase(cx) as ph:
                    stage = mk_stage(ph)
                    Wq = ph.sb("Wq", [128, 2, 3, 768], BF16)
                    for which, src in enumerate((w_uqr, w_uqs)):
                        for c in range(3):
                            load_cast(ph, lambda c0, c1, which=which, c=c: (Wq[:, which, c, c0:c1], Wq.b),
                                      lambda c0, c1, src=src, c=c: src[l, c * 128:(c + 1) * 128, c0:c1], 128, 768, stage)
                    Wkv = ph.sb("Wkv", [128, 1024], BF16)
                    load_cast(ph, lambda c0, c1: (Wkv[:, c0:c1], Wkv.b), lambda c0, c1: w_ukv[l, :, c0:c1], 128, 1024, stage)
                    cqn = ph.sb("cqn", [128, 3, TOK], BF16)
                    cx.dma("sp", cqn[:], cqT_d.t.rearrange("(c p) t -> p c t", p=128), reads=(cqT_d.b,), writes=(cqn.b,))
                    ckv = ph.sb("ckv", [128, SEQ], BF16)
                    KTs = [ph.sb(f"KT{i}", [96, SEQ], BF16) for i in range(2)]
                    for r in range(2):
                        cx.dma("sp", ckv[:, r * TOK:(r + 1) * TOK], xout_d[r * XROWS:r * XROWS + 128, :], reads=(xout_d.b,), writes=(ckv.b,))
                        for i in range(2):
                            cx.dma("sp", KTs[i][64:96, r * TOK:(r + 1) * TOK], xout_d[r * XROWS + 128:r * XROWS + 160, :],
                                   reads=(xout_d.b,), writes=(KTs[i].b,))
                    Vs = [ph.sb(f"V{i}", [128, 64, 128], BF16) for i in range(2)]
                    for i in range(2):
                        cx.op("pool", lambda e, i=i: e.memset(Vs[i][:, :, 64:128], 0.0), writes=(Vs[i].b,))
                        cx.op("pool", lambda e, i=i: e.memset(Vs[i][:, :, 64:65], 1.0), reads=(Vs[i].b,), writes=(Vs[i].b,))
                    QTs = [ph.sb(f"QT{i}", [96, TOK], BF16) for i in range(2)]
                    ropet = ph.sb("ropet", [96, 2, TOK], F32)
                    cx.dma("sp", ropet[64:96], rope_d.t.rearrange("w p t -> p w t"), reads=(rope_d.b,), writes=(ropet.b,))
                    r1 = ph.sb("r1", [96, 512], F32)
                    r2 = ph.sb("r2", [96, 512], F32)
                    ones1 = ph.sb("ones1", [65, 64], F32)
                    cx.op("dve", lambda e: e.memset(ones1[:], 1.0), writes=(ones1.b,))
                    PT2 = Rot([ph.sb(f"PT{i}", [128, 1024], BF16) for i in range(4)])
                    osbs = Rot([ph.sb(f"osb{i}", [65, 512], F32) for i in range(2)])
                    ost = Rot([ph.sb(f"ost{i}", [64, 512], BF16) for i in range(2)])
                    pS2 = Rot([ph.ps(f"pS{i}", [128, 1024]) for i in range(3)])
                    pO = Rot([ph.ps(f"pO{i}", [128, 512]) for i in range(1)])
                    pX = Rot([ph.ps(f"pX{i}", [128, 512]) for i in range(1)])

                    half_state = [None, 0]

                    def half_bank():
                        if half_state[0] is None or half_state[1] == 2:
                            half_state[0] = pS2.next()
                            half_state[1] = 0
                        t_, o_ = half_state[0], half_state[1] * 512
                        half_state[1] += 1
                        return t_, o_

                    def prep_head_gen(h):
                        KT, V, QT = KTs[h % 2], Vs[h % 2], QTs[h % 2]
                        for ch in range(16):
                            bk, o = half_bank()
                            cx.op("pe", lambda e, bk=bk, o=o, ch=ch: e.matmul(out=bk[0:64, o:o + 512], lhsT=Wkv[:, h * 128:h * 128 + 64], rhs=ckv[:, ch * 512:(ch + 1) * 512],
                                                                             start=True, stop=True), reads=(Wkv.b, ckv.b), writes=(bk.b,))
                            cx.op("dve", lambda e, bk=bk, o=o, ch=ch: e.tensor_copy(out=KT[0:64, ch * 512:(ch + 1) * 512], in_=bk[0:64, o:o + 512]),
                                  reads=(bk.b,), writes=(KT.b,))
                            yield
                        for g8 in range(8):
                            bk, o = half_bank()
                            for j in range(8):
                                kt = g8 * 8 + j
                                cx.op("pe", lambda e, bk=bk, o=o, kt=kt, j=j: e.matmul(out=bk[:, o + j * 64:o + (j + 1) * 64], lhsT=ckv[:, kt * 128:(kt + 1) * 128],
                                                                                      rhs=Wkv[:, h * 128 + 64:h * 128 + 128], start=True, stop=True),
                                      inc=(j == 7), reads=(Wkv.b, ckv.b), writes=(bk.b,))
                            cx.op("dve", lambda e, bk=bk, o=o, g8=g8: e.tensor_copy(out=V[:, g8 * 8:(g8 + 1) * 8, 0:64],
                                                                                   in_=bk[:, o:o + 512].rearrange("p (j c) -> p j c", j=8)),
                                  reads=(bk.b,), writes=(V.b,))
                            yield
                        for ch in range(8):
                            half_state[1] = 2
                            bk, oA = half_bank()
                            _, oB = half_bank()
                            for which, o in ((0, oA), (1, oB)):
                                for c in range(3):
                                    cx.op("pe", lambda e, which=which, o=o, c=c, ch=ch: e.matmul(
                                        out=bk[0:96, o:o + 512], lhsT=Wq[:, which, c, h * 96:(h + 1) * 96], rhs=cqn[:, c, ch * 512:(ch + 1) * 512],
                                        start=(c == 0), stop=(c == 2)), inc=(c == 2), reads=(Wq.b, cqn.b), writes=(bk.b,))
                            cx.op("dve", lambda e, ch=ch: e.tensor_copy(out=QT[0:64, ch * 512:(ch + 1) * 512], in_=bk[0:64, oA:oA + 512]),
                                  reads=(bk.b,), writes=(QT.b,))
                            cx.op("dve", lambda e, ch=ch: e.tensor_tensor(out=r1[64:96, :], in0=bk[64:96, oA:oA + 512], in1=ropet[64:96, 0, ch * 512:(ch + 1) * 512],
                                                                          op=ALU.mult), reads=(bk.b, ropet.b), writes=(r1.b,))
                            cx.op("dve", lambda e, ch=ch: e.tensor_tensor(out=r2[64:96, :], in0=bk[64:96, oB:oB + 512], in1=ropet[64:96, 1, ch * 512:(ch + 1) * 512],
                                                                          op=ALU.mult), reads=(bk.b, ropet.b), writes=(r2.b,))
                            cx.op("dve", lambda e, ch=ch: e.tensor_tensor(out=QT[64:96, ch * 512:(ch + 1) * 512], in0=r1[64:96, :], in1=r2[64:96, :], op=ALU.add),
                                  reads=(r1.b, r2.b), writes=(QT.b,))
                            yield

                    def prep_head(h):
                        for _ in prep_head_gen(h):
                            pass

                    LOOK = 2
                    items = [(h, qc, pr) for h in range(8) for qc in range(8) for pr in range(32)]
                    pend = []
                    inflight = []
                    cur_po = {}

                    def s_stage(h, qc, pr):
                        KT, QT = KTs[h % 2], QTs[h % 2]
                        ps2 = pS2.next()
                        pt2 = PT2.next()
                        for j in range(2):
                            kt = 2 * pr + j
                            cx.op("pe", lambda e, j=j, kt=kt: e.matmul(out=ps2[:, j * 512:(j + 1) * 512], lhsT=KT[:, kt * 128:(kt + 1) * 128],
                                                                       rhs=QT[:, qc * 512:(qc + 1) * 512], start=True, stop=True),
                                  inc=(j == 1), reads=(KT.b, QT.b), writes=(ps2.b,))
                        cx.op("act", lambda e: e.activation(out=pt2[:], in_=ps2[:], func=AF.Exp, scale=MLA_SCALE),
                              reads=(ps2.b,), writes=(pt2.b,))
                        return pt2

                    def pv_stage(h, qc, pr, pt2, step):
                        V = Vs[h % 2]
                        if pr == 0:
                            cur_po[(h, qc)] = pO.next()
                        po = cur_po[(h, qc)]
                        for j in range(2):
                            kt = 2 * pr + j
                            cx.op("pe", lambda e, j=j, kt=kt: e.matmul(out=po[:, :], lhsT=V[:, kt, :], rhs=pt2[:, j * 512:(j + 1) * 512],
                                                                       start=(kt == 0), stop=(kt == 63)),
                                  inc=(j == 1), reads=(V.b, pt2.b), writes=(po.b,))
                        if pr == 31:
                            osb = osbs.next()
                            cx.op("dve", lambda e: e.tensor_copy(out=osb[:], in_=po[0:65, :]), reads=(po.b,), writes=(osb.b,))
                            cx.op("act", lambda e: e.activation(out=osb[64:65, :], in_=osb[64:65, :], func=AF.Ln), reads=(osb.b,), writes=(osb.b,))
                            cx.op("act", lambda e: e.activation(out=osb[64:65, :], in_=osb[64:65, :], func=AF.Exp, scale=-1.0), reads=(osb.b,), writes=(osb.b,))

                            def fin(h=h, qc=qc, osb=osb):
                                bc = pX.next()
                                cx.op("pe", lambda e: e.matmul(out=bc[0:64, :], lhsT=ones1[64:65, :], rhs=osb[64:65, :], start=True, stop=True),
                                      reads=(ones1.b, osb.b), writes=(bc.b,))
                                o = ost.next()
                                cx.op("dve", lambda e: e.tensor_tensor(out=o[:], in0=osb[0:64, :], in1=bc[0:64, :], op=ALU.mult),
                                      reads=(osb.b, bc.b), writes=(o.b,))
                                cx.dma("pool", oaT_d[h * 64:(h + 1) * 64, qc * 512:(qc + 1) * 512], o[:], reads=(o.b,), writes=(oaT_d.b,))
                            pend.append((step + 6, fin))

                    prep_head(0)
                    prep_head(1)
                    prep_gen = [None]
                    n_it = len(items)
                    for step in range(n_it + LOOK + 8):
                        if step < n_it:
                            h, qc, pr = items[step]
                            inflight.append((items[step], s_stage(h, qc, pr)))
                        if step >= LOOK and inflight and step - LOOK < n_it:
                            (h, qc, pr), pt2 = inflight.pop(0)
                            pv_stage(h, qc, pr, pt2, step)
                            if qc == 7 and pr == 31 and h + 2 < 8:
                                assert prep_gen[0] is None
                                prep_gen[0] = prep_head_gen(h + 2)
                        if prep_gen[0] is not None and step % 7 == 0:
                            try:
                                next(prep_gen[0])
                            except StopIteration:
                                prep_gen[0] = None
                        while pend and pend[0][0] <= step:
                            pend.pop(0)[1]()
                    assert not pend and not inflight and prep_gen[0] is None

            if want("P4"):
                with Phase(cx) as ph:
                    kcT = ph.sb("kcT", [128, TOK + 256], BF16)
                    cx.dma("sp", kcT[:], kcT_d.t, reads=(kcT_d.b,), writes=(kcT.b,))
                    vc = ph.sb("vc", [128, 34, 2, 65], BF16)
                    cx.op("pool", lambda e: e.memset(vc[:, :, :, 64:65], 1.0), writes=(vc.b,))
                    for kvh in range(2):
                        cx.dma("sp", vc[:, :, kvh, 0:64], vc_d.t.rearrange("(t p) c -> p t c", p=128)[:, :, kvh * 64:(kvh + 1) * 64],
                               reads=(vc_d.b,), writes=(vc.b,))
                    qcT = ph.sb("qcT", [128, 4, TOK], BF16)
                    cx.dma("sp", qcT[:], qcT_d.t, reads=(qcT_d.b,), writes=(qcT.b,))
                    posq_i = ph.sb("posq_i", [128, TOK], I32)
                    cx.dma("sp", posq_i[:], pos_in.partition_broadcast(128), reads=(), writes=(posq_i.b,))
                    posq = ph.sb("posq", [128, TOK], F32)
                    cx.op("dve", lambda e: e.tensor_copy(out=posq[:], in_=posq_i[:]), reads=(posq_i.b,), writes=(posq.b,))
                    posk_i = ph.sb("posk_i", [128, 34], I32)
                    cx.dma("sp", posk_i[:], posx_in.rearrange("t p -> p t"), reads=(), writes=(posk_i.b,), allow_slow_non_contiguous=True)
                    posk = ph.sb("posk", [128, 34], F32)
                    cx.op("dve", lambda e: e.tensor_copy(out=posk[:], in_=posk_i[:]), reads=(posk_i.b,), writes=(posk.b,))
                    cx.op("dve", lambda e: e.tensor_scalar(out=posk[:], in0=posk[:], scalar1=-1.0, scalar2=None, op0=ALU.mult),
                          reads=(posk.b,), writes=(posk.b,))
                    wm = ph.sb("wm", [128, 4, 128], F32)
                    cx.dma("sp", wm[:], wmask_in.rearrange("w p c -> p w c"), reads=(), writes=(wm.b,))
                    es_ = ph.sb("es", [65, 8], F32)
                    cx.dma("sp", es_[64:65, :], sink_in[l:l + 1, :], reads=(), writes=(es_.b,))
                    cx.op("act", lambda e: e.activation(out=es_[64:65, :], in_=es_[64:65, :], func=AF.Exp), reads=(es_.b,), writes=(es_.b,))
                    ones1 = ph.sb("ones1", [65, 64], F32)
                    cx.op("dve", lambda e: e.memset(ones1[:], 1.0), writes=(ones1.b,))
                    slopeT = ph.sb("slopeT", [128, 2, 4, 128], F32)
                    for kvh in range(2):
                        for g in range(4):
                            cx.op("dve", lambda e, kvh=kvh, g=g: e.memset(slopeT[:, kvh, g, :], -SLOPES[kvh * 4 + g]), writes=(slopeT.b,))
                    Ds = Rot([ph.sb(f"Dm{i}", [128, 128], F32) for i in range(6)])
                    biases = Rot([ph.sb(f"bias{i}", [128, 512], F32) for i in range(12)])
                    tmps = Rot([ph.sb(f"tmp{i}", [128, 512], F32) for i in range(3)])
                    PTs = Rot([ph.sb(f"PT{i}", [128, 512], BF16) for i in range(4)])
                    osbs = Rot([ph.sb(f"osb{i}", [65, 512], F32) for i in range(2)])
                    ost = Rot([ph.sb(f"ost{i}", [64, 512], BF16) for i in range(2)])
                    pS = Rot([ph.ps(f"pS{i}", [128, 512]) for i in range(4)])
                    pO = Rot([ph.ps(f"pO{i}", [128, 512]) for i in range(2)])
                    pX = Rot([ph.ps(f"pX{i}", [128, 512]) for i in range(2)])
                    ocv = ocT_d.t.rearrange("(h d) t -> d h t", d=64)
                    LOOK = 2
                    steps = [(j, kvh, kk) for j in range(NT) for kvh in range(2) for kk in range(3)]
                    blk_bias = {}
                    cur_po = {}
                    pend = []
                    inflight = []

                    def block_prep(j):
                        bl = {}
                        for kk in range(3):
                            tt = j + kk
                            Dm = Ds.next()
                            cx.op("act", lambda e, Dm=Dm, tt=tt: e.activation(out=Dm[:], in_=posq[:, j * 128:(j + 1) * 128], func=AF.Abs,
                                                                              bias=posk[:, tt:tt + 1], scale=1.0),
                                  reads=(posq.b, posk.b), writes=(Dm.b,))
                            mi = None
                            if kk == 0:
                                mi = 2 if j == 0 else 0
                            elif kk == 2:
                                mi = 3 if j == NT - 1 else 1
                            if mi is not None:
                                cx.op("dve", lambda e, Dm=Dm, mi=mi: e.tensor_tensor(out=Dm[:], in0=Dm[:], in1=wm[:, mi, :], op=ALU.add),
                                      reads=(Dm.b, wm.b), writes=(Dm.b,))
                            for kvh in range(2):
                                bt = biases.next()
                                cx.op("dve", lambda e, bt=bt, Dm=Dm, kvh=kvh: e.tensor_tensor(
                                    out=bt[:].rearrange("p (g q) -> p g q", g=4), in0=Dm[:].unsqueeze(1).to_broadcast([128, 4, 128]),
                                    in1=slopeT[:, kvh, :, :], op=ALU.mult), reads=(Dm.b, slopeT.b), writes=(bt.b,))
                                bl[(kvh, kk)] = bt
                        blk_bias[j] = bl

                    def s_stage(j, kvh, kk):
                        tt = j + kk
                        ps_ = pS.next()
                        for g in range(4):
                            cx.op("pe", lambda e, g=g: e.matmul(
                                out=ps_[:, g * 128:(g + 1) * 128], lhsT=kcT[kvh * 64:(kvh + 1) * 64, tt * 128:(tt + 1) * 128],
                                rhs=qcT[kvh * 64:(kvh + 1) * 64, g, j * 128:(j + 1) * 128], start=True, stop=True),
                                inc=(g == 3), reads=(kcT.b, qcT.b), writes=(ps_.b,))
                        tmp = tmps.next()
                        bt = blk_bias[j][(kvh, kk)]
                        cx.op("dve", lambda e: e.tensor_tensor(out=tmp[:], in0=ps_[:], in1=bt[:], op=ALU.add),
                              reads=(ps_.b, bt.b), writes=(tmp.b,))
                        pt = PTs.next()
                        cx.op("act", lambda e: e.activation(out=pt[:], in_=tmp[:], func=AF.Exp), reads=(tmp.b,), writes=(pt.b,))
                        return pt

                    def pv_stage(j, kvh, kk, pt, step):
                        tt = j + kk
                        if kk == 0:
                            cur_po[(j, kvh)] = pO.next()
                        po = cur_po[(j, kvh)]
                        cx.op("pe", lambda e: e.matmul(out=po[0:65, :], lhsT=vc[:, tt, kvh, :], rhs=pt[:], start=(kk == 0), stop=(kk == 2)),
                              reads=(vc.b, pt.b), writes=(po.b,))
                        if kk == 2:
                            osb = osbs.next()
                            cx.op("act", lambda e: e.copy(out=osb[:], in_=po[0:65, :]), reads=(po.b,), writes=(osb.b,))
                            cx.op("dve", lambda e: e.tensor_tensor(out=osb[64:65, :].rearrange("p (g q) -> p g q", g=4),
                                                                   in0=osb[64:65, :].rearrange("p (g q) -> p g q", g=4),
                                                                   in1=es_[64:65, kvh * 4:(kvh + 1) * 4].unsqueeze(2).to_broadcast([1, 4, 128]), op=ALU.add),
                                  reads=(osb.b, es_.b), writes=(osb.b,))
                            cx.op("act", lambda e: e.activation(out=osb[64:65, :], in_=osb[64:65, :], func=AF.Ln), reads=(osb.b,), writes=(osb.b,))
                            cx.op("act", lambda e: e.activation(out=osb[64:65, :], in_=osb[64:65, :], func=AF.Exp, scale=-1.0), reads=(osb.b,), writes=(osb.b,))

                            def fin(osb=osb):
                                bc = pX.next()
                                cx.op("pe", lambda e: e.matmul(out=bc[0:64, :], lhsT=ones1[64:65, :], rhs=osb[64:65, :], start=True, stop=True),
                                      reads=(ones1.b, osb.b), writes=(bc.b,))
                                o = ost.next()
                                cx.op("dve", lambda e: e.tensor_tensor(out=o[:], in0=osb[0:64, :], in1=bc[0:64, :], op=ALU.mult),
                                      reads=(osb.b, bc.b), writes=(o.b,))
                                cx.dma("pool", ocv[:, kvh * 4:(kvh + 1) * 4, j * 128:(j + 1) * 128], o[:].rearrange("p (g t) -> p g t", g=4),
                                       reads=(o.b,), writes=(ocT_d.b,))
                            pend.append((step + 3, fin))

                    block_prep(0)
                    n_st = len(steps)
                    for step in range(n_st + LOOK + 6):
                        if step < n_st:
                            j, kvh, kk = steps[step]
                            if kvh == 0 and kk == 0 and j + 1 < NT:
                                block_prep(j + 1)
                            inflight.append((steps[step], s_stage(j, kvh, kk)))
                        if step >= LOOK and inflight and step - LOOK < n_st:
                            (j, kvh, kk), pt = inflight.pop(0)
                            pv_stage(j, kvh, kk, pt, step)
                        while pend and pend[0][0] <= step:
                            pend.pop(0)[1]()
                    assert not pend and not inflight

            if want("P6"):
                with Phase(cx) as ph:
                    stage = mk_stage(ph, 6)
                    Wg = ph.sb("Wg", [128, 8, 3 * D], BF16)
                    load_w_kc(ph, Wg, w_gates[l], 8, 3 * D, stage)
                    Wa = ph.sb("Wa", [128, 4, D], BF16)
                    load_w_kc(ph, Wa, w_a[l], 4, D, stage)
                    Wc = ph.sb("Wc", [128, 4, D], BF16)
                    load_w_kc(ph, Wc, w_c[l], 4, D, stage)
                    Wo = ph.sb("Wo", [128, 8, D], BF16)
                    load_w_kc(ph, Wo, w_out[l], 8, D, stage)
                    ident = ph.sb("ident", [128, 128], BF16)
                    cx.dma("sp", ident[:], ident_in, reads=(), writes=(ident.b,))
                    gpost = gain_bc(ph, "gpost", g_mix_post[l:l + 1, :])
                    hTt = [ph.sb(f"hTt{i}", [128, 8, 128], BF16) for i in range(2)]
                    oat = [ph.sb(f"oat{i}", [128, 4, 128], BF16) for i in range(2)]
                    oct_ = [ph.sb(f"oct{i}", [128, 4, 128], BF16) for i in range(2)]
                    brs = [ph.sb(f"brs{i}", [128, D], F32) for i in range(2)]
                    xts = [ph.sb(f"xt{i}", [128, D], F32) for i in range(4)]
                    sg = [ph.sb(f"sg{i}", [128, D], F32) for i in range(3)]
                    mg = ph.sb("mg", [128, D], F32)
                    tm = ph.sb("tm", [128, D], F32)
                    mbs = [ph.sb(f"mb{i}", [128, D], BF16) for i in range(2)]
                    mTs = [ph.sb(f"mT{i}", [128, 8, 128], BF16) for i in range(2)]
                    junk = ph.sb("junk", [128, D], F32)
                    sss = [ph.sb(f"ss{i}", [128, 1], F32) for i in range(2)]
                    xos = [ph.sb(f"xo{i}", [128, D], F32) for i in range(2)]
                    pG = Rot([ph.ps(f"pG{i}", [128, 1024]) for i in range(3)])
                    pT = ph.ps("pT", [128, 8, 128], BF16)
                    hv = hT_d.t.rearrange("(kc p) t -> p kc t", p=128)
                    oav = oaT_d.t.rearrange("(kc p) t -> p kc t", p=128)
                    ocv2 = ocT_d.t.rearrange("(kc p) t -> p kc t", p=128)

                    def st_a(t):
                        sl = slice(t * 128, (t + 1) * 128)
                        hT, oa, oc, br, xt = hTt[t % 2], oat[t % 2], oct_[t % 2], brs[t % 2], xts[t % 4]
                        cx.dma("sp", hT[:], hv[:, :, sl], reads=(hT_d.b,), writes=(hT.b,))
                        cx.dma("sp", oa[:], oav[:, :, sl], reads=(oaT_d.b,), writes=(oa.b,))
                        cx.dma("sp", oc[:], ocv2[:, :, sl], reads=(ocT_d.b,), writes=(oc.b,))
                        cx.dma("sp", br[:], Brs_d[sl, :], reads=(Brs_d.b,), writes=(br.b,))
                        cx.dma("sp", xt[:], x_src[sl, :], reads=(x_src_b,), writes=(xt.b,))

                    def st_b(t):
                        hT, oa, oc, br, mb = hTt[t % 2], oat[t % 2], oct_[t % 2], brs[t % 2], mbs[t % 2]
                        for gi in range(3):
                            pg = pG.next()
                            for n in range(2):
                                for kc in range(8):
                                    cx.op("pe", lambda e, pg=pg, kc=kc, n=n, gi=gi: e.matmul(
                                        out=pg[:, n * 512:(n + 1) * 512], lhsT=hT[:, kc, :], rhs=Wg[:, kc, gi * D + n * 512:gi * D + (n + 1) * 512],
                                        start=(kc == 0), stop=(kc == 7)), inc=(kc == 7), reads=(hT.b, Wg.b), writes=(pg.b,))
                            cx.op("act", lambda e, pg=pg, gi=gi: e.activation(out=sg[gi][:], in_=pg[:], func=AF.Sigmoid),
                                  reads=(pg.b,), writes=(sg[gi].b,))
                        pa = pG.next()
                        for n in range(2):
                            for k4 in range(4):
                                cx.op("pe", lambda e, k4=k4, n=n: e.matmul(out=pa[:, n * 512:(n + 1) * 512], lhsT=oa[:, k4, :],
                                                                          rhs=Wa[:, k4, n * 512:(n + 1) * 512], start=(k4 == 0), stop=(k4 == 3)),
                                      inc=(k4 == 3), reads=(oa.b, Wa.b), writes=(pa.b,))
                        cx.op("dve", lambda e: e.tensor_tensor(out=mg[:], in0=pa[:], in1=sg[0][:], op=ALU.mult),
                              reads=(pa.b, sg[0].b), writes=(mg.b,))
                        pc = pG.next()
                        for n in range(2):
                            for k4 in range(4):
                                cx.op("pe", lambda e, k4=k4, n=n: e.matmul(out=pc[:, n * 512:(n + 1) * 512], lhsT=oc[:, k4, :],
                                                                          rhs=Wc[:, k4, n * 512:(n + 1) * 512], start=(k4 == 0), stop=(k4 == 3)),
                                      inc=(k4 == 3), reads=(oc.b, Wc.b), writes=(pc.b,))
                        cx.op("dve", lambda e: e.tensor_tensor(out=tm[:], in0=pc[:], in1=sg[2][:], op=ALU.mult),
                              reads=(pc.b, sg[2].b), writes=(tm.b,))
                        cx.op("dve", lambda e: e.tensor_tensor(out=mg[:], in0=mg[:], in1=tm[:], op=ALU.add), reads=(mg.b, tm.b), writes=(mg.b,))
                        cx.op("dve", lambda e: e.tensor_tensor(out=br[:], in0=br[:], in1=sg[1][:], op=ALU.mult),
                              reads=(br.b, sg[1].b), writes=(br.b,))
                        cx.op("dve", lambda e: e.tensor_tensor(out=mb[:], in0=mg[:], in1=br[:], op=ALU.add),
                              reads=(mg.b, br.b), writes=(mb.b,))

                    def st_c(t):
                        mb, mT = mbs[t % 2], mTs[t % 2]
                        for kc in range(8):
                            cx.op("pe", lambda e, kc=kc: e.transpose(out=pT[:, kc, :], in_=mb[:, kc * 128:(kc + 1) * 128], identity=ident[:]), inc=(kc == 7), reads=(mb.b, ident.b), writes=(pT.b,))
                        cx.op("act", lambda e: e.copy(out=mT[:], in_=pT[:]), reads=(pT.b,), writes=(mT.b,))

                    def st_d(t):
                        sl = slice(t * 128, (t + 1) * 128)
                        mT, xt, ss, xo_ = mTs[t % 2], xts[t % 4], sss[t % 2], xos[t % 2]
                        py = pG.next()
                        for n in range(2):
                            for kc in range(8):
                                cx.op("pe", lambda e, kc=kc, n=n: e.matmul(out=py[:, n * 512:(n + 1) * 512], lhsT=mT[:, kc, :],
                                                                          rhs=Wo[:, kc, n * 512:(n + 1) * 512], start=(kc == 0), stop=(kc == 7)),
                                      inc=(kc == 7), reads=(mT.b, Wo.b), writes=(py.b,))
                        rms_rstd_tokmajor(ph, py[:], (py.b,), junk, ss, D)
                        cx.op("dve", lambda e: e.scalar_tensor_tensor(out=xo_[:], in0=py[:], scalar=ss[:, 0:1], in1=gpost[:],
                                                                      op0=ALU.mult, op1=ALU.mult),
                              reads=(py.b, ss.b, gpost.b), writes=(xo_.b,))
                        cx.op("dve", lambda e: e.tensor_tensor(out=xo_[:], in0=xo_[:], in1=xt[:], op=ALU.add),
                              reads=(xo_.b, xt.b), writes=(xo_.b,))
                        cx.dma("pool", xb[sl, :], xo_[:], reads=(xo_.b,), writes=(xb.b,))

                    skew(NT, [st_a, st_b, st_c, st_d])

            if want("P7"):
                with Phase(cx) as ph:
                    stage = mk_stage(ph, 6)
                    Wfg = ph.sb("Wfg", [128, 8, FFN], BF16)
                    load_w_kc(ph, Wfg, w_fg[l], 8, FFN, stage)
                    Wfu = ph.sb("Wfu", [128, 8, FFN], BF16)
                    load_w_kc(ph, Wfu, w_fu[l], 8, FFN, stage)
                    ident = ph.sb("ident", [128, 128], BF16)
                    cx.dma("sp", ident[:], ident_in, reads=(), writes=(ident.b,))
                    gpre = gain_bc(ph, "gfpre", g_ffn_pre[l:l + 1, :])
                    xts = [ph.sb(f"xt{i}", [128, D], F32) for i in range(4)]

                    def ld_x(t):
                        if t < NT:
                            cx.dma("sp", xts[t % 4][:], xb[t * 128:(t + 1) * 128, :], reads=(xb.b,), writes=(xts[t % 4].b,))
                    ld_x(0)
                    ld_x(1)
                    junk = ph.sb("junk", [128, D], F32)
                    sss = Rot([ph.sb(f"ss{i}", [128, 1], F32) for i in range(3)])
                    hbs = Rot([ph.sb(f"hb{i}", [128, D], BF16) for i in range(2)])
                    hTs = Rot([ph.sb(f"hT{i}", [128, 8, 512], BF16, nsub=4) for i in range(2)])
                    ast = Rot([ph.sb(f"ast{i}", [128, 512], BF16) for i in range(3)])
                    sil = Rot([ph.sb(f"sil{i}", [128, 512], F32) for i in range(2)])
                    pGU = Rot([ph.ps(f"pGU{i}", [128, 512]) for i in range(6)])
                    pT = ph.ps("pT", [128, 8, 128], BF16)
                    hT_l = hTs.items

                    def tile_front(t):
                        ci, tt = t // 4, t % 4
                        hT = hT_l[ci % 2]
                        xt = xts[t % 4]
                        ss = sss.next(); hb = hbs.next()
                        ld_x(t + 2)
                        rms_rstd_tokmajor(ph, xt[:], (xt.b,), junk, ss, D)
                        cx.op("dve", lambda e: e.scalar_tensor_tensor(
                            out=hb[:], in0=xt[:], scalar=ss[:, 0:1], in1=gpre[:], op0=ALU.mult, op1=ALU.mult),
                            reads=(xt.b, ss.b, gpre.b), writes=(hb.b,))
                        for kc in range(8):
                            cx.op("pe", lambda e, kc=kc: e.transpose(out=pT[:, kc, :], in_=hb[:, kc * 128:(kc + 1) * 128], identity=ident[:]),
                                  inc=(kc == 7), reads=(hb.b, ident.b), writes=(pT.b,))
                        cx.op("act", lambda e: e.copy(out=hT[:, :, tt * 128:(tt + 1) * 128], in_=pT[:]),
                              reads=(pT.b,), writes=(hT.bs[tt],))

                    for tt in range(4):
                        tile_front(tt)
                    for ci in range(NCH):
                        hT = hT_l[ci % 2]
                        for f in range(NF):
                            if ci + 1 < NCH and f in (3, 8, 13, 18):
                                tile_front((ci + 1) * 4 + (f - 3) // 5)
                            pg = pGU.next(); pu = pGU.next()
                            for pp, W in ((pg, Wfg), (pu, Wfu)):
                                for kc in range(8):
                                    cx.op("pe", lambda e, pp=pp, W=W, kc=kc, f=f, hT=hT: e.matmul(
                                        out=pp[:], lhsT=W[:, kc, f * 128:(f + 1) * 128], rhs=hT[:, kc, :], start=(kc == 0), stop=(kc == 7)),
                                        inc=(kc == 7), reads=tuple(hT.bs) + (W.b,), writes=(pp.b,))
                            s_ = sil.next()
                            cx.op("act", lambda e, s_=s_, pg=pg: e.activation(out=s_[:], in_=pg[:], func=AF.Silu), reads=(pg.b,), writes=(s_.b,))
                            a_ = ast.next()
                            cx.op("dve", lambda e, s_=s_, pu=pu, a_=a_: e.tensor_tensor(out=a_[:], in0=pu[:], in1=s_[:], op=ALU.mult),
                                  reads=(pu.b, s_.b), writes=(a_.b,))
                            cx.dma("pool", act_d[f * 128:(f + 1) * 128, ci * 512:(ci + 1) * 512], a_[:], reads=(a_.b,), writes=(act_d.b,))
                with Phase(cx) as ph:
                    stage = mk_stage(ph, 6)
                    Wfd = ph.sb("Wfd", [128, NF, D], BF16)
                    load_w_kc(ph, Wfd, w_fd[l], NF, D, stage)
                    gpost = gain_bc(ph, "gfpost", g_ffn_post[l:l + 1, :])
                    aTs = Rot([ph.sb(f"aT{i}", [128, NF, 512], BF16) for i in range(2)])
                    xts = Rot([ph.sb(f"xt{i}", [128, D], F32) for i in range(3)])
                    junk = ph.sb("junk", [128, D], F32)
                    sss = Rot([ph.sb(f"ss{i}", [128, 1], F32) for i in range(3)])
                    xo = Rot([ph.sb(f"xo{i}", [128, D], F32) for i in range(2)])
                    pD = Rot([ph.ps(f"pD{i}", [128, 1024]) for i in range(3)])
                    av = act_d.t.rearrange("(f p) t -> p f t", p=128)
                    for ci in range(NCH):
                        aT = aTs.next()
                        cx.dma("sp", aT[:], av[:, :, ci * 512:(ci + 1) * 512], reads=(act_d.b,), writes=(aT.b,))
                        for tt in range(4):
                            t = ci * 4 + tt
                            xt = xts.next()
                            cx.dma("sp", xt[:], xb[t * 128:(t + 1) * 128, :], reads=(xb.b,), writes=(xt.b,))
                            pd = pD.next()
                            for n in range(2):
                                for f in range(NF):
                                    cx.op("pe", lambda e, pd=pd, f=f, n=n, tt=tt, aT=aT: e.matmul(
                                        out=pd[:, n * 512:(n + 1) * 512], lhsT=aT[:, f, tt * 128:(tt + 1) * 128], rhs=Wfd[:, f, n * 512:(n + 1) * 512],
                                        start=(f == 0), stop=(f == NF - 1)), inc=(f == NF - 1), reads=(aT.b, Wfd.b), writes=(pd.b,))
                            ss = sss.next()
                            rms_rstd_tokmajor(ph, pd[:], (pd.b,), junk, ss, D)
                            xo_ = xo.next()
                            cx.op("dve", lambda e, pd=pd, ss=ss, xo_=xo_: e.scalar_tensor_tensor(out=xo_[:], in0=pd[:], scalar=ss[:, 0:1], in1=gpost[:],
                                                                                                op0=ALU.mult, op1=ALU.mult),
                                  reads=(pd.b, ss.b, gpost.b), writes=(xo_.b,))
                            cx.op("dve", lambda e, xo_=xo_, xt=xt: e.tensor_tensor(out=xo_[:], in0=xo_[:], in1=xt[:], op=ALU.add),
                                  reads=(xo_.b, xt.b), writes=(xo_.b,))
                            cx.dma("pool", xa[t * 128:(t + 1) * 128, :], xo_[:], reads=(xo_.b,), writes=(xa.b,))

            if want("P8"):
                last = (l == nlayers - 1)
                with Phase(cx) as ph:
                    stage = mk_stage(ph, 6)
                    Wpg = ph.sb("Wpg", [128, 8, D], BF16)
                    load_w_kc(ph, Wpg, w_pg[l], 8, D, stage)
                    Wpp = ph.sb("Wpp", [128, 2, D], BF16)
                    load_w_kc(ph, Wpp, w_pp[l], 2, D, stage)
                    ident = ph.sb("ident", [128, 128], BF16)
                    cx.dma("sp", ident[:], ident_in, reads=(), writes=(ident.b,))
                    gple = gain_bc(ph, "gple", g_ple[l:l + 1, :])
                    xts = [ph.sb(f"xt{i}", [128, D], F32) for i in range(7)]
                    pts = [ph.sb(f"pt{i}", [128, 256], F32) for i in range(4)]
                    xbf = [ph.sb(f"xbf{i}", [128, D + 256], BF16) for i in range(2)]
                    xTs = [ph.sb(f"xT{i}", [128, 10, 128], BF16) for i in range(2)]
                    sgts = [ph.sb(f"sgt{i}", [128, D], F32) for i in range(2)]
                    ets = [ph.sb(f"et{i}", [128, D], F32) for i in range(2)]
                    junk = ph.sb("junk", [128, D], F32)
                    sss = [ph.sb(f"ss{i}", [128, 1], F32) for i in range(2)]
                    xos = [ph.sb(f"xo{i}", [128, D], F32) for i in range(2)]
                    pGs = [ph.ps(f"pG{i}", [128, 1024]) for i in range(2)]
                    pTs = [ph.ps(f"pT{i}", [128, 16, 128], BF16) for i in range(2)]

                    def st_ld(t):
                        sl = slice(t * 128, (t + 1) * 128)
                        xt, pt = xts[t % 7], pts[t % 4]
                        cx.dma("sp", xt[:], xa[sl, :], reads=(xa.b,), writes=(xt.b,))
                        cx.dma("sp", pt[:], p_in[l, sl, :], reads=(), writes=(pt.b,))

                    def st_a(t):
                        xt, pt, xb_ = xts[t % 7], pts[t % 4], xbf[t % 2]
                        cx.op("dve", lambda e: e.tensor_copy(out=xb_[:, 0:D], in_=xt[:]), reads=(xt.b,), writes=(xb_.b,))
                        cx.op("act", lambda e: e.copy(out=xb_[:, D:D + 256], in_=pt[:]), reads=(pt.b,), writes=(xb_.b,))

                    def st_b(t):
                        xb_, xT_, pT = xbf[t % 2], xTs[t % 2], pTs[t % 2]
                        for kc in range(10):
                            cx.op("pe", lambda e, kc=kc: e.transpose(out=pT[:, kc, :], in_=xb_[:, kc * 128:(kc + 1) * 128], identity=ident[:]), inc=(kc == 9), reads=(xb_.b, ident.b), writes=(pT.b,))
                        cx.op("act", lambda e: e.copy(out=xT_[:], in_=pT[:, 0:10, :]), reads=(pT.b,), writes=(xT_.b,))

                    def st_c(t):
                        xT_, sgt, et = xTs[t % 2], sgts[t % 2], ets[t % 2]
                        pg = pGs[0]
                        for n in range(2):
                            for kc in range(8):
                                cx.op("pe", lambda e, kc=kc, n=n: e.matmul(out=pg[:, n * 512:(n + 1) * 512], lhsT=xT_[:, kc, :],
                                                                          rhs=Wpg[:, kc, n * 512:(n + 1) * 512], start=(kc == 0), stop=(kc == 7)),
                                      inc=(kc == 7), reads=(xT_.b, Wpg.b), writes=(pg.b,))
                        cx.op("act", lambda e: e.activation(out=sgt[:], in_=pg[:], func=AF.Sigmoid), reads=(pg.b,), writes=(sgt.b,))
                        pe_ = pGs[1]
                        for n in range(2):
                            for kc in range(2):
                                cx.op("pe", lambda e, kc=kc, n=n: e.matmul(out=pe_[:, n * 512:(n + 1) * 512], lhsT=xT_[:, 8 + kc, :],
                                                                          rhs=Wpp[:, kc, n * 512:(n + 1) * 512], start=(kc == 0), stop=(kc == 1)),
                                      inc=(kc == 1), reads=(xT_.b, Wpp.b), writes=(pe_.b,))
                        cx.op("dve", lambda e: e.tensor_tensor(out=et[:], in0=pe_[:], in1=sgt[:], op=ALU.mult),
                              reads=(pe_.b, sgt.b), writes=(et.b,))

                    def st_d(t):
                        sl = slice(t * 128, (t + 1) * 128)
                        xt, et, ss, xo_ = xts[t % 7], ets[t % 2], sss[t % 2], xos[t % 2]
                        rms_rstd_tokmajor(ph, et[:], (et.b,), junk, ss, D)
                        cx.op("dve", lambda e: e.scalar_tensor_tensor(out=xo_[:], in0=et[:], scalar=ss[:, 0:1], in1=gple[:],
                                                                      op0=ALU.mult, op1=ALU.mult),
                              reads=(et.b, ss.b, gple.b), writes=(xo_.b,))
                        cx.op("dve", lambda e: e.tensor_tensor(out=xo_[:], in0=xo_[:], in1=xt[:], op=ALU.add),
                              reads=(xo_.b, xt.b), writes=(xo_.b,))
                        if last:
                            cx.dma("pool", y_out[sl, :], xo_[:], reads=(xo_.b,), writes=(ybuf,))
                        else:
                            cx.dma("pool", xc[sl, :], xo_[:], reads=(xo_.b,), writes=(xc.b,))

                    skew(NT, [st_ld, st_a, st_b, st_c, st_d], [0, 2, 3, 4, 5])
        cx.barrier()
    return nc


def _consts(half):
    bf = ml_dtypes.bfloat16
    ident = np.eye(128, dtype=np.float32).astype(bf)
    invf = (np.float32(10000.0) ** (-np.arange(16, dtype=np.float32) / np.float32(16))).astype(np.float32)
    invf = np.concatenate([invf, invf]).reshape(32, 1)
    c = np.arange(256)
    th = 2.0 * np.pi * np.outer(c, c) / 256.0
    cdft = np.stack([np.cos(th) / 16.0, -np.sin(th) / 16.0]).astype(np.float32).astype(bf)
    s1 = np.arange(128)[None, :, None]
    k1 = np.arange(256)[None, None, :]
    s2 = np.arange(32)[:, None, None]
    ph = (4096 * half * k1 + 32 * s1 * k1 + s2 * k1) % 8192
    ang = 2.0 * np.pi * ph / 8192.0
    nrm = 1.0 / np.sqrt(8192.0)
    G = np.stack([np.cos(ang) * nrm, -np.sin(ang) * nrm, np.sin(ang) * nrm], axis=1).astype(np.float32).astype(bf)
    a2 = 2.0 * np.pi * np.outer(np.arange(32), np.arange(32)) / 32.0
    Er = np.kron(np.eye(4), np.cos(a2))
    Es = np.kron(np.eye(4), np.sin(a2))
    E = np.stack([Er, Es]).astype(np.float32).astype(bf)
    BIG = 1.0e7
    k = np.arange(128)[:, None]
    q = np.arange(128)[None, :]
    m0 = np.where(k >= q, 0.0, BIG)
    m2 = np.where(k <= q, 0.0, BIG)
    e0 = np.full((128, 128), BIG) if half == 0 else m0
    e3 = np.full((128, 128), BIG) if half == 1 else m2
    wmask = np.stack([m0, m2, e0, e3]).astype(np.float32)
    return dict(ident=ident, invf=invf, cdft=cdft, Gmat=G, Emat=E, wmask=wmask)


def _prep_weights(inp):
    w_in = np.asarray(inp["w_in"])
    cq = w_in[:, :, 0:384]
    ckv = w_in[:, :, 384:512]
    kr = w_in[:, :, 512:544]
    krs = np.concatenate([kr[:, :, 16:32], kr[:, :, 0:16]], axis=2)
    qc = w_in[:, :, 544:1056]
    idx = np.concatenate([np.concatenate([np.arange((0 * 4 + g) * 64, (0 * 4 + g) * 64 + 64),
                                          np.arange((1 * 4 + g) * 64, (1 * 4 + g) * 64 + 64)]) for g in range(4)])
    qcp = qc[:, :, idx]
    kc = w_in[:, :, 1056:1184]
    vc = w_in[:, :, 1184:1312]
    pad = np.zeros(kr.shape[:2] + (64,), dtype=kr.dtype)
    w_small = np.ascontiguousarray(np.concatenate([cq, ckv, pad, kr, pad, krs, qcp, kc, vc], axis=2))
    w_gates = np.ascontiguousarray(w_in[:, :, 1312:])
    w_uq = np.asarray(inp["w_uq"])
    ir, isw = [], []
    for h in range(8):
        b = h * 96
        ir += list(range(b, b + 96))
        isw += list(range(b, b + 64)) + list(range(b + 80, b + 96)) + list(range(b + 64, b + 80))
    return dict(w_small=w_small, w_gates=w_gates, w_uqr=np.ascontiguousarray(w_uq[:, :, ir]),
                w_uqs=np.ascontiguousarray(w_uq[:, :, isw]), w_ukv=np.asarray(inp["w_ukv"]),
                w_a=np.asarray(inp["w_branch_a"]), w_b=np.asarray(inp["w_branch_b"]), w_c=np.asarray(inp["w_branch_c"]),
                w_out=np.asarray(inp["w_out"]), w_fg=np.asarray(inp["w_ffn_gate"]), w_fu=np.asarray(inp["w_ffn_up"]),
                w_fd=np.asarray(inp["w_ffn_down"]), w_pp=np.asarray(inp["w_ple_proj"]), w_pg=np.asarray(inp["w_ple_gate"]),
                g_mix_pre=np.asarray(inp["norm_mix_pre"]), g_q=np.asarray(inp["mla_q_norm"]), g_kv=np.asarray(inp["mla_kv_norm"]),
                g_mix_post=np.asarray(inp["norm_mix_post"]), g_ffn_pre=np.asarray(inp["norm_ffn_pre"]),
                g_ffn_post=np.asarray(inp["norm_ffn_post"]), g_ple=np.asarray(inp["norm_ple"]), sink=np.asarray(inp["gqa_sink"]))


def make_in_maps(inp):
    shared = _prep_weights(inp)
    shared = {k: np.ascontiguousarray(v, dtype=np.float32) for k, v in shared.items()}
    x = np.asarray(inp["x"]); p = np.asarray(inp["p"]); pos = np.asarray(inp["positions"])
    maps = []
    for c in range(8):
        b, half = c // 2, c % 2
        s0 = half * TOK
        m = dict(shared)
        m.update(_consts(half))
        m["x"] = np.ascontiguousarray(x[b, s0:s0 + TOK, :], dtype=np.float32)
        m["p"] = np.ascontiguousarray(p[:, b, s0:s0 + TOK, :], dtype=np.float32)
        m["pos"] = np.ascontiguousarray(pos[b, s0:s0 + TOK].reshape(1, TOK), dtype=np.int32)
        px = np.zeros(TOK + 256, dtype=np.int32)
        lo, hi = s0 - 128, s0 + TOK + 128
        slo, shi = max(lo, 0), min(hi, SEQ)
        px[slo - lo:shi - lo] = pos[b, slo:shi]
        m["posx"] = np.ascontiguousarray(px.reshape(34, 128))
        maps.append(m)
    return maps


_NC_CACHE = {}


def kernel(**inputs):
    if "nc" not in _NC_CACHE:
        _NC_CACHE["nc"] = build_program()
    nc = _NC_CACHE["nc"]
    maps = make_in_maps(inputs)
    res = run_bass_kernel_spmd(nc, maps, core_ids=list(range(8)))
    out = np.empty((NB, SEQ, D), dtype=np.float32)
    for c in range(8):
        b, half = c // 2, c % 2
        out[b, half * TOK:(half + 1) * TOK, :] = res.results[c]["y"]
    return out
```
